# Optimizing a Trainium2 kernel written in Bass

```python
import math
import jax, jax.numpy as jnp
from jax import lax
import numpy as np

D_MODEL = 2048
BATCH = 8
SEQ = 2048
DEPTH = 1
DEC_BATCH = 16
DEC_SEQ = 64
PAST_LEN = 4096

CHUNK = 64
ATTN_WIDTH = 1024
SGU_WIDTH = 1024
HEAD_DIM = 64
N_HEADS = ATTN_WIDTH // HEAD_DIM
N_KV_HEADS = 2
GQA_GROUP = N_HEADS // N_KV_HEADS
WINDOW = 128
N_BAND = WINDOW // CHUNK
N_BUCKETS = 32
MAX_DISTANCE = 128
SGU_CHUNK = 128
SGU_GROUPS = 8
SGU_GROUP_CH = SGU_WIDTH // SGU_GROUPS
D_FF = 4 * D_MODEL
KV_WIDTH = N_KV_HEADS * HEAD_DIM
IN_WIDTH = ATTN_WIDTH + 2 * KV_WIDTH + 2 * SGU_WIDTH
EPS = 1e-6
NEG_INF = -1e30

kernel_name = 'hymba_swa_sink_gmlp_stream_step'


def rmsnorm(x, g):
    xf = x.astype(jnp.float32)
    y = xf * lax.rsqrt(jnp.mean(xf * xf, axis=-1, keepdims=True) + EPS)
    return (y * g.astype(jnp.float32)).astype(x.dtype)


def t5_bucket(n):
    half = N_BUCKETS // 2
    max_exact = half // 2
    offset = jnp.where(n < 0, half, 0)
    a = jnp.abs(n)
    af = jnp.maximum(a, 1).astype(jnp.float32)
    large = max_exact + (jnp.log(af / max_exact) / math.log(MAX_DISTANCE / max_exact)
                         * (half - max_exact)).astype(jnp.int32)
    large = jnp.minimum(large, half - 1)
    return offset + jnp.where(a < max_exact, a, large)


def rel_bias(table, q_pos, k_pos):
    n = q_pos[:, None] - k_pos[None, :]
    b = table[t5_bucket(n)].astype(jnp.float32)
    return jnp.transpose(b, (2, 0, 1)).reshape(N_KV_HEADS, GQA_GROUP, q_pos.shape[0], k_pos.shape[0])


def sink_attention(q, k, v, bias, sinks, valid=None):
    s = jnp.einsum('...qhgd,...khd->...hgqk', q, k).astype(jnp.float32) * (HEAD_DIM ** -0.5) + bias
    if valid is not None:
        s = jnp.where(valid, s, NEG_INF)
    sink = sinks.astype(jnp.float32)[..., None]
    m = jnp.maximum(jnp.max(s, axis=-1), sink)
    p = jnp.exp(s - m[..., None])
    w = p / (jnp.sum(p, axis=-1, keepdims=True) + jnp.exp(sink - m)[..., None])
    return jnp.einsum('...hgqk,...khd->...qhgd', w.astype(v.dtype), v)


def project(x, ln_g, w_in, q_g, k_g):
    lead = x.shape[:-1]
    z = rmsnorm(x, ln_g) @ w_in
    q = z[..., :ATTN_WIDTH].reshape(*lead, N_KV_HEADS, GQA_GROUP, HEAD_DIM)
    k = z[..., ATTN_WIDTH:ATTN_WIDTH + KV_WIDTH].reshape(*lead, N_KV_HEADS, HEAD_DIM)
    v = z[..., ATTN_WIDTH + KV_WIDTH:ATTN_WIDTH + 2 * KV_WIDTH].reshape(*lead, N_KV_HEADS, HEAD_DIM)
    zs = z[..., ATTN_WIDTH + 2 * KV_WIDTH:]
    return rmsnorm(q, q_g), rmsnorm(k, k_g), v, zs


def attn_prompt(q, k, v, table, sinks):
    b, s = q.shape[0], q.shape[1]
    n_c = s // CHUNK
    band = (N_BAND + 1) * CHUNK
    qc = q.reshape(b, n_c, CHUNK, N_KV_HEADS, GQA_GROUP, HEAD_DIM)
    pad = ((0, 0), (N_BAND * CHUNK, 0), (0, 0), (0, 0))
    kc = jnp.pad(k, pad).reshape(b, n_c + N_BAND, CHUNK, N_KV_HEADS, HEAD_DIM)
    vc = jnp.pad(v, pad).reshape(b, n_c + N_BAND, CHUNK, N_KV_HEADS, HEAD_DIM)
    kb = jnp.concatenate([kc[:, i:i + n_c] for i in range(N_BAND + 1)], axis=2)
    vb = jnp.concatenate([vc[:, i:i + n_c] for i in range(N_BAND + 1)], axis=2)
    bias = rel_bias(table, N_BAND * CHUNK + jnp.arange(CHUNK), jnp.arange(band))
    key_pos = jnp.arange(n_c)[:, None] * CHUNK + jnp.arange(band)[None, :] - N_BAND * CHUNK
    valid = (key_pos >= 0).reshape(n_c, 1, 1, 1, band)
    o = sink_attention(qc, kb, vb, bias, sinks, valid)
    return o.reshape(b, s, ATTN_WIDTH)


def attn_sample(q, k_all, v_all, table, sinks):
    b, t = q.shape[0], q.shape[1]
    n_k = k_all.shape[1]
    past = n_k - t
    bias = rel_bias(table, past + jnp.arange(t), jnp.arange(n_k))
    o = sink_attention(q, k_all, v_all, bias, sinks)
    return o.reshape(b, t, ATTN_WIDTH)


def sgu_inputs(zs, g):
    a = jax.nn.gelu(zs)
    return a[..., :SGU_WIDTH], rmsnorm(a[..., SGU_WIDTH:], g)


def sgu_mask(n):
    i = jnp.arange(n)
    return (i[None, :] // CHUNK) <= (i[:, None] // CHUNK)


def sgu_prompt(u, v, w_s, b_s):
    b, s = u.shape[0], u.shape[1]
    nb = s // SGU_CHUNK
    vb = v.reshape(b, nb, SGU_CHUNK, SGU_GROUPS, SGU_GROUP_CH)
    ws = jnp.where(sgu_mask(SGU_CHUNK)[None], w_s, 0)
    sp = jnp.einsum('gij,bnjgc->bnigc', ws, vb) + b_s.T[:, :, None]
    return u * sp.reshape(b, s, SGU_WIDTH)


def sgu_sample(u, v, w_s, b_s):
    b, t = u.shape[0], u.shape[1]
    ws = jnp.where(sgu_mask(t)[None], w_s[:, :t, :t], 0)
    sp = jnp.einsum('gij,bjgc->bigc', ws, v.reshape(b, t, SGU_GROUPS, SGU_GROUP_CH)) + b_s[:, :t].T[:, :, None]
    return u * sp.reshape(b, t, SGU_WIDTH)


def finish(x, attn_o, sgu_o, g_a, g_s, w_out, ln_ffn, w_up, w_down):
    mix = jnp.concatenate([rmsnorm(attn_o, g_a), rmsnorm(sgu_o, g_s)], axis=-1)
    x = x + mix @ w_out
    h = rmsnorm(x, ln_ffn)
    return x + jnp.square(jax.nn.relu(h @ w_up)) @ w_down


def setup_inputs(seed: int = 0) -> dict:
    key = jax.random.key(seed)
    ks = jax.random.split(key, 20)
    f32 = jnp.float32
    cache_len = min(WINDOW, PAST_LEN)

    def nrm(k, shape, scale):
        return jax.random.normal(k, shape, f32) * scale

    def gain(k, shape):
        return 1.0 + 0.05 * jax.random.normal(k, shape, f32)

    return {
        'x_prompt': nrm(ks[0], (BATCH, SEQ, D_MODEL), 1.0),
        'x_sample': nrm(ks[1], (DEC_BATCH, DEC_SEQ, D_MODEL), 1.0),
        'cache_attn_k': nrm(ks[2], (DEPTH, DEC_BATCH, cache_len, N_KV_HEADS, HEAD_DIM), 1.0),
        'cache_attn_v': nrm(ks[3], (DEPTH, DEC_BATCH, cache_len, N_KV_HEADS, HEAD_DIM), 1.0),
        'rel_bias_table': nrm(ks[4], (N_BUCKETS, N_HEADS), 0.5),
        'ln_mix_g': gain(ks[5], (DEPTH, D_MODEL)),
        'w_in': nrm(ks[6], (DEPTH, D_MODEL, IN_WIDTH), D_MODEL ** -0.5),
        'q_norm_g': gain(ks[7], (DEPTH, HEAD_DIM)),
        'k_norm_g': gain(ks[8], (DEPTH, HEAD_DIM)),
        'attn_sinks': nrm(ks[9], (DEPTH, N_KV_HEADS, GQA_GROUP), 0.5),
        'sgu_norm_g': gain(ks[10], (DEPTH, SGU_WIDTH)),
        'sgu_w': nrm(ks[11], (DEPTH, SGU_GROUPS, SGU_CHUNK, SGU_CHUNK), SGU_CHUNK ** -0.5),
        'sgu_b': 1.0 + 0.1 * jax.random.normal(ks[12], (DEPTH, SGU_GROUPS, SGU_CHUNK), f32),
        'out_norm_attn_g': gain(ks[13], (DEPTH, ATTN_WIDTH)),
        'out_norm_sgu_g': gain(ks[14], (DEPTH, SGU_WIDTH)),
        'w_out': nrm(ks[15], (DEPTH, ATTN_WIDTH + SGU_WIDTH, D_MODEL), (ATTN_WIDTH + SGU_WIDTH) ** -0.5),
        'ln_ffn_g': gain(ks[16], (DEPTH, D_MODEL)),
        'w_ffn_up': nrm(ks[17], (DEPTH, D_MODEL, D_FF), D_MODEL ** -0.5),
        'w_ffn_down': nrm(ks[18], (DEPTH, D_FF, D_MODEL), D_FF ** -0.5),
    }


def reference(x_prompt, x_sample, cache_attn_k, cache_attn_v, rel_bias_table, ln_mix_g, w_in,
              q_norm_g, k_norm_g, attn_sinks, sgu_norm_g, sgu_w, sgu_b, out_norm_attn_g,
              out_norm_sgu_g, w_out, ln_ffn_g, w_ffn_up, w_ffn_down):
    xp, xs = x_prompt, x_sample
    keep_p = min(WINDOW, xp.shape[1])
    kp_rows, vp_rows, ks_rows, vs_rows, sgu_rows = [], [], [], [], []
    for l in range(DEPTH):
        q, k, v, zs = project(xp, ln_mix_g[l], w_in[l], q_norm_g[l], k_norm_g[l])
        a_o = attn_prompt(q, k, v, rel_bias_table, attn_sinks[l])
        u, sv = sgu_inputs(zs, sgu_norm_g[l])
        s_o = sgu_prompt(u, sv, sgu_w[l], sgu_b[l])
        kp_rows.append(k[:, -keep_p:])
        vp_rows.append(v[:, -keep_p:])
        xp = finish(xp, a_o, s_o, out_norm_attn_g[l], out_norm_sgu_g[l], w_out[l],
                    ln_ffn_g[l], w_ffn_up[l], w_ffn_down[l])
        q, k, v, zs = project(xs, ln_mix_g[l], w_in[l], q_norm_g[l], k_norm_g[l])
        k_all = jnp.concatenate([cache_attn_k[l], k], axis=1)
        v_all = jnp.concatenate([cache_attn_v[l], v], axis=1)
        a_o = attn_sample(q, k_all, v_all, rel_bias_table, attn_sinks[l])
        u, sv = sgu_inputs(zs, sgu_norm_g[l])
        s_o = sgu_sample(u, sv, sgu_w[l], sgu_b[l])
        ks_rows.append(k)
        vs_rows.append(v)
        sgu_rows.append(sv)
        xs = finish(xs, a_o, s_o, out_norm_attn_g[l], out_norm_sgu_g[l], w_out[l],
                    ln_ffn_g[l], w_ffn_up[l], w_ffn_down[l])
    new_attn_k_prompt = jnp.stack(kp_rows)
    new_attn_v_prompt = jnp.stack(vp_rows)
    new_attn_k_sample = jnp.stack(ks_rows)
    new_attn_v_sample = jnp.stack(vs_rows)
    new_sgu_v_sample = jnp.stack(sgu_rows)
    return (xp, xs, new_attn_k_prompt, new_attn_v_prompt, new_attn_k_sample, new_attn_v_sample, new_sgu_v_sample)
```

```python
import numpy as np
import concourse.bass as bass
import concourse.mybir as mybir
from concourse.bass_utils import run_bass_kernel_spmd

F32 = mybir.dt.float32
BF16 = mybir.dt.bfloat16
AF = mybir.ActivationFunctionType
ALU = mybir.AluOpType
AX = mybir.AxisListType

D = 2048
DFF = 8192
NCORES = 8
EPS = 1e-6
NEGB = -30000.0
import os
SAMPLE_RING = int(os.environ.get("SAMPLE_RING", "6"))
SAME_ENG_RAW_ONLY = os.environ.get("SAME_ENG_RAW_ONLY", "1") == "1"
ENGS = ("pe", "act", "dve", "pool", "sp")


class _Op:
    __slots__ = ("eng", "fn", "reads", "writes", "dma", "deps", "signal", "ev", "idx", "raw")

    def __init__(self, eng, fn, reads, writes, dma):
        self.eng, self.fn, self.reads, self.writes, self.dma = eng, fn, reads, writes, dma
        self.deps = set()
        self.raw = set()
        self.signal = False
        self.ev = None


class Prog:
    def __init__(self, nc, same_engine_sync=True):
        self.nc = nc
        self.ops = []
        self.last_w = {}
        self.readers = {}
        self.same_engine_sync = same_engine_sync

    def _add(self, eng, fn, reads, writes, dma=None):
        op = _Op(eng, fn, tuple(reads), tuple(writes), dma)
        op.idx = len(self.ops)
        for r in op.reads:
            w = self.last_w.get(r)
            if w is not None:
                op.deps.add(w)
                op.raw.add(w)
        for w_ in op.writes:
            w = self.last_w.get(w_)
            if w is not None:
                op.deps.add(w)
            latest = {}
            for rd in self.readers.get(w_, ()):
                ro = self.ops[rd]
                if ro.dma is not None:
                    op.deps.add(rd)
                else:
                    latest[ro.eng] = max(latest.get(ro.eng, -1), rd)
            op.deps.update(latest.values())
        for r in op.reads:
            self.readers.setdefault(r, []).append(op.idx)
        for w_ in op.writes:
            self.last_w[w_] = op.idx
            self.readers[w_] = []
        op.deps.discard(op.idx)
        self.ops.append(op)
        return op

    def op(self, eng, fn, reads=(), writes=()):
        return self._add(eng, fn, reads, writes)

    def dma(self, eng, sem_name, fn, reads=(), writes=()):
        return self._add(eng, fn, reads, writes, dma=sem_name)

    def finalize(self, block):
        import os
        nc = self.nc
        ops = self.ops
        nmax = int(os.environ.get("BISECT_N", "0"))
        if nmax:
            ops = ops[:nmax]
            for i, o in enumerate(ops[-3:]):
                print("last ops:", o.idx, o.eng, o.reads, o.writes, o.dma)
        print("n_ops", len(ops))
        for op in ops:
            for d in op.deps:
                p = ops[d]
                if p.dma is not None:
                    continue
                if p.eng == op.eng and (p.eng == "pe" or not self.same_engine_sync or (SAME_ENG_RAW_ONLY and d not in op.raw)):
                    continue
                p.signal = True
        eng_sem = {e: nc.alloc_semaphore("sem_" + e) for e in ENGS}
        self.all_sems = list(eng_sem.values())
        cnt = {e: 0 for e in ENGS}
        dsem, dcnt = {}, {}
        for op in ops:
            if op.dma is not None:
                if op.dma not in dsem:
                    dsem[op.dma] = nc.alloc_semaphore("dsem_" + op.dma)
                    self.all_sems.append(dsem[op.dma])
                    dcnt[op.dma] = 0
                dcnt[op.dma] += 16
                op.ev = (op.dma, dcnt[op.dma])
            elif op.signal:
                cnt[op.eng] += 1
                op.ev = (op.eng, cnt[op.eng])
        issued = {k: 0 for k in dsem}
        waits_for = []
        for op in ops:
            w = {}
            for d in op.deps:
                p = ops[d]
                if p.dma is not None:
                    w[("d", p.dma)] = max(w.get(("d", p.dma), 0), issued[p.dma])
                elif p.ev is not None:
                    if p.eng == op.eng and (p.eng == "pe" or (SAME_ENG_RAW_ONLY and d not in op.raw)):
                        continue
                    w[("e", p.eng)] = max(w.get(("e", p.eng), 0), p.ev[1])
            waits_for.append(w)
            if op.dma is not None:
                issued[op.dma] += 16
        final = dict(dcnt)

        def semof(k):
            return dsem[k[1]] if k[0] == "d" else eng_sem[k[1]]

        def emit(engname, engobj):
            waited = {}
            for op, w in zip(ops, waits_for):
                if op.eng != engname:
                    continue
                for k, v in w.items():
                    if waited.get(k, 0) >= v:
                        continue
                    engobj.wait_ge(semof(k), v)
                    waited[k] = v
                inst = op.fn(engobj)
                if op.dma is not None:
                    inst.then_inc(dsem[op.dma], 16)
                elif op.signal:
                    inst.then_inc(eng_sem[op.eng], 1)
            if engname == "sp":
                for k, v in final.items():
                    engobj.wait_ge(dsem[k], v)

        for sm in self.all_sems:
            nc.sync.sem_clear(sm)
        nc.all_engine_barrier()
        block = nc.Block().__enter__()
        self._block = block

        @block.tensor
        def _(e):
            emit("pe", e)

        @block.scalar
        def _(e):
            emit("act", e)

        @block.vector
        def _(e):
            emit("dve", e)

        @block.gpsimd
        def _(e):
            emit("pool", e)

        @block.sync
        def _(e):
            emit("sp", e)

        block.__exit__(None, None, None)


class _Stop(Exception):
    pass


def build_program(stage=99, tiles=None):
    nc = bass.Bass("TRN2", target_bir_lowering=False)

    def din(name, shape):
        return nc.dram_tensor(name, list(shape), F32, kind="ExternalInput")

    def dout(name, shape):
        return nc.dram_tensor(name, list(shape), F32, kind="ExternalOutput")

    xp = din("xp", (2048, D))
    xs = din("xs", (128, D))
    ckT = din("ckT", (2, 128, 128))
    cv = din("cv", (2, 128, 128))
    table = din("table", (32, 16))
    oh = din("oh", (32, 384))
    ident = din("ident", (128, 128))
    w_in = din("w_in", (128, 16 * 3328))
    w_out = din("w_out", (128, 4 * 8192))
    w_up = din("w_up", (128, 16 * 8192))
    w_down = din("w_down", (128, 16 * 8192))
    wimg = {"in": w_in, "out": w_out, "up": w_up, "down": w_down}
    wscr = {k: nc.dram_tensor("scr_" + k, list(v.shape), BF16, kind="ExternalOutput") for k, v in wimg.items()}
    g_mix_d = din("g_mix", (1, D))
    g_ffn_d = din("g_ffn", (1, D))
    g_sgu_d = din("g_sgu", (1, 1024))
    g_oa_d = din("g_oa", (1, 1024))
    g_os_d = din("g_os", (1, 1024))
    gq_d = din("gq", (1, 64))
    gk_d = din("gk", (1, 64))
    sinks_d = din("sinks", (1, 16))
    wsT_d = din("wsT", (128, 8, 128))
    bsT_d = din("bsT", (128, 8))
    scr = nc.dram_tensor("scr", [16, 384], F32, kind="Internal")

    yp = dout("yp", (2048, D))
    ys = dout("ys", (128, D))
    kp_o = dout("kp", (128, 128))
    vp_o = dout("vp", (128, 128))
    ks_o = dout("ks", (128, 128))
    vs_o = dout("vs", (128, 128))
    sgv_o = dout("sgv", (128, 1024))

    def sb(name, shape, dt):
        return nc.alloc_sbuf_tensor(name, list(shape), dt)

    identb = sb("identb", (128, 128), BF16)
    g_mix = sb("g_mix_s", (128, D), F32)
    g_ffn = sb("g_ffn_s", (128, D), F32)
    g_sgu = sb("g_sgu_s", (128, 1024), F32)
    g_oa = sb("g_oa_s", (128, 1024), F32)
    g_os = sb("g_os_s", (128, 1024), F32)
    gq_t = sb("gq_s", (128, 64), F32)
    gk_t = sb("gk_s", (128, 64), F32)
    esink = sb("esink", (128, 16), F32)
    epsb = sb("epsb", (128, 1), F32)
    onesb = sb("onesb", (128, 2), BF16)
    WsT = sb("WsT", (128, 8, 128), BF16)
    bsT = sb("bsT_s", (128, 8), F32)
    bsTs = sb("bsTs", (128, 8), F32)
    biasP = sb("biasP", (128, 16, 128), F32)
    biasO = sb("biasO", (128, 16, 128), F32)
    NSLOT = 5
    kTz = [[sb(f"kTz{hk}_{s}", (128, 128), BF16) for s in range(NSLOT)] for hk in range(2)]
    Vb = [sb(f"Vb{s}", (128, 128), BF16) for s in range(NSLOT)]
    qT = sb("qT", (128, 8, 128), BF16)
    stat = sb("stat", (128, 64), F32)
    tmpR = [sb(f"tmpR{i}", (128, 512), F32) for i in range(2)]
    X1 = sb("X1", (128, 4, D), F32)
    Hr = sb("Hr", (128, 16384), F32)
    Hrb = Hr[:, :].bitcast(BF16)
    Tr = sb("Tr", (128, 16, 512), BF16)
    Wr = [sb(f"Wr{i}", (128, 8192), BF16) for i in range(2)]
    ps = nc.alloc_psum_tensor("ps", [128, 4096], F32)

    def bank(i, n=512):
        return ps[:, i * 512:i * 512 + n]

    ptr = ps[:, 0:1024].bitcast(BF16)
    ptr2 = ps[:, 3072:4096].bitcast(BF16)
    tstate = {"n": 0}

    class HAlloc:
        def __init__(self):
            self.off = 0

        def take(self, nbytes, dt, shape=None):
            start = (self.off + 1023) // 1024 * 1024
            self.off = start + nbytes
            assert self.off <= 65536, self.off
            ap = Hr[:, start // 4:(start + nbytes) // 4]
            if dt == BF16:
                ap = Hrb[:, start // 2:(start + nbytes) // 2]
            names = [f"H{i}" for i in range(start // 1024, (start + nbytes + 1023) // 1024)]
            return ap, names

    ha = HAlloc()
    zqk, zqk_n = [], []
    for b in range(4):
        a, n = ha.take(1152 * 4, F32)
        zqk.append(a)
        zqk_n.append(n)
    tmpq, tmpq_n = ha.take(1152 * 4, F32)
    qknb, qknb_n = ha.take(1152 * 2, BF16)
    tmpS, tmpS_n = [], []
    for i in range(2):
        a, n = ha.take(512 * 4, F32)
        tmpS.append(a)
        tmpS_n.append(n)
    PT, PT_n = {}, {}
    for hk in range(2):
        for kt in range(3):
            a, n = ha.take(1024 * 2, BF16)
            PT[(hk, kt)] = a
            PT_n[(hk, kt)] = n
    o32, o32_n = ha.take(1024 * 4, F32)
    vnb, vnb_n = ha.take(1024 * 2, BF16)
    xtmp, xtmp_n = ha.take(D * 4, F32)
    hank, hank_n = xtmp, xtmp_n
    mixb, mixb_n = ha.take(D * 2, BF16)
    mixb_lo_n, mixb_hi_n = mixb_n[:2], mixb_n[2:]
    assert len(mixb_n) == 4
    h_used = ha.off
    tbl = tmpR[1][0:32, 384:400]
    ohs = tmpR[0][0:32, 0:384]
    srow = tmpR[1][0:16, 0:384]
    kn32 = o32[:, 0:128]
    v32 = o32[:, 128:256]
    zq1_start = 5 * 1024
    biasPB = Hr[:, zq1_start // 4:(zq1_start + 8192) // 4]
    wsts_start = 15 * 1024
    WsTs = Hrb[:, wsts_start // 2:(wsts_start + 2048) // 2].rearrange("p (g i) -> p g i", g=8)
    WsTs_n = ["H15", "H16"]
    biasPB_n = [f"H{i}" for i in range(zq1_start // 1024, zq1_start // 1024 + 8)]
    ALLH = [f"H{i}" for i in range(64)]

    hstate = {"compact": False}

    def hid(f, t0=0, t1=512):
        if hstate["compact"]:
            return Hrb[:, f * 128 + t0:f * 128 + t1]
        return Hrb[:, f * 512 + t0:f * 512 + t1]

    def hid_n(f):
        return f"H{f // 4}" if hstate["compact"] else f"H{f}"

    P = Prog(nc)
    _cst = {"n": 0}
    _orig_dma = P.dma

    def _dma(eng, sem_name, fn, reads=(), writes=()):
        if sem_name == "cst":
            _cst["n"] += 1
            sem_name = f"cst{_cst['n']}"
        return _orig_dma(eng, sem_name, fn, reads=reads, writes=writes)

    P.dma = _dma

    TILES = [("p", t) for t in range(4)] + [("s", 0)]
    chunks = []
    for _ in (TILES if tiles is None else tiles):
        chunks += [("in", c) for c in range(7)]
        chunks += [("out", c) for c in range(4)]
        chunks += [("up", c) for c in range(16)]
        chunks += [("down", r, g) for r in range(2) for g in range(8)]
    IN_COL0 = [0, 512, 1024, 1280, 1792, 2304, 2816]
    IN_W = [512, 512, 256, 512, 512, 512, 512]
    wstate = {"i": 0, "issued": 0}

    NCH = 43
    SLOT_AP = [Wr[0][:, :], Wr[1][:, :], X1[:, 1:3, :].rearrange("p b n -> p (b n)").bitcast(BF16),
               Hrb[:, 8192:16384], Hrb[:, 16384:24576], Hrb[:, 24576:32768]]
    SLOT_N = [["w0"], ["w1"], ["X1_1_lo", "X1_1_hi", "X1_2_lo", "X1_2_hi"],
              [f"H{i}" for i in range(16, 32)], [f"H{i}" for i in range(32, 48)], [f"H{i}" for i in range(48, 64)]]
    n_tiles_run = len(TILES if tiles is None else tiles)
    tile_kinds = [k for k, _ in (TILES if tiles is None else tiles)]

    def slot_of(i):
        j = i % NCH
        if tile_kinds[i // NCH] == "s":
            if SAMPLE_RING <= 2:
                return i % 2
            return [0, 1, 2][j % 3] if j < 11 else [2, 0, 1, 3, 4, 5][(j - 11) % 6]
        return i % 2

    def ahead_of(i):
        j = i % NCH
        if tile_kinds[i // NCH] == "s":
            if SAMPLE_RING <= 2:
                return 1
            return 2 if j < 11 else 5
        return 1

    slot_last = {}

    def issue_chunk(i):
        c = chunks[i]
        s = slot_of(i)
        assert slot_last.get(s, -1) < wstate["i"], (i, s, slot_last.get(s), wstate["i"])
        slot_last[s] = i
        tile_i, j = i // NCH, i % NCH
        multi = n_tiles_run >= 4
        if not multi:
            cast, wback = tile_i == 0, tile_i == 0
        elif tile_i == 0:
            cast, wback = True, (j % 3 == 0)
        elif tile_i == 1:
            cast, wback = (j % 3 != 0), (j % 3 == 1)
        elif tile_i == 2:
            cast, wback = (j % 3 == 2), (j % 3 == 2)
        else:
            cast, wback = False, False
        kind = c[0]
        if kind == "in":
            off, ln = 16 * IN_COL0[c[1]], 16 * IN_W[c[1]]
        elif kind == "down":
            off, ln = (c[1] * 8 + c[2]) * 8192, 8192
        else:
            off, ln = c[1] * 8192, 8192
        d = SLOT_AP[s][:, 0:ln]
        rname = f"scr_{kind}_{off}"
        if cast:
            src = wimg[kind].ap()[:, off:off + ln]
            P.dma("pool", f"w{s}q", lambda e, d=d, src=src: e.dma_start(out=d, in_=src), writes=SLOT_N[s])
            if wback:
                dsts = wscr[kind].ap()[:, off:off + ln]
                P.dma("sp", "wst", lambda e, d=d, dsts=dsts: e.dma_start(out=dsts, in_=d), reads=SLOT_N[s], writes=[rname])
        else:
            src = wscr[kind].ap()[:, off:off + ln]
            P.dma("sp", f"w{s}", lambda e, d=d, src=src: e.dma_start(out=d, in_=src), reads=[rname], writes=SLOT_N[s])

    def next_chunk(kind):
        i = wstate["i"]
        assert chunks[i][0] == kind, (chunks[i], kind)
        while wstate["issued"] <= min(i + ahead_of(i), len(chunks) - 1):
            nxt = wstate["issued"]
            if nxt // NCH != i // NCH and nxt > i + 1:
                break
            issue_chunk(nxt)
            wstate["issued"] += 1
        wstate["i"] += 1
        return slot_of(i)

    def rstd_of(ss_col, r_col, inv_n, reads, tag):
        P.op("act", lambda e: e.activation(out=r_col, in_=ss_col, func=AF.Sqrt, scale=inv_n, bias=epsb[:]),
             reads=reads + ["epsb"], writes=[tag + "_r"])
        P.op("dve", lambda e: e.reciprocal(out=r_col, in_=r_col), reads=[tag + "_r"], writes=[tag + "_r"])

    def transposes_to(src_tile, nch, dst_ap, src_names, dst_names, evac_eng="act"):
        k = tstate["n"] % 2
        tstate["n"] += 1
        pt = [ptr, ptr2][k]
        pn = [["ps0", "ps1"], ["ps6", "ps7"]][k]
        for c in range(nch):
            P.op("pe", lambda e, c=c, pt=pt: e.matmul(pt[:, c * 128:(c + 1) * 128], lhsT=src_tile[:, c * 128:(c + 1) * 128],
                                                    rhs=identb[:], start=True, stop=True, is_transpose=True),
                 reads=src_names + ["identb"], writes=pn)
        src = pt[:, 0:nch * 128].rearrange("p (c t) -> p c t", c=nch)
        if evac_eng == "act":
            P.op("act", lambda e: e.copy(out=dst_ap, in_=src), reads=pn, writes=dst_names)
        else:
            P.op("dve", lambda e: e.tensor_copy(out=dst_ap, in_=src), reads=pn, writes=dst_names)

    def toeplitz(dst, dst_names, c0):
        hk_ = hank.rearrange("p (h q) -> p h q", h=16)
        P.dma("sp", "cst", lambda e: e.dma_start(out=hk_, in_=bass.AP(tensor=scr, offset=c0, ap=[[1, 128], [384, 16], [1, 128]])),
              reads=["scr"], writes=hank_n)
        t = hank
        rev = bass.AP(tensor=t.tensor, offset=t.offset + 127, ap=[list(t.ap[0]), [128, 16], [-1, 128]])
        P.op("dve", lambda e: e.tensor_copy(out=dst, in_=rev), reads=hank_n, writes=dst_names)

    block = None
    if True:
        P.op("dve", lambda e: e.memset(epsb[:], EPS), writes=["epsb"])
        P.op("dve", lambda e: e.memset(onesb[:], 1.0), writes=["onesb"])
        for hk in range(2):
            for s in range(NSLOT):
                P.op("dve", lambda e, hk=hk, s=s: e.memset(kTz[hk][s][:], 0.0), writes=[f"kTz{hk}_{s}"])

        def bc_load(dst, src, n, name):
            P.dma("sp", "cst", lambda e: e.dma_start(out=dst[:], in_=src.ap().partition_broadcast(128)[:, 0, :]), writes=[name])

        idf = xtmp[:, 0:128]
        P.dma("sp", "cst", lambda e: e.dma_start(out=idf, in_=ident.ap()), writes=xtmp_n)
        P.op("dve", lambda e: e.tensor_copy(out=identb[:], in_=idf), reads=xtmp_n, writes=["identb"])
        bc_load(g_mix, g_mix_d, D, "g_mix")
        P.dma("sp", "cst", lambda e: e.dma_start(out=tbl, in_=table.ap()), writes=["tmpR1"])
        P.dma("sp", "cst", lambda e: e.dma_start(out=ohs, in_=oh.ap()), writes=["tmpR0"])

    def late_setup():
        bc_load(g_ffn, g_ffn_d, D, "g_ffn")
        bc_load(g_sgu, g_sgu_d, 1024, "g_sgu")
        bc_load(g_oa, g_oa_d, 1024, "g_oa")
        bc_load(g_os, g_os_d, 1024, "g_os")
        bc_load(gq_t, gq_d, 64, "gq")
        bc_load(gk_t, gk_d, 64, "gk")
        bc_load(esink, sinks_d, 16, "esink")
        P.op("act", lambda e: e.activation(out=esink[:], in_=esink[:], func=AF.Exp), reads=["esink"], writes=["esink"])
        wst = hank.rearrange("p (g i) -> p g i", g=16)[:, 0:8, :]
        P.dma("sp", "cst", lambda e: e.dma_start(out=wst, in_=wsT_d.ap()), writes=hank_n)
        P.op("dve", lambda e: e.tensor_copy(out=WsT[:], in_=wst), reads=hank_n, writes=["WsT"])
        P.op("dve", lambda e: e.memset(WsT[64:128, :, 0:64], 0.0), reads=[], writes=["WsT"])
        P.dma("sp", "cst", lambda e: e.dma_start(out=bsT[:], in_=bsT_d.ap()), writes=["bsT"])
        P.dma("sp", "cst", lambda e: e.dma_start(out=bsTs[0:64, :], in_=bsT_d.ap()[0:64, :]), writes=["bsTs"])
        P.dma("sp", "cst", lambda e: e.dma_start(out=bsTs[64:128, :], in_=bsT_d.ap()[0:64, :]), writes=["bsTs"])
        P.op("pe", lambda e: e.matmul(bank(2)[0:16, 0:384], lhsT=tbl, rhs=ohs, start=True, stop=True),
             reads=["tmpR0", "tmpR1"], writes=["ps2"])
        P.op("dve", lambda e: e.tensor_copy(out=srow, in_=bank(2)[0:16, 0:384]), reads=["ps2"], writes=["tmpR1"])
        P.dma("sp", "cst", lambda e: e.dma_start(out=scr.ap(), in_=srow), reads=["tmpR1"], writes=["scr"])
        toeplitz(biasP[:], ["biasP"], 0)
        P.op("dve", lambda e: e.memset(biasP[0:64, :, 64:128], NEGB), writes=["biasP"])
        toeplitz(biasO[:], ["biasO"], 128)
        P.op("dve", lambda e: e.memset(biasO[64:128, :, 0:64], NEGB), writes=["biasO"])

    late_done = {"v": False}
    if True:
        def ckpt(n):
            if stage <= n:
                raise _Stop()

        def run_tile(kind, t):
            sample = kind == "s"
            hstate["compact"] = sample and os.environ.get("NO_COMPACT", "0") != "1"
            nb = 1 if sample else 4
            ntok = nb * 128
            gB = [16] if sample else [4 * t + i for i in range(4)]
            xsrc = (lambda bi: xs.ap()) if sample else (lambda bi: xp.ap()[(4 * t + bi) * 128:(4 * t + bi + 1) * 128, :])

            if sample and not late_done["v"]:
                late_done["v"] = True
                late_setup()
            if sample:
                P.op("dve", lambda e: e.memset(biasP[:, :, 64:128], NEGB), writes=["biasP"])
                P.op("dve", lambda e: e.memset(biasO[0:64, :, 64:128], NEGB), writes=["biasO"])
                toeplitz(biasPB.rearrange("p (h q) -> p h q", h=16), biasPB_n, 64)
                P.op("dve", lambda e: e.memset(biasPB.rearrange("p (h q) -> p h q", h=16)[:, :, 0:64], NEGB), writes=biasPB_n)
                wstg = xtmp.rearrange("p (g i) -> p g i", g=16)[:, 0:8, :]
                P.dma("pool", "xld", lambda e: e.dma_start(out=wstg[0:64, :, 0:64], in_=wsT_d.ap()[0:64, :, 0:64]), writes=xtmp_n)
                P.dma("pool", "xld", lambda e: e.dma_start(out=wstg[64:128, :, 64:128], in_=wsT_d.ap()[0:64, :, 0:64]), writes=xtmp_n)
                P.op("dve", lambda e: e.memset(WsTs, 0.0), writes=WsTs_n)
                P.op("dve", lambda e: e.tensor_copy(out=WsTs[0:64, :, 0:64], in_=wstg[0:64, :, 0:64]), reads=xtmp_n, writes=WsTs_n)
                P.op("dve", lambda e: e.tensor_copy(out=WsTs[64:128, :, 64:128], in_=wstg[64:128, :, 64:128]), reads=xtmp_n, writes=WsTs_n)
                for sq in range(2):
                    slot = 2 + sq
                    st_ = xtmp[:, sq * 256:sq * 256 + 128]
                    sv_ = xtmp[:, sq * 256 + 128:sq * 256 + 256]
                    P.dma("pool", "xld", lambda e, st_=st_, sq=sq: e.dma_start(out=st_, in_=ckT.ap()[sq]), writes=xtmp_n)
                    P.dma("pool", "xld", lambda e, sv_=sv_, sq=sq: e.dma_start(out=sv_, in_=cv.ap()[sq]), writes=xtmp_n)
                    P.op("dve", lambda e, st_=st_, slot=slot: e.tensor_copy(out=kTz[0][slot][0:64, :], in_=st_[0:64, :]),
                         reads=xtmp_n, writes=[f"kTz0_{slot}"])
                    P.op("dve", lambda e, st_=st_, slot=slot: e.tensor_copy(out=kTz[1][slot][64:128, :], in_=st_[64:128, :]),
                         reads=xtmp_n, writes=[f"kTz1_{slot}"])
                    P.op("dve", lambda e, sv_=sv_, slot=slot: e.tensor_copy(out=Vb[slot][:], in_=sv_), reads=xtmp_n, writes=[f"Vb{slot}"])

            ckpt(0)
            for bi in range(nb):
                P.dma("pool", f"xld{bi}", lambda e, bi=bi: e.dma_start(out=X1[:, bi, :], in_=xsrc(bi)),
                      writes=[f"X1_{bi}_lo", f"X1_{bi}_hi"])
            for bi in range(nb):
                x1n = [f"X1_{bi}_lo", f"X1_{bi}_hi"]
                P.op("act", lambda e, bi=bi: e.activation(out=mixb, in_=X1[:, bi, :], func=AF.Square, accum_out=stat[:, 0:1]),
                     reads=x1n, writes=mixb_n + ["st0"])
                rstd_of(stat[:, 0:1], stat[:, 1:2], 1.0 / D, ["st0"], "n1")
                P.op("dve", lambda e, bi=bi: e.scalar_tensor_tensor(out=mixb, in0=X1[:, bi, :], scalar=stat[:, 1:2], in1=g_mix[:],
                                                                    op0=ALU.mult, op1=ALU.mult),
                     reads=x1n + ["n1_r", "g_mix"], writes=mixb_n)
                transposes_to(mixb, 16, Tr[:, :, bi * 128:(bi + 1) * 128], mixb_n, [f"T{bi}"])
            if not late_done["v"]:
                late_done["v"] = True
                late_setup()
            ckpt(1)
            for c in range(7):
                s = next_chunk("in")
                wdt = IN_W[c]
                wv = SLOT_AP[s][:, 0:16 * wdt].rearrange("p (c n) -> p c n", c=16)
                for bi in range(nb):
                    pb = 2 + (c * nb + bi) % 4
                    for dc in range(16):
                        P.op("pe", lambda e, pb=pb, dc=dc, bi=bi, wv=wv, wdt=wdt: e.matmul(
                            bank(pb, wdt), lhsT=Tr[:, dc, bi * 128:(bi + 1) * 128], rhs=wv[:, dc, :], start=(dc == 0), stop=(dc == 15)),
                            reads=[f"T{bi}"] + SLOT_N[s], writes=[f"ps{pb}"])
                    slot = gB[bi] % NSLOT
                    if c < 2:
                        P.op("dve", lambda e, pb=pb, bi=bi, c=c: e.tensor_copy(out=zqk[bi][:, c * 512:(c + 1) * 512], in_=bank(pb)),
                             reads=[f"ps{pb}"], writes=zqk_n[bi])
                    elif c == 2:
                        P.op("dve", lambda e, pb=pb, bi=bi: e.tensor_copy(out=zqk[bi][:, 1024:1152], in_=bank(pb, 128)),
                             reads=[f"ps{pb}"], writes=zqk_n[bi])
                        P.op("dve", lambda e, pb=pb, slot=slot: e.tensor_copy(out=Vb[slot][:], in_=ps[:, pb * 512 + 128:pb * 512 + 256]),
                             reads=[f"ps{pb}"], writes=[f"Vb{slot}"])
                        if sample or (t == 3 and bi == 3):
                            P.op("dve", lambda e, pb=pb: e.tensor_copy(out=v32, in_=ps[:, pb * 512 + 128:pb * 512 + 256]),
                                 reads=[f"ps{pb}"], writes=o32_n)
                            vo = vs_o if sample else vp_o
                            P.dma("pool", "ost", lambda e, vo=vo: e.dma_start(out=vo.ap(), in_=v32), reads=o32_n)
                    else:
                        half = (c - 3) % 2
                        which = "lo" if c < 5 else "hi"
                        col0 = (0 if c < 5 else 1024) + half * 512
                        P.op("act", lambda e, pb=pb, bi=bi, col0=col0: e.activation(out=X1[:, bi, col0:col0 + 512], in_=bank(pb),
                                                                                   func=AF.Gelu_apprx_tanh),
                             reads=[f"ps{pb}"], writes=[f"X1_{bi}_{which}"])

            ckpt(2)
            for bi in range(nb):
                B = gB[bi]
                slot = B % NSLOT
                zq3 = zqk[bi].rearrange("p (h d) -> p h d", d=64)
                tq3 = tmpq.rearrange("p (h d) -> p h d", d=64)
                P.op("act", lambda e, bi=bi: e.activation(out=tmpq, in_=zqk[bi], func=AF.Square),
                     reads=zqk_n[bi], writes=tmpq_n)
                P.op("dve", lambda e: e.tensor_reduce(out=stat[:, 8:26], in_=tq3, op=ALU.add, axis=AX.X),
                     reads=tmpq_n, writes=["qk_r"])
                rstd_of(stat[:, 8:26], stat[:, 8:26], 1.0 / 64, ["qk_r"], "qk")
                P.op("dve", lambda e, zq3=zq3: e.tensor_tensor(out=tq3, in0=zq3, in1=stat[:, 8:26].unsqueeze(2).to_broadcast([128, 18, 64]),
                                                             op=ALU.mult),
                     reads=zqk_n[bi] + ["qk_r"], writes=tmpq_n)
                P.op("dve", lambda e: e.tensor_tensor(out=qknb[:, 0:1024].rearrange("p (h d) -> p h d", d=64), in0=tq3[:, 0:16, :],
                                                      in1=gq_t[:].unsqueeze(1).to_broadcast([128, 16, 64]), op=ALU.mult),
                     reads=tmpq_n + ["gq"], writes=qknb_n)
                P.op("dve", lambda e: e.tensor_tensor(out=qknb[:, 1024:1152].rearrange("p (h d) -> p h d", d=64), in0=tq3[:, 16:18, :],
                                                      in1=gk_t[:].unsqueeze(1).to_broadcast([128, 2, 64]), op=ALU.mult),
                     reads=tmpq_n + ["gk"], writes=qknb_n)
                if sample or (t == 3 and bi == 3):
                    P.op("dve", lambda e: e.tensor_tensor(out=kn32.rearrange("p (h d) -> p h d", d=64), in0=tq3[:, 16:18, :],
                                                          in1=gk_t[:].unsqueeze(1).to_broadcast([128, 2, 64]), op=ALU.mult),
                         reads=tmpq_n + ["gk"], writes=o32_n)
                    ko = ks_o if sample else kp_o
                    P.dma("pool", "ost", lambda e, ko=ko: e.dma_start(out=ko.ap(), in_=kn32), reads=o32_n)
                for c in range(9):
                    P.op("pe", lambda e, c=c: e.matmul(ptr[:, c * 128:(c + 1) * 128], lhsT=qknb[:, c * 128:(c + 1) * 128],
                                                     rhs=identb[:], start=True, stop=True, is_transpose=True),
                         reads=qknb_n + ["identb"], writes=["ps0", "ps1"])
                P.op("act", lambda e: e.copy(out=qT[:], in_=ptr[:, 0:1024].rearrange("p (c t) -> p c t", c=8)),
                     reads=["ps0", "ps1"], writes=["qT"])
                P.op("dve", lambda e, slot=slot: e.tensor_copy(out=kTz[0][slot][0:64, :], in_=ptr[0:64, 1024:1152]),
                     reads=["ps0", "ps1"], writes=[f"kTz0_{slot}"])
                P.op("dve", lambda e, slot=slot: e.tensor_copy(out=kTz[1][slot][64:128, :], in_=ptr[64:128, 1024:1152]),
                     reads=["ps0", "ps1"], writes=[f"kTz1_{slot}"])
                if sample:
                    kts = [(2, biasP[:], ["biasP"]), (3, biasPB.rearrange("p (h q) -> p h q", h=16), biasPB_n), (slot, biasO[:], ["biasO"])]
                elif B == 0:
                    kts = [(slot, biasO[:], ["biasO"])]
                else:
                    kts = [((B - 1) % NSLOT, biasP[:], ["biasP"]), (slot, biasO[:], ["biasO"])]
                zv = X1[:, bi, 0:1024]
                uu = X1[:, bi, 1024:2048]
                P.op("act", lambda e, zv=zv: e.activation(out=vnb, in_=zv, func=AF.Square, accum_out=stat[:, 4:5]),
                     reads=[f"X1_{bi}_lo"], writes=vnb_n + ["st4"])
                rstd_of(stat[:, 4:5], stat[:, 5:6], 1.0 / 1024, ["st4"], "sv")
                if sample:
                    P.op("dve", lambda e, zv=zv: e.scalar_tensor_tensor(out=o32, in0=zv, scalar=stat[:, 5:6], in1=g_sgu[:],
                                                                        op0=ALU.mult, op1=ALU.mult),
                         reads=[f"X1_{bi}_lo", "sv_r", "g_sgu"], writes=o32_n)
                    P.dma("pool", "ost", lambda e: e.dma_start(out=sgv_o.ap(), in_=o32), reads=o32_n)
                P.op("dve", lambda e, zv=zv: e.scalar_tensor_tensor(out=vnb, in0=zv, scalar=stat[:, 5:6], in1=g_sgu[:],
                                                                    op0=ALU.mult, op1=ALU.mult),
                     reads=[f"X1_{bi}_lo", "sv_r", "g_sgu"], writes=vnb_n)
                Wg = WsTs if sample else WsT
                Wgn = WsTs_n if sample else ["WsT"]
                bg = bsTs if sample else bsT
                bgn = "bsTs" if sample else "bsT"
                cnt = 0
                for hk in range(2):
                    for ki, (ks_, bt, btn) in enumerate(kts):
                        for half in range(2):
                            pb = 4 + cnt % 2
                            ts_ = cnt % 2
                            cnt += 1
                            P.op("pe", lambda e, pb=pb, hk=hk, ks_=ks_, half=half: e.matmul(
                                bank(pb), lhsT=kTz[hk][ks_][:], rhs=qT[:, 4 * half:4 * half + 4, :], start=True, stop=True),
                                reads=[f"kTz{hk}_{ks_}", "qT"], writes=[f"ps{pb}"])
                            h0 = hk * 8 + 4 * half
                            P.op("dve", lambda e, pb=pb, ts_=ts_, bt=bt, h0=h0: e.scalar_tensor_tensor(
                                out=tmpS[ts_], in0=bank(pb), scalar=0.125, in1=bt[:, h0:h0 + 4, :].rearrange("p h q -> p (h q)"),
                                op0=ALU.mult, op1=ALU.add),
                                reads=[f"ps{pb}"] + btn, writes=tmpS_n[ts_])
                            P.op("act", lambda e, ts_=ts_, hk=hk, ki=ki, half=half: e.activation(
                                out=PT[(hk, ki)][:, half * 512:(half + 1) * 512], in_=tmpS[ts_], func=AF.Exp),
                                reads=tmpS_n[ts_], writes=PT_n[(hk, ki)])
                for g in range(8):
                    P.op("pe", lambda e, g=g, Wg=Wg: e.matmul(ps[:, (3 - g // 4) * 512 + (g % 4) * 128:(3 - g // 4) * 512 + (g % 4 + 1) * 128], lhsT=Wg[:, g, :],
                                                            rhs=vnb[:, g * 128:(g + 1) * 128], start=True, stop=True),
                         reads=Wgn + vnb_n, writes=[f"ps{3 - g // 4}"])
                for g in range(8):
                    P.op("dve", lambda e, g=g, bg=bg, bi=bi: e.scalar_tensor_tensor(
                        out=X1[:, bi, g * 128:(g + 1) * 128], in0=ps[:, (3 - g // 4) * 512 + (g % 4) * 128:(3 - g // 4) * 512 + (g % 4 + 1) * 128], scalar=bg[:, g:g + 1],
                        in1=X1[:, bi, 1024 + g * 128:1024 + (g + 1) * 128], op0=ALU.add, op1=ALU.mult),
                        reads=[f"ps{3 - g // 4}", bgn, f"X1_{bi}_hi"], writes=[f"X1_{bi}_lo"])
                P.op("act", lambda e, zv=zv: e.activation(out=mixb[:, 1024:2048], in_=zv, func=AF.Square, accum_out=stat[:, 6:7]),
                     reads=[f"X1_{bi}_lo"], writes=mixb_hi_n + ["st6"])
                rstd_of(stat[:, 6:7], stat[:, 7:8], 1.0 / 1024, ["st6"], "os")
                P.op("dve", lambda e, zv=zv: e.scalar_tensor_tensor(out=mixb[:, 1024:2048], in0=zv, scalar=stat[:, 7:8], in1=g_os[:],
                                                                    op0=ALU.mult, op1=ALU.mult),
                     reads=[f"X1_{bi}_lo", "os_r", "g_os"], writes=mixb_hi_n)
                nk = len(kts)
                for hk in range(2):
                    for g in range(8):
                        h = hk * 8 + g
                        for ki, (ks_, bt, btn) in enumerate(kts):
                            P.op("pe", lambda e, hk=hk, g=g, ki=ki, ks_=ks_, nk=nk: e.matmul(
                                ps[:, (6 + hk) * 512 + g * 64:(6 + hk) * 512 + (g + 1) * 64], lhsT=PT[(hk, ki)][:, g * 128:(g + 1) * 128],
                                rhs=Vb[ks_][:, hk * 64:(hk + 1) * 64], start=(ki == 0), stop=(ki == nk - 1)),
                                reads=PT_n[(hk, ki)] + [f"Vb{ks_}"], writes=[f"ps{6 + hk}"])
                        for ki, (ks_, bt, btn) in enumerate(kts):
                            P.op("pe", lambda e, hk=hk, g=g, ki=ki, h=h, nk=nk: e.matmul(
                                ps[:, 2 * 512 + h:2 * 512 + h + 1], lhsT=PT[(hk, ki)][:, g * 128:(g + 1) * 128],
                                rhs=onesb[:, 0:1], start=(ki == 0), stop=(ki == nk - 1)),
                                reads=PT_n[(hk, ki)] + ["onesb"], writes=["ps2"])
                P.op("dve", lambda e: e.tensor_tensor(out=stat[:, 32:48], in0=ps[:, 1024:1040], in1=esink[:], op=ALU.add),
                     reads=["ps2", "esink"], writes=["st_den"])
                P.op("dve", lambda e: e.reciprocal(out=stat[:, 32:48], in_=stat[:, 32:48]), reads=["st_den"], writes=["st_rden"])
                for hk in range(2):
                    P.op("dve", lambda e, hk=hk: e.tensor_tensor(
                        out=o32[:, hk * 512:(hk + 1) * 512].rearrange("p (g d) -> p g d", d=64),
                        in0=ps[:, (6 + hk) * 512:(7 + hk) * 512].rearrange("p (g d) -> p g d", d=64),
                        in1=stat[:, 32 + hk * 8:40 + hk * 8].unsqueeze(2).to_broadcast([128, 8, 64]), op=ALU.mult),
                        reads=[f"ps{6 + hk}", "st_rden"], writes=o32_n)
                P.op("act", lambda e: e.activation(out=mixb[:, 0:1024], in_=o32, func=AF.Square, accum_out=stat[:, 2:3]),
                     reads=o32_n, writes=mixb_lo_n + ["st2"])
                rstd_of(stat[:, 2:3], stat[:, 3:4], 1.0 / 1024, ["st2"], "oa")
                P.op("dve", lambda e: e.scalar_tensor_tensor(out=mixb[:, 0:1024], in0=o32, scalar=stat[:, 3:4], in1=g_oa[:],
                                                             op0=ALU.mult, op1=ALU.mult),
                     reads=o32_n + ["oa_r", "g_oa"], writes=mixb_lo_n)
                transposes_to(mixb, 16, Tr[:, :, bi * 128:(bi + 1) * 128], mixb_n, [f"T{bi}"])
                P.dma("pool", "xld2", lambda e, bi=bi: e.dma_start(out=X1[:, bi, :], in_=xsrc(bi)),
                      writes=[f"X1_{bi}_lo", f"X1_{bi}_hi"])

            ckpt(3)
            for c in range(4):
                s = next_chunk("out")
                wv = SLOT_AP[s].rearrange("p (c n) -> p c n", c=16)
                for bi in range(nb):
                    pb = 2 + (c * nb + bi) % 4
                    for kc in range(16):
                        P.op("pe", lambda e, pb=pb, kc=kc, bi=bi, wv=wv: e.matmul(
                            bank(pb), lhsT=Tr[:, kc, bi * 128:(bi + 1) * 128], rhs=wv[:, kc, :], start=(kc == 0), stop=(kc == 15)),
                            reads=[f"T{bi}"] + SLOT_N[s], writes=[f"ps{pb}"])
                    which = "lo" if c < 2 else "hi"
                    P.op("dve", lambda e, pb=pb, bi=bi, c=c: e.tensor_tensor(out=X1[:, bi, c * 512:(c + 1) * 512], in0=bank(pb),
                                                                           in1=X1[:, bi, c * 512:(c + 1) * 512], op=ALU.add),
                         reads=[f"ps{pb}", f"X1_{bi}_{which}"], writes=[f"X1_{bi}_{which}"])
            for bi in range(nb):
                x1n = [f"X1_{bi}_lo", f"X1_{bi}_hi"]
                P.op("act", lambda e, bi=bi: e.activation(out=mixb, in_=X1[:, bi, :], func=AF.Square, accum_out=stat[:, 0:1]),
                     reads=x1n, writes=mixb_n + ["st0"])
                rstd_of(stat[:, 0:1], stat[:, 1:2], 1.0 / D, ["st0"], "n1")
                P.op("dve", lambda e, bi=bi: e.scalar_tensor_tensor(out=mixb, in0=X1[:, bi, :], scalar=stat[:, 1:2], in1=g_ffn[:],
                                                                    op0=ALU.mult, op1=ALU.mult),
                     reads=x1n + ["n1_r", "g_ffn"], writes=mixb_n)
                transposes_to(mixb, 16, Tr[:, :, bi * 128:(bi + 1) * 128], mixb_n, [f"T{bi}"])

            ckpt(4)
            Tn = [f"T{bi}" for bi in range(nb)]
            ev = 0
            for fg in range(16):
                s = next_chunk("up")
                wv = SLOT_AP[s].rearrange("p (c n) -> p c n", c=16)
                for fc in range(4):
                    f = fg * 4 + fc
                    pb = f % 4
                    for dc in range(16):
                        P.op("pe", lambda e, pb=pb, dc=dc, fc=fc, wv=wv: e.matmul(
                            bank(pb, ntok), lhsT=wv[:, dc, fc * 128:(fc + 1) * 128], rhs=Tr[:, dc, 0:ntok], start=(dc == 0), stop=(dc == 15)),
                            reads=Tn + SLOT_N[s], writes=[f"ps{pb}"])
                    tr = ev % 2
                    ev += 1
                    P.op("act", lambda e, pb=pb, tr=tr: e.activation(out=tmpR[tr][:, 0:ntok], in_=bank(pb, ntok), func=AF.Relu),
                         reads=[f"ps{pb}"], writes=[f"tmpR{tr}"])
                    hv = hid(f, 0, ntok)
                    P.op("dve", lambda e, hv=hv, tr=tr: e.tensor_tensor(out=hv, in0=tmpR[tr][:, 0:ntok], in1=tmpR[tr][:, 0:ntok],
                                                                     op=ALU.mult),
                         reads=[f"tmpR{tr}"], writes=[hid_n(f)])

            ckpt(5)
            ydst = (lambda bi: ys.ap()) if sample else (lambda bi: yp.ap()[(4 * t + bi) * 128:(4 * t + bi + 1) * 128, :])
            yt = 0
            for r in range(2):
                for g8 in range(8):
                    s = next_chunk("down")
                    wv = SLOT_AP[s].rearrange("p (c n) -> p c n", c=8)
                    for fc in range(8):
                        f = g8 * 8 + fc
                        for bi in range(nb):
                            for nh in range(2):
                                pb = bi * 2 + nh
                                hv = hid(f, bi * 128, (bi + 1) * 128)
                                P.op("pe", lambda e, pb=pb, f=f, hv=hv, nh=nh, fc=fc, wv=wv: e.matmul(
                                    bank(pb), lhsT=hv, rhs=wv[:, fc, nh * 512:(nh + 1) * 512],
                                    start=(f == 0), stop=(f == 63)),
                                    reads=[hid_n(f)] + SLOT_N[s], writes=[f"ps{pb}"])
                which = "lo" if r == 0 else "hi"
                for bi in range(nb):
                    ysl = yt % 2
                    yt += 1
                    ytmp = Tr[:, 8 * ysl:8 * ysl + 4, :].rearrange("p c n -> p (c n)").bitcast(F32)
                    for nh in range(2):
                        pb = bi * 2 + nh
                        P.op("dve", lambda e, pb=pb, bi=bi, nh=nh, ytmp=ytmp, r=r: e.tensor_tensor(
                            out=ytmp[:, nh * 512:(nh + 1) * 512], in0=bank(pb), in1=X1[:, bi, r * 1024 + nh * 512:r * 1024 + (nh + 1) * 512],
                            op=ALU.add),
                            reads=[f"ps{pb}", f"X1_{bi}_{which}"], writes=[f"ytmp{ysl}"] + [f"T{i}" for i in range(4)])
                    P.dma("pool", f"yst{ysl}", lambda e, bi=bi, ytmp=ytmp, r=r: e.dma_start(out=ydst(bi)[:, r * 1024:(r + 1) * 1024], in_=ytmp),
                          reads=[f"ytmp{ysl}"])
            for ysl in range(2):
                P.op("dve", lambda e: e.memset(stat[:, 60:61], 0.0), writes=[f"ytmp{ysl}"] + [f"T{i}" for i in range(4)] + ["st60"])

        try:
            for kind, t in (TILES if tiles is None else tiles):
                run_tile(kind, t)
            if tiles is None:
                assert wstate["i"] == len(chunks), (wstate, len(chunks))
        except _Stop:
            pass
        P.finalize(block)
    return nc


def _t5_bucket_static(n):
    import math
    try:
        import jax
        import jax.numpy as jnp
        with jax.default_device(jax.devices("cpu")[0]):
            nn = jnp.asarray(n, dtype=jnp.int32)
            half, max_exact = 16, 8
            offset = jnp.where(nn < 0, half, 0)
            a = jnp.abs(nn)
            af = jnp.maximum(a, 1).astype(jnp.float32)
            large = max_exact + (jnp.log(af / max_exact) / math.log(128 / max_exact) * (half - max_exact)).astype(jnp.int32)
            large = jnp.minimum(large, half - 1)
            return np.asarray(offset + jnp.where(a < max_exact, a, large))
    except Exception:
        nn = np.asarray(n, dtype=np.int32)
        half, max_exact = 16, 8
        offset = np.where(nn < 0, half, 0)
        a = np.abs(nn)
        af = np.maximum(a, 1).astype(np.float32)
        large = max_exact + (np.log(af / np.float32(max_exact)) / np.float32(math.log(128 / max_exact))
                             * np.float32(half - max_exact)).astype(np.int32)
        large = np.minimum(large, half - 1)
        return offset + np.where(a < max_exact, a, large)


_NC_CACHE = {}


def prep_inputs(x_prompt, x_sample, cache_attn_k, cache_attn_v, rel_bias_table, ln_mix_g, w_in,
                q_norm_g, k_norm_g, attn_sinks, sgu_norm_g, sgu_w, sgu_b, out_norm_attn_g,
                out_norm_sgu_g, w_out, ln_ffn_g, w_ffn_up, w_ffn_down):
    f = lambda a: np.ascontiguousarray(np.asarray(a, dtype=np.float32))
    x_prompt, x_sample = f(x_prompt), f(x_sample)
    hk, g, d = np.meshgrid(np.arange(2), np.arange(8), np.arange(64), indexing="ij")
    qcols = ((hk * 8 + g) * 64 + d).transpose(1, 0, 2).reshape(-1)
    perm = np.concatenate([qcols, np.arange(1024, 1280), np.arange(2304, 3328), np.arange(1280, 2304)])
    w_in_p = np.asarray(w_in)[0][:, perm]

    def img_cols(w, col0, wdt):
        return w[:, col0:col0 + wdt].reshape(16, 128, wdt).transpose(1, 0, 2).reshape(128, 16 * wdt)

    IN_COL0 = [0, 512, 1024, 1280, 1792, 2304, 2816]
    IN_W = [512, 512, 256, 512, 512, 512, 512]
    w_in_img = f(np.concatenate([img_cols(w_in_p, c0, wd) for c0, wd in zip(IN_COL0, IN_W)], axis=1))
    wo = np.asarray(w_out)[0]
    w_out_img = f(np.concatenate([img_cols(wo, c * 512, 512) for c in range(4)], axis=1))
    wu = np.asarray(w_ffn_up)[0]
    w_up_img = f(np.concatenate([img_cols(wu, c * 512, 512) for c in range(16)], axis=1))
    wd_ = np.asarray(w_ffn_down)[0]
    w_down_img = f(np.concatenate(
        [wd_[g * 1024:(g + 1) * 1024, r * 1024:(r + 1) * 1024].reshape(8, 128, 1024).transpose(1, 0, 2).reshape(128, 8192)
         for r in range(2) for g in range(8)], axis=1))
    m = np.arange(384)
    bucket = _t5_bucket_static(255 - m)
    oh = np.zeros((32, 384), np.float32)
    oh[bucket, m] = 1.0
    common = {
        "table": f(rel_bias_table), "oh": oh, "ident": np.eye(128, dtype=np.float32),
        "w_in": w_in_img, "w_out": w_out_img, "w_up": w_up_img, "w_down": w_down_img,
        "g_mix": f(ln_mix_g), "g_ffn": f(ln_ffn_g), "g_sgu": f(sgu_norm_g), "g_oa": f(out_norm_attn_g), "g_os": f(out_norm_sgu_g),
        "gq": f(np.asarray(q_norm_g)[0][None, :]), "gk": f(np.asarray(k_norm_g)[0][None, :]), "sinks": f(np.asarray(attn_sinks)[0].reshape(1, 16)),
        "wsT": f(np.asarray(sgu_w)[0].transpose(2, 0, 1)), "bsT": f(np.asarray(sgu_b)[0].T),
    }
    ck = np.asarray(cache_attn_k)[0]
    cvv = np.asarray(cache_attn_v)[0]
    in_maps = []
    for c in range(NCORES):
        mm = dict(common)
        mm["xp"] = x_prompt[c]
        mm["xs"] = f(x_sample[2 * c:2 * c + 2].reshape(128, D))
        mm["ckT"] = f(ck[2 * c:2 * c + 2].reshape(2, 128, 128).transpose(0, 2, 1))
        mm["cv"] = f(cvv[2 * c:2 * c + 2].reshape(2, 128, 128))
        in_maps.append(mm)
    return in_maps


def kernel(**inputs):
    in_maps = prep_inputs(**inputs)
    if "nc" not in _NC_CACHE:
        _NC_CACHE["nc"] = build_program()
    nc = _NC_CACHE["nc"]
    res = run_bass_kernel_spmd(nc, in_maps, core_ids=list(range(NCORES)))
    R = res.results
    y_prompt = np.stack([R[c]["yp"] for c in range(NCORES)]).astype(np.float32)
    y_sample = np.concatenate([R[c]["ys"].reshape(2, 64, D) for c in range(NCORES)]).astype(np.float32)
    kpo = np.stack([R[c]["kp"].reshape(128, 2, 64) for c in range(NCORES)])[None].astype(np.float32)
    vpo = np.stack([R[c]["vp"].reshape(128, 2, 64) for c in range(NCORES)])[None].astype(np.float32)
    kso = np.concatenate([R[c]["ks"].reshape(2, 64, 2, 64) for c in range(NCORES)])[None].astype(np.float32)
    vso = np.concatenate([R[c]["vs"].reshape(2, 64, 2, 64) for c in range(NCORES)])[None].astype(np.float32)
    sgo = np.concatenate([R[c]["sgv"].reshape(2, 64, 1024) for c in range(NCORES)])[None].astype(np.float32)
    return (y_prompt, y_sample, kpo, vpo, kso, vso, sgo)
```

```python
import numpy as np
import concourse.bass as bass
import concourse.mybir as mybir
from concourse.bass_utils import run_bass_kernel_spmd

F32 = mybir.dt.float32
BF16 = mybir.dt.bfloat16
AF = mybir.ActivationFunctionType
ALU = mybir.AluOpType
AX = mybir.AxisListType

D = 2048
DFF = 8192
NCORES = 8
EPS = 1e-6
NEGB = -30000.0
import os
SAMPLE_RING = int(os.environ.get("SAMPLE_RING", "6"))
SAME_ENG_RAW_ONLY = os.environ.get("SAME_ENG_RAW_ONLY", "1") == "1"
ENGS = ("pe", "act", "dve", "pool", "sp")


class _Op:
    __slots__ = ("eng", "fn", "reads", "writes", "dma", "deps", "signal", "ev", "idx", "raw")

    def __init__(self, eng, fn, reads, writes, dma):
        self.eng, self.fn, self.reads, self.writes, self.dma = eng, fn, reads, writes, dma
        self.deps = set()
        self.raw = set()
        self.signal = False
        self.ev = None


class Prog:
    def __init__(self, nc, same_engine_sync=True):
        self.nc = nc
        self.ops = []
        self.last_w = {}
        self.readers = {}
        self.same_engine_sync = same_engine_sync

    def _add(self, eng, fn, reads, writes, dma=None):
        op = _Op(eng, fn, tuple(reads), tuple(writes), dma)
        op.idx = len(self.ops)
        for r in op.reads:
            w = self.last_w.get(r)
            if w is not None:
                op.deps.add(w)
                op.raw.add(w)
        for w_ in op.writes:
            w = self.last_w.get(w_)
            if w is not None:
                op.deps.add(w)
            latest = {}
            for rd in self.readers.get(w_, ()):
                ro = self.ops[rd]
                if ro.dma is not None:
                    op.deps.add(rd)
                else:
                    latest[ro.eng] = max(latest.get(ro.eng, -1), rd)
            op.deps.update(latest.values())
        for r in op.reads:
            self.readers.setdefault(r, []).append(op.idx)
        for w_ in op.writes:
            self.last_w[w_] = op.idx
            self.readers[w_] = []
        op.deps.discard(op.idx)
        self.ops.append(op)
        return op

    def op(self, eng, fn, reads=(), writes=()):
        return self._add(eng, fn, reads, writes)

    def dma(self, eng, sem_name, fn, reads=(), writes=()):
        return self._add(eng, fn, reads, writes, dma=sem_name)

    def finalize(self, block):
        import os
        nc = self.nc
        ops = self.ops
        nmax = int(os.environ.get("BISECT_N", "0"))
        if nmax:
            ops = ops[:nmax]
            for i, o in enumerate(ops[-3:]):
                print("last ops:", o.idx, o.eng, o.reads, o.writes, o.dma)
        print("n_ops", len(ops))
        for op in ops:
            for d in op.deps:
                p = ops[d]
                if p.dma is not None:
                    continue
                if p.eng == op.eng and (p.eng == "pe" or not self.same_engine_sync or (SAME_ENG_RAW_ONLY and d not in op.raw)):
                    continue
                p.signal = True
        eng_sem = {e: nc.alloc_semaphore("sem_" + e) for e in ENGS}
        self.all_sems = list(eng_sem.values())
        cnt = {e: 0 for e in ENGS}
        dsem, dcnt = {}, {}
        for op in ops:
            if op.dma is not None:
                if op.dma not in dsem:
                    dsem[op.dma] = nc.alloc_semaphore("dsem_" + op.dma)
                    self.all_sems.append(dsem[op.dma])
                    dcnt[op.dma] = 0
                dcnt[op.dma] += 16
                op.ev = (op.dma, dcnt[op.dma])
            elif op.signal:
                cnt[op.eng] += 1
                op.ev = (op.eng, cnt[op.eng])
        issued = {k: 0 for k in dsem}
        waits_for = []
        for op in ops:
            w = {}
            for d in op.deps:
                p = ops[d]
                if p.dma is not None:
                    w[("d", p.dma)] = max(w.get(("d", p.dma), 0), issued[p.dma])
                elif p.ev is not None:
                    if p.eng == op.eng and (p.eng == "pe" or (SAME_ENG_RAW_ONLY and d not in op.raw)):
                        continue
                    w[("e", p.eng)] = max(w.get(("e", p.eng), 0), p.ev[1])
            waits_for.append(w)
            if op.dma is not None:
                issued[op.dma] += 16
        final = dict(dcnt)

        def semof(k):
            return dsem[k[1]] if k[0] == "d" else eng_sem[k[1]]

        def emit(engname, engobj):
            waited = {}
            for op, w in zip(ops, waits_for):
                if op.eng != engname:
                    continue
                for k, v in w.items():
                    if waited.get(k, 0) >= v:
                        continue
                    engobj.wait_ge(semof(k), v)
                    waited[k] = v
                inst = op.fn(engobj)
                if op.dma is not None:
                    inst.then_inc(dsem[op.dma], 16)
                elif op.signal:
                    inst.then_inc(eng_sem[op.eng], 1)
            if engname == "sp":
                for k, v in final.items():
                    engobj.wait_ge(dsem[k], v)

        for sm in self.all_sems:
            nc.sync.sem_clear(sm)
        nc.all_engine_barrier()
        block = nc.Block().__enter__()
        self._block = block

        @block.tensor
        def _(e):
            emit("pe", e)

        @block.scalar
        def _(e):
            emit("act", e)

        @block.vector
        def _(e):
            emit("dve", e)

        @block.gpsimd
        def _(e):
            emit("pool", e)

        @block.sync
        def _(e):
            emit("sp", e)

        block.__exit__(None, None, None)


class _Stop(Exception):
    pass


def build_program(stage=99, tiles=None):
    nc = bass.Bass("TRN2", target_bir_lowering=False)

    def din(name, shape):
        return nc.dram_tensor(name, list(shape), F32, kind="ExternalInput")

    def dout(name, shape):
        return nc.dram_tensor(name, list(shape), F32, kind="ExternalOutput")

    xp = din("xp", (2048, D))
    xs = din("xs", (128, D))
    ckT = din("ckT", (2, 128, 128))
    cv = din("cv", (2, 128, 128))
    table = din("table", (32, 16))
    oh = din("oh", (32, 384))
    ident = din("ident", (128, 128))
    w_in = din("w_in", (128, 16 * 3328))
    w_out = din("w_out", (128, 4 * 8192))
    w_up = din("w_up", (128, 16 * 8192))
    w_down = din("w_down", (128, 16 * 8192))
    wimg = {"in": w_in, "out": w_out, "up": w_up, "down": w_down}
    wscr = {k: nc.dram_tensor("scr_" + k, list(v.shape), BF16, kind="ExternalOutput") for k, v in wimg.items()}
    g_mix_d = din("g_mix", (1, D))
    g_ffn_d = din("g_ffn", (1, D))
    g_sgu_d = din("g_sgu", (1, 1024))
    g_oa_d = din("g_oa", (1, 1024))
    g_os_d = din("g_os", (1, 1024))
    gq_d = din("gq", (1, 64))
    gk_d = din("gk", (1, 64))
    sinks_d = din("sinks", (1, 16))
    wsT_d = din("wsT", (128, 8, 128))
    bsT_d = din("bsT", (128, 8))
    scr = nc.dram_tensor("scr", [16, 384], F32, kind="Internal")

    yp = dout("yp", (2048, D))
    ys = dout("ys", (128, D))
    kp_o = dout("kp", (128, 128))
    vp_o = dout("vp", (128, 128))
    ks_o = dout("ks", (128, 128))
    vs_o = dout("vs", (128, 128))
    sgv_o = dout("sgv", (128, 1024))

    def sb(name, shape, dt):
        return nc.alloc_sbuf_tensor(name, list(shape), dt)

    identb = sb("identb", (128, 128), BF16)
    g_mix = sb("g_mix_s", (128, D), F32)
    g_ffn = sb("g_ffn_s", (128, D), F32)
    g_sgu = sb("g_sgu_s", (128, 1024), F32)
    g_oa = sb("g_oa_s", (128, 1024), F32)
    g_os = sb("g_os_s", (128, 1024), F32)
    gq_t = sb("gq_s", (128, 64), F32)
    gk_t = sb("gk_s", (128, 64), F32)
    esink = sb("esink", (128, 16), F32)
    epsb = sb("epsb", (128, 1), F32)
    onesb = sb("onesb", (128, 2), BF16)
    WsT = sb("WsT", (128, 8, 128), BF16)
    bsT = sb("bsT_s", (128, 8), F32)
    bsTs = sb("bsTs", (128, 8), F32)
    biasP = sb("biasP", (128, 16, 128), F32)
    biasO = sb("biasO", (128, 16, 128), F32)
    NSLOT = 5
    kTz = [[sb(f"kTz{hk}_{s}", (128, 128), BF16) for s in range(NSLOT)] for hk in range(2)]
    Vb = [sb(f"Vb{s}", (128, 128), BF16) for s in range(NSLOT)]
    qT = sb("qT", (128, 8, 128), BF16)
    stat = sb("stat", (128, 64), F32)
    tmpR = [sb(f"tmpR{i}", (128, 512), F32) for i in range(2)]
    X1 = sb("X1", (128, 4, D), F32)
    Hr = sb("Hr", (128, 16384), F32)
    Hrb = Hr[:, :].bitcast(BF16)
    Tr = sb("Tr", (128, 16, 512), BF16)
    Wr = [sb(f"Wr{i}", (128, 8192), BF16) for i in range(2)]
    ps = nc.alloc_psum_tensor("ps", [128, 4096], F32)

    def bank(i, n=512):
        return ps[:, i * 512:i * 512 + n]

    ptr = ps[:, 0:1024].bitcast(BF16)
    ptr2 = ps[:, 3072:4096].bitcast(BF16)
    tstate = {"n": 0}

    class HAlloc:
        def __init__(self):
            self.off = 0

        def take(self, nbytes, dt, shape=None):
            start = (self.off + 1023) // 1024 * 1024
            self.off = start + nbytes
            assert self.off <= 65536, self.off
            ap = Hr[:, start // 4:(start + nbytes) // 4]
            if dt == BF16:
                ap = Hrb[:, start // 2:(start + nbytes) // 2]
            names = [f"H{i}" for i in range(start // 1024, (start + nbytes + 1023) // 1024)]
            return ap, names

    ha = HAlloc()
    zqk, zqk_n = [], []
    for b in range(4):
        a, n = ha.take(1152 * 4, F32)
        zqk.append(a)
        zqk_n.append(n)
    tmpq, tmpq_n = ha.take(1152 * 4, F32)
    qknb, qknb_n = ha.take(1152 * 2, BF16)
    tmpS, tmpS_n = [], []
    for i in range(2):
        a, n = ha.take(512 * 4, F32)
        tmpS.append(a)
        tmpS_n.append(n)
    PT, PT_n = {}, {}
    for hk in range(2):
        for kt in range(3):
            a, n = ha.take(1024 * 2, BF16)
            PT[(hk, kt)] = a
            PT_n[(hk, kt)] = n
    o32, o32_n = ha.take(1024 * 4, F32)
    vnb, vnb_n = ha.take(1024 * 2, BF16)
    xtmp, xtmp_n = ha.take(D * 4, F32)
    hank, hank_n = xtmp, xtmp_n
    mixb, mixb_n = ha.take(D * 2, BF16)
    mixb_lo_n, mixb_hi_n = mixb_n[:2], mixb_n[2:]
    assert len(mixb_n) == 4
    h_used = ha.off
    tbl = tmpR[1][0:32, 384:400]
    ohs = tmpR[0][0:32, 0:384]
    srow = tmpR[1][0:16, 0:384]
    kn32 = o32[:, 0:128]
    v32 = o32[:, 128:256]
    zq1_start = 5 * 1024
    biasPB = Hr[:, zq1_start // 4:(zq1_start + 8192) // 4]
    wsts_start = 15 * 1024
    WsTs = Hrb[:, wsts_start // 2:(wsts_start + 2048) // 2].rearrange("p (g i) -> p g i", g=8)
    WsTs_n = ["H15", "H16"]
    biasPB_n = [f"H{i}" for i in range(zq1_start // 1024, zq1_start // 1024 + 8)]
    ALLH = [f"H{i}" for i in range(64)]

    hstate = {"compact": False}

    def hid(f, t0=0, t1=512):
        if hstate["compact"]:
            return Hrb[:, f * 128 + t0:f * 128 + t1]
        return Hrb[:, f * 512 + t0:f * 512 + t1]

    def hid_n(f):
        return f"H{f // 4}" if hstate["compact"] else f"H{f}"

    P = Prog(nc)
    _cst = {"n": 0}
    _orig_dma = P.dma

    def _dma(eng, sem_name, fn, reads=(), writes=()):
        if sem_name == "cst":
            _cst["n"] += 1
            sem_name = f"cst{_cst['n']}"
        return _orig_dma(eng, sem_name, fn, reads=reads, writes=writes)

    P.dma = _dma

    TILES = [("p", t) for t in range(4)] + [("s", 0)]
    chunks = []
    for _ in (TILES if tiles is None else tiles):
        chunks += [("in", c) for c in range(7)]
        chunks += [("out", c) for c in range(4)]
        chunks += [("up", c) for c in range(16)]
        chunks += [("down", r, g) for r in range(2) for g in range(8)]
    IN_COL0 = [0, 512, 1024, 1280, 1792, 2304, 2816]
    IN_W = [512, 512, 256, 512, 512, 512, 512]
    wstate = {"i": 0, "issued": 0}

    NCH = 43
    SLOT_AP = [Wr[0][:, :], Wr[1][:, :], X1[:, 1:3, :].rearrange("p b n -> p (b n)").bitcast(BF16),
               Hrb[:, 8192:16384], Hrb[:, 16384:24576], Hrb[:, 24576:32768]]
    SLOT_N = [["w0"], ["w1"], ["X1_1_lo", "X1_1_hi", "X1_2_lo", "X1_2_hi"],
              [f"H{i}" for i in range(16, 32)], [f"H{i}" for i in range(32, 48)], [f"H{i}" for i in range(48, 64)]]
    n_tiles_run = len(TILES if tiles is None else tiles)
    tile_kinds = [k for k, _ in (TILES if tiles is None else tiles)]

    def slot_of(i):
        j = i % NCH
        if tile_kinds[i // NCH] == "s":
            if SAMPLE_RING <= 2:
                return i % 2
            return [0, 1, 2][j % 3] if j < 11 else [2, 0, 1, 3, 4, 5][(j - 11) % 6]
        return i % 2

    def ahead_of(i):
        j = i % NCH
        if tile_kinds[i // NCH] == "s":
            if SAMPLE_RING <= 2:
                return 1
            return 2 if j < 11 else 5
        return 1

    slot_last = {}

    def issue_chunk(i):
        c = chunks[i]
        s = slot_of(i)
        assert slot_last.get(s, -1) < wstate["i"], (i, s, slot_last.get(s), wstate["i"])
        slot_last[s] = i
        tile_i, j = i // NCH, i % NCH
        multi = n_tiles_run >= 3
        if not multi:
            cast, wback = tile_i == 0, tile_i == 0
        elif tile_i == 0:
            cast, wback = True, (j % 2 == 0)
        elif tile_i == 1:
            cast, wback = (j % 2 == 1), (j % 2 == 1)
        else:
            cast, wback = False, False
        kind = c[0]
        if kind == "in":
            off, ln = 16 * IN_COL0[c[1]], 16 * IN_W[c[1]]
        elif kind == "down":
            off, ln = (c[1] * 8 + c[2]) * 8192, 8192
        else:
            off, ln = c[1] * 8192, 8192
        d = SLOT_AP[s][:, 0:ln]
        rname = f"scr_{kind}_{off}"
        if cast:
            src = wimg[kind].ap()[:, off:off + ln]
            P.dma("pool", f"w{s}q", lambda e, d=d, src=src: e.dma_start(out=d, in_=src), writes=SLOT_N[s])
            if wback:
                dsts = wscr[kind].ap()[:, off:off + ln]
                P.dma("sp", "wst", lambda e, d=d, dsts=dsts: e.dma_start(out=dsts, in_=d), reads=SLOT_N[s], writes=[rname])
        else:
            src = wscr[kind].ap()[:, off:off + ln]
            P.dma("sp", f"w{s}", lambda e, d=d, src=src: e.dma_start(out=d, in_=src), reads=[rname], writes=SLOT_N[s])

    def next_chunk(kind):
        i = wstate["i"]
        assert chunks[i][0] == kind, (chunks[i], kind)
        while wstate["issued"] <= min(i + ahead_of(i), len(chunks) - 1):
            nxt = wstate["issued"]
            if nxt // NCH != i // NCH and nxt > i + 1:
                break
            issue_chunk(nxt)
            wstate["issued"] += 1
        wstate["i"] += 1
        return slot_of(i)

    def rstd_of(ss_col, r_col, inv_n, reads, tag):
        P.op("act", lambda e: e.activation(out=r_col, in_=ss_col, func=AF.Sqrt, scale=inv_n, bias=epsb[:]),
             reads=reads + ["epsb"], writes=[tag + "_r"])
        P.op("dve", lambda e: e.reciprocal(out=r_col, in_=r_col), reads=[tag + "_r"], writes=[tag + "_r"])

    def transposes_to(src_tile, nch, dst_ap, src_names, dst_names, evac_eng="act"):
        k = tstate["n"] % 2
        tstate["n"] += 1
        pt = [ptr, ptr2][k]
        pn = [["ps0", "ps1"], ["ps6", "ps7"]][k]
        for c in range(nch):
            P.op("pe", lambda e, c=c, pt=pt: e.matmul(pt[:, c * 128:(c + 1) * 128], lhsT=src_tile[:, c * 128:(c + 1) * 128],
                                                    rhs=identb[:], start=True, stop=True, is_transpose=True),
                 reads=src_names + ["identb"], writes=pn)
        src = pt[:, 0:nch * 128].rearrange("p (c t) -> p c t", c=nch)
        if evac_eng == "act":
            P.op("act", lambda e: e.copy(out=dst_ap, in_=src), reads=pn, writes=dst_names)
        else:
            P.op("dve", lambda e: e.tensor_copy(out=dst_ap, in_=src), reads=pn, writes=dst_names)

    def toeplitz(dst, dst_names, c0):
        hk_ = hank.rearrange("p (h q) -> p h q", h=16)
        P.dma("sp", "cst", lambda e: e.dma_start(out=hk_, in_=bass.AP(tensor=scr, offset=c0, ap=[[1, 128], [384, 16], [1, 128]])),
              reads=["scr"], writes=hank_n)
        t = hank
        rev = bass.AP(tensor=t.tensor, offset=t.offset + 127, ap=[list(t.ap[0]), [128, 16], [-1, 128]])
        P.op("dve", lambda e: e.tensor_copy(out=dst, in_=rev), reads=hank_n, writes=dst_names)

    block = None
    if True:
        P.op("dve", lambda e: e.memset(epsb[:], EPS), writes=["epsb"])
        P.op("dve", lambda e: e.memset(onesb[:], 1.0), writes=["onesb"])
        for hk in range(2):
            for s in range(NSLOT):
                P.op("dve", lambda e, hk=hk, s=s: e.memset(kTz[hk][s][:], 0.0), writes=[f"kTz{hk}_{s}"])

        def bc_load(dst, src, n, name):
            P.dma("sp", "cst", lambda e: e.dma_start(out=dst[:], in_=src.ap().partition_broadcast(128)[:, 0, :]), writes=[name])

        idf = xtmp[:, 0:128]
        P.dma("sp", "cst", lambda e: e.dma_start(out=idf, in_=ident.ap()), writes=xtmp_n)
        P.op("dve", lambda e: e.tensor_copy(out=identb[:], in_=idf), reads=xtmp_n, writes=["identb"])
        bc_load(g_mix, g_mix_d, D, "g_mix")
        P.dma("sp", "cst", lambda e: e.dma_start(out=tbl, in_=table.ap()), writes=["tmpR1"])
        P.dma("sp", "cst", lambda e: e.dma_start(out=ohs, in_=oh.ap()), writes=["tmpR0"])

    def late_setup():
        bc_load(g_ffn, g_ffn_d, D, "g_ffn")
        bc_load(g_sgu, g_sgu_d, 1024, "g_sgu")
        bc_load(g_oa, g_oa_d, 1024, "g_oa")
        bc_load(g_os, g_os_d, 1024, "g_os")
        bc_load(gq_t, gq_d, 64, "gq")
        bc_load(gk_t, gk_d, 64, "gk")
        bc_load(esink, sinks_d, 16, "esink")
        P.op("act", lambda e: e.activation(out=esink[:], in_=esink[:], func=AF.Exp), reads=["esink"], writes=["esink"])
        wst = hank.rearrange("p (g i) -> p g i", g=16)[:, 0:8, :]
        P.dma("sp", "cst", lambda e: e.dma_start(out=wst, in_=wsT_d.ap()), writes=hank_n)
        P.op("dve", lambda e: e.tensor_copy(out=WsT[:], in_=wst), reads=hank_n, writes=["WsT"])
        P.op("dve", lambda e: e.memset(WsT[64:128, :, 0:64], 0.0), reads=[], writes=["WsT"])
        P.dma("sp", "cst", lambda e: e.dma_start(out=bsT[:], in_=bsT_d.ap()), writes=["bsT"])
        P.dma("sp", "cst", lambda e: e.dma_start(out=bsTs[0:64, :], in_=bsT_d.ap()[0:64, :]), writes=["bsTs"])
        P.dma("sp", "cst", lambda e: e.dma_start(out=bsTs[64:128, :], in_=bsT_d.ap()[0:64, :]), writes=["bsTs"])
        P.op("pe", lambda e: e.matmul(bank(2)[0:16, 0:384], lhsT=tbl, rhs=ohs, start=True, stop=True),
             reads=["tmpR0", "tmpR1"], writes=["ps2"])
        P.op("dve", lambda e: e.tensor_copy(out=srow, in_=bank(2)[0:16, 0:384]), reads=["ps2"], writes=["tmpR1"])
        P.dma("sp", "cst", lambda e: e.dma_start(out=scr.ap(), in_=srow), reads=["tmpR1"], writes=["scr"])
        toeplitz(biasP[:], ["biasP"], 0)
        P.op("dve", lambda e: e.memset(biasP[0:64, :, 64:128], NEGB), writes=["biasP"])
        toeplitz(biasO[:], ["biasO"], 128)
        P.op("dve", lambda e: e.memset(biasO[64:128, :, 0:64], NEGB), writes=["biasO"])

    late_done = {"v": False}
    if True:
        def ckpt(n):
            if stage <= n:
                raise _Stop()

        def run_tile(kind, t):
            sample = kind == "s"
            hstate["compact"] = sample and os.environ.get("NO_COMPACT", "0") != "1"
            nb = 1 if sample else 4
            ntok = nb * 128
            gB = [16] if sample else [4 * t + i for i in range(4)]
            xsrc = (lambda bi: xs.ap()) if sample else (lambda bi: xp.ap()[(4 * t + bi) * 128:(4 * t + bi + 1) * 128, :])

            if sample and not late_done["v"]:
                late_done["v"] = True
                late_setup()
            if sample:
                P.op("dve", lambda e: e.memset(biasP[:, :, 64:128], NEGB), writes=["biasP"])
                P.op("dve", lambda e: e.memset(biasO[0:64, :, 64:128], NEGB), writes=["biasO"])
                toeplitz(biasPB.rearrange("p (h q) -> p h q", h=16), biasPB_n, 64)
                P.op("dve", lambda e: e.memset(biasPB.rearrange("p (h q) -> p h q", h=16)[:, :, 0:64], NEGB), writes=biasPB_n)
                wstg = xtmp.rearrange("p (g i) -> p g i", g=16)[:, 0:8, :]
                P.dma("pool", "xld", lambda e: e.dma_start(out=wstg[0:64, :, 0:64], in_=wsT_d.ap()[0:64, :, 0:64]), writes=xtmp_n)
                P.dma("pool", "xld", lambda e: e.dma_start(out=wstg[64:128, :, 64:128], in_=wsT_d.ap()[0:64, :, 0:64]), writes=xtmp_n)
                P.op("dve", lambda e: e.memset(WsTs, 0.0), writes=WsTs_n)
                P.op("dve", lambda e: e.tensor_copy(out=WsTs[0:64, :, 0:64], in_=wstg[0:64, :, 0:64]), reads=xtmp_n, writes=WsTs_n)
                P.op("dve", lambda e: e.tensor_copy(out=WsTs[64:128, :, 64:128], in_=wstg[64:128, :, 64:128]), reads=xtmp_n, writes=WsTs_n)
                for sq in range(2):
                    slot = 2 + sq
                    st_ = xtmp[:, sq * 256:sq * 256 + 128]
                    sv_ = xtmp[:, sq * 256 + 128:sq * 256 + 256]
                    P.dma("pool", "xld", lambda e, st_=st_, sq=sq: e.dma_start(out=st_, in_=ckT.ap()[sq]), writes=xtmp_n)
                    P.dma("pool", "xld", lambda e, sv_=sv_, sq=sq: e.dma_start(out=sv_, in_=cv.ap()[sq]), writes=xtmp_n)
                    P.op("dve", lambda e, st_=st_, slot=slot: e.tensor_copy(out=kTz[0][slot][0:64, :], in_=st_[0:64, :]),
                         reads=xtmp_n, writes=[f"kTz0_{slot}"])
                    P.op("dve", lambda e, st_=st_, slot=slot: e.tensor_copy(out=kTz[1][slot][64:128, :], in_=st_[64:128, :]),
                         reads=xtmp_n, writes=[f"kTz1_{slot}"])
                    P.op("dve", lambda e, sv_=sv_, slot=slot: e.tensor_copy(out=Vb[slot][:], in_=sv_), reads=xtmp_n, writes=[f"Vb{slot}"])

            ckpt(0)
            for bi in range(nb):
                P.dma("pool", f"xld{bi}", lambda e, bi=bi: e.dma_start(out=X1[:, bi, :], in_=xsrc(bi)),
                      writes=[f"X1_{bi}_lo", f"X1_{bi}_hi"])
            for bi in range(nb):
                x1n = [f"X1_{bi}_lo", f"X1_{bi}_hi"]
                P.op("act", lambda e, bi=bi: e.activation(out=mixb, in_=X1[:, bi, :], func=AF.Square, accum_out=stat[:, 0:1]),
                     reads=x1n, writes=mixb_n + ["st0"])
                rstd_of(stat[:, 0:1], stat[:, 1:2], 1.0 / D, ["st0"], "n1")
                P.op("dve", lambda e, bi=bi: e.scalar_tensor_tensor(out=mixb, in0=X1[:, bi, :], scalar=stat[:, 1:2], in1=g_mix[:],
                                                                    op0=ALU.mult, op1=ALU.mult),
                     reads=x1n + ["n1_r", "g_mix"], writes=mixb_n)
                transposes_to(mixb, 16, Tr[:, :, bi * 128:(bi + 1) * 128], mixb_n, [f"T{bi}"])
            if not late_done["v"]:
                late_done["v"] = True
                late_setup()
            ckpt(1)
            for c in range(7):
                s = next_chunk("in")
                wdt = IN_W[c]
                wv = SLOT_AP[s][:, 0:16 * wdt].rearrange("p (c n) -> p c n", c=16)
                for bi in range(nb):
                    pb = 2 + (c * nb + bi) % 4
                    for dc in range(16):
                        P.op("pe", lambda e, pb=pb, dc=dc, bi=bi, wv=wv, wdt=wdt: e.matmul(
                            bank(pb, wdt), lhsT=Tr[:, dc, bi * 128:(bi + 1) * 128], rhs=wv[:, dc, :], start=(dc == 0), stop=(dc == 15)),
                            reads=[f"T{bi}"] + SLOT_N[s], writes=[f"ps{pb}"])
                    slot = gB[bi] % NSLOT
                    if c < 2:
                        P.op("dve", lambda e, pb=pb, bi=bi, c=c: e.tensor_copy(out=zqk[bi][:, c * 512:(c + 1) * 512], in_=bank(pb)),
                             reads=[f"ps{pb}"], writes=zqk_n[bi])
                    elif c == 2:
                        P.op("dve", lambda e, pb=pb, bi=bi: e.tensor_copy(out=zqk[bi][:, 1024:1152], in_=bank(pb, 128)),
                             reads=[f"ps{pb}"], writes=zqk_n[bi])
                        P.op("dve", lambda e, pb=pb, slot=slot: e.tensor_copy(out=Vb[slot][:], in_=ps[:, pb * 512 + 128:pb * 512 + 256]),
                             reads=[f"ps{pb}"], writes=[f"Vb{slot}"])
                        if sample or (t == 3 and bi == 3):
                            P.op("dve", lambda e, pb=pb: e.tensor_copy(out=v32, in_=ps[:, pb * 512 + 128:pb * 512 + 256]),
                                 reads=[f"ps{pb}"], writes=o32_n)
                            vo = vs_o if sample else vp_o
                            P.dma("pool", "ost", lambda e, vo=vo: e.dma_start(out=vo.ap(), in_=v32), reads=o32_n)
                    else:
                        half = (c - 3) % 2
                        which = "lo" if c < 5 else "hi"
                        col0 = (0 if c < 5 else 1024) + half * 512
                        P.op("act", lambda e, pb=pb, bi=bi, col0=col0: e.activation(out=X1[:, bi, col0:col0 + 512], in_=bank(pb),
                                                                                   func=AF.Gelu_apprx_tanh),
                             reads=[f"ps{pb}"], writes=[f"X1_{bi}_{which}"])

            ckpt(2)
            for bi in range(nb):
                B = gB[bi]
                slot = B % NSLOT
                zq3 = zqk[bi].rearrange("p (h d) -> p h d", d=64)
                tq3 = tmpq.rearrange("p (h d) -> p h d", d=64)
                P.op("act", lambda e, bi=bi: e.activation(out=tmpq, in_=zqk[bi], func=AF.Square),
                     reads=zqk_n[bi], writes=tmpq_n)
                P.op("dve", lambda e: e.tensor_reduce(out=stat[:, 8:26], in_=tq3, op=ALU.add, axis=AX.X),
                     reads=tmpq_n, writes=["qk_r"])
                rstd_of(stat[:, 8:26], stat[:, 8:26], 1.0 / 64, ["qk_r"], "qk")
                P.op("dve", lambda e, zq3=zq3: e.tensor_tensor(out=tq3, in0=zq3, in1=stat[:, 8:26].unsqueeze(2).to_broadcast([128, 18, 64]),
                                                             op=ALU.mult),
                     reads=zqk_n[bi] + ["qk_r"], writes=tmpq_n)
                P.op("dve", lambda e: e.tensor_tensor(out=qknb[:, 0:1024].rearrange("p (h d) -> p h d", d=64), in0=tq3[:, 0:16, :],
                                                      in1=gq_t[:].unsqueeze(1).to_broadcast([128, 16, 64]), op=ALU.mult),
                     reads=tmpq_n + ["gq"], writes=qknb_n)
                P.op("dve", lambda e: e.tensor_tensor(out=qknb[:, 1024:1152].rearrange("p (h d) -> p h d", d=64), in0=tq3[:, 16:18, :],
                                                      in1=gk_t[:].unsqueeze(1).to_broadcast([128, 2, 64]), op=ALU.mult),
                     reads=tmpq_n + ["gk"], writes=qknb_n)
                if sample or (t == 3 and bi == 3):
                    P.op("dve", lambda e: e.tensor_tensor(out=kn32.rearrange("p (h d) -> p h d", d=64), in0=tq3[:, 16:18, :],
                                                          in1=gk_t[:].unsqueeze(1).to_broadcast([128, 2, 64]), op=ALU.mult),
                         reads=tmpq_n + ["gk"], writes=o32_n)
                    ko = ks_o if sample else kp_o
                    P.dma("pool", "ost", lambda e, ko=ko: e.dma_start(out=ko.ap(), in_=kn32), reads=o32_n)
                for c in range(9):
                    P.op("pe", lambda e, c=c: e.matmul(ptr[:, c * 128:(c + 1) * 128], lhsT=qknb[:, c * 128:(c + 1) * 128],
                                                     rhs=identb[:], start=True, stop=True, is_transpose=True),
                         reads=qknb_n + ["identb"], writes=["ps0", "ps1"])
                P.op("act", lambda e: e.copy(out=qT[:], in_=ptr[:, 0:1024].rearrange("p (c t) -> p c t", c=8)),
                     reads=["ps0", "ps1"], writes=["qT"])
                P.op("dve", lambda e, slot=slot: e.tensor_copy(out=kTz[0][slot][0:64, :], in_=ptr[0:64, 1024:1152]),
                     reads=["ps0", "ps1"], writes=[f"kTz0_{slot}"])
                P.op("dve", lambda e, slot=slot: e.tensor_copy(out=kTz[1][slot][64:128, :], in_=ptr[64:128, 1024:1152]),
                     reads=["ps0", "ps1"], writes=[f"kTz1_{slot}"])
                if sample:
                    kts = [(2, biasP[:], ["biasP"]), (3, biasPB.rearrange("p (h q) -> p h q", h=16), biasPB_n), (slot, biasO[:], ["biasO"])]
                elif B == 0:
                    kts = [(slot, biasO[:], ["biasO"])]
                else:
                    kts = [((B - 1) % NSLOT, biasP[:], ["biasP"]), (slot, biasO[:], ["biasO"])]
                zv = X1[:, bi, 0:1024]
                uu = X1[:, bi, 1024:2048]
                P.op("act", lambda e, zv=zv: e.activation(out=vnb, in_=zv, func=AF.Square, accum_out=stat[:, 4:5]),
                     reads=[f"X1_{bi}_lo"], writes=vnb_n + ["st4"])
                rstd_of(stat[:, 4:5], stat[:, 5:6], 1.0 / 1024, ["st4"], "sv")
                if sample:
                    P.op("dve", lambda e, zv=zv: e.scalar_tensor_tensor(out=o32, in0=zv, scalar=stat[:, 5:6], in1=g_sgu[:],
                                                                        op0=ALU.mult, op1=ALU.mult),
                         reads=[f"X1_{bi}_lo", "sv_r", "g_sgu"], writes=o32_n)
                    P.dma("pool", "ost", lambda e: e.dma_start(out=sgv_o.ap(), in_=o32), reads=o32_n)
                P.op("dve", lambda e, zv=zv: e.scalar_tensor_tensor(out=vnb, in0=zv, scalar=stat[:, 5:6], in1=g_sgu[:],
                                                                    op0=ALU.mult, op1=ALU.mult),
                     reads=[f"X1_{bi}_lo", "sv_r", "g_sgu"], writes=vnb_n)
                Wg = WsTs if sample else WsT
                Wgn = WsTs_n if sample else ["WsT"]
                bg = bsTs if sample else bsT
                bgn = "bsTs" if sample else "bsT"
                cnt = 0
                for hk in range(2):
                    for ki, (ks_, bt, btn) in enumerate(kts):
                        for half in range(2):
                            pb = 4 + cnt % 2
                            ts_ = cnt % 2
                            cnt += 1
                            P.op("pe", lambda e, pb=pb, hk=hk, ks_=ks_, half=half: e.matmul(
                                bank(pb), lhsT=kTz[hk][ks_][:], rhs=qT[:, 4 * half:4 * half + 4, :], start=True, stop=True),
                                reads=[f"kTz{hk}_{ks_}", "qT"], writes=[f"ps{pb}"])
                            h0 = hk * 8 + 4 * half
                            P.op("dve", lambda e, pb=pb, ts_=ts_, bt=bt, h0=h0: e.scalar_tensor_tensor(
                                out=tmpS[ts_], in0=bank(pb), scalar=0.125, in1=bt[:, h0:h0 + 4, :].rearrange("p h q -> p (h q)"),
                                op0=ALU.mult, op1=ALU.add),
                                reads=[f"ps{pb}"] + btn, writes=tmpS_n[ts_])
                            P.op("act", lambda e, ts_=ts_, hk=hk, ki=ki, half=half: e.activation(
                                out=PT[(hk, ki)][:, half * 512:(half + 1) * 512], in_=tmpS[ts_], func=AF.Exp),
                                reads=tmpS_n[ts_], writes=PT_n[(hk, ki)])
                for g in range(8):
                    P.op("pe", lambda e, g=g, Wg=Wg: e.matmul(ps[:, (3 - g // 4) * 512 + (g % 4) * 128:(3 - g // 4) * 512 + (g % 4 + 1) * 128], lhsT=Wg[:, g, :],
                                                            rhs=vnb[:, g * 128:(g + 1) * 128], start=True, stop=True),
                         reads=Wgn + vnb_n, writes=[f"ps{3 - g // 4}"])
                for g in range(8):
                    P.op("dve", lambda e, g=g, bg=bg, bi=bi: e.scalar_tensor_tensor(
                        out=X1[:, bi, g * 128:(g + 1) * 128], in0=ps[:, (3 - g // 4) * 512 + (g % 4) * 128:(3 - g // 4) * 512 + (g % 4 + 1) * 128], scalar=bg[:, g:g + 1],
                        in1=X1[:, bi, 1024 + g * 128:1024 + (g + 1) * 128], op0=ALU.add, op1=ALU.mult),
                        reads=[f"ps{3 - g // 4}", bgn, f"X1_{bi}_hi"], writes=[f"X1_{bi}_lo"])
                P.op("act", lambda e, zv=zv: e.activation(out=mixb[:, 1024:2048], in_=zv, func=AF.Square, accum_out=stat[:, 6:7]),
                     reads=[f"X1_{bi}_lo"], writes=mixb_hi_n + ["st6"])
                rstd_of(stat[:, 6:7], stat[:, 7:8], 1.0 / 1024, ["st6"], "os")
                P.op("dve", lambda e, zv=zv: e.scalar_tensor_tensor(out=mixb[:, 1024:2048], in0=zv, scalar=stat[:, 7:8], in1=g_os[:],
                                                                    op0=ALU.mult, op1=ALU.mult),
                     reads=[f"X1_{bi}_lo", "os_r", "g_os"], writes=mixb_hi_n)
                nk = len(kts)
                for hk in range(2):
                    for g in range(8):
                        h = hk * 8 + g
                        for ki, (ks_, bt, btn) in enumerate(kts):
                            P.op("pe", lambda e, hk=hk, g=g, ki=ki, ks_=ks_, nk=nk: e.matmul(
                                ps[:, (6 + hk) * 512 + g * 64:(6 + hk) * 512 + (g + 1) * 64], lhsT=PT[(hk, ki)][:, g * 128:(g + 1) * 128],
                                rhs=Vb[ks_][:, hk * 64:(hk + 1) * 64], start=(ki == 0), stop=(ki == nk - 1)),
                                reads=PT_n[(hk, ki)] + [f"Vb{ks_}"], writes=[f"ps{6 + hk}"])
                        for ki, (ks_, bt, btn) in enumerate(kts):
                            P.op("pe", lambda e, hk=hk, g=g, ki=ki, h=h, nk=nk: e.matmul(
                                ps[:, 2 * 512 + h:2 * 512 + h + 1], lhsT=PT[(hk, ki)][:, g * 128:(g + 1) * 128],
                                rhs=onesb[:, 0:1], start=(ki == 0), stop=(ki == nk - 1)),
                                reads=PT_n[(hk, ki)] + ["onesb"], writes=["ps2"])
                P.op("dve", lambda e: e.tensor_tensor(out=stat[:, 32:48], in0=ps[:, 1024:1040], in1=esink[:], op=ALU.add),
                     reads=["ps2", "esink"], writes=["st_den"])
                P.op("dve", lambda e: e.reciprocal(out=stat[:, 32:48], in_=stat[:, 32:48]), reads=["st_den"], writes=["st_rden"])
                for hk in range(2):
                    P.op("dve", lambda e, hk=hk: e.tensor_tensor(
                        out=o32[:, hk * 512:(hk + 1) * 512].rearrange("p (g d) -> p g d", d=64),
                        in0=ps[:, (6 + hk) * 512:(7 + hk) * 512].rearrange("p (g d) -> p g d", d=64),
                        in1=stat[:, 32 + hk * 8:40 + hk * 8].unsqueeze(2).to_broadcast([128, 8, 64]), op=ALU.mult),
                        reads=[f"ps{6 + hk}", "st_rden"], writes=o32_n)
                P.op("act", lambda e: e.activation(out=mixb[:, 0:1024], in_=o32, func=AF.Square, accum_out=stat[:, 2:3]),
                     reads=o32_n, writes=mixb_lo_n + ["st2"])
                rstd_of(stat[:, 2:3], stat[:, 3:4], 1.0 / 1024, ["st2"], "oa")
                P.op("dve", lambda e: e.scalar_tensor_tensor(out=mixb[:, 0:1024], in0=o32, scalar=stat[:, 3:4], in1=g_oa[:],
                                                             op0=ALU.mult, op1=ALU.mult),
                     reads=o32_n + ["oa_r", "g_oa"], writes=mixb_lo_n)
                transposes_to(mixb, 16, Tr[:, :, bi * 128:(bi + 1) * 128], mixb_n, [f"T{bi}"])
                P.dma("pool", "xld2", lambda e, bi=bi: e.dma_start(out=X1[:, bi, :], in_=xsrc(bi)),
                      writes=[f"X1_{bi}_lo", f"X1_{bi}_hi"])

            ckpt(3)
            for c in range(4):
                s = next_chunk("out")
                wv = SLOT_AP[s].rearrange("p (c n) -> p c n", c=16)
                for bi in range(nb):
                    pb = 2 + (c * nb + bi) % 4
                    for kc in range(16):
                        P.op("pe", lambda e, pb=pb, kc=kc, bi=bi, wv=wv: e.matmul(
                            bank(pb), lhsT=Tr[:, kc, bi * 128:(bi + 1) * 128], rhs=wv[:, kc, :], start=(kc == 0), stop=(kc == 15)),
                            reads=[f"T{bi}"] + SLOT_N[s], writes=[f"ps{pb}"])
                    which = "lo" if c < 2 else "hi"
                    P.op("dve", lambda e, pb=pb, bi=bi, c=c: e.tensor_tensor(out=X1[:, bi, c * 512:(c + 1) * 512], in0=bank(pb),
                                                                           in1=X1[:, bi, c * 512:(c + 1) * 512], op=ALU.add),
                         reads=[f"ps{pb}", f"X1_{bi}_{which}"], writes=[f"X1_{bi}_{which}"])
            for bi in range(nb):
                x1n = [f"X1_{bi}_lo", f"X1_{bi}_hi"]
                P.op("act", lambda e, bi=bi: e.activation(out=mixb, in_=X1[:, bi, :], func=AF.Square, accum_out=stat[:, 0:1]),
                     reads=x1n, writes=mixb_n + ["st0"])
                rstd_of(stat[:, 0:1], stat[:, 1:2], 1.0 / D, ["st0"], "n1")
                P.op("dve", lambda e, bi=bi: e.scalar_tensor_tensor(out=mixb, in0=X1[:, bi, :], scalar=stat[:, 1:2], in1=g_ffn[:],
                                                                    op0=ALU.mult, op1=ALU.mult),
                     reads=x1n + ["n1_r", "g_ffn"], writes=mixb_n)
                transposes_to(mixb, 16, Tr[:, :, bi * 128:(bi + 1) * 128], mixb_n, [f"T{bi}"])

            ckpt(4)
            Tn = [f"T{bi}" for bi in range(nb)]
            ev = 0
            for fg in range(16):
                s = next_chunk("up")
                wv = SLOT_AP[s].rearrange("p (c n) -> p c n", c=16)
                for fc in range(4):
                    f = fg * 4 + fc
                    pb = f % 4
                    for dc in range(16):
                        P.op("pe", lambda e, pb=pb, dc=dc, fc=fc, wv=wv: e.matmul(
                            bank(pb, ntok), lhsT=wv[:, dc, fc * 128:(fc + 1) * 128], rhs=Tr[:, dc, 0:ntok], start=(dc == 0), stop=(dc == 15)),
                            reads=Tn + SLOT_N[s], writes=[f"ps{pb}"])
                    tr = ev % 2
                    ev += 1
                    P.op("act", lambda e, pb=pb, tr=tr: e.activation(out=tmpR[tr][:, 0:ntok], in_=bank(pb, ntok), func=AF.Relu),
                         reads=[f"ps{pb}"], writes=[f"tmpR{tr}"])
                    hv = hid(f, 0, ntok)
                    P.op("dve", lambda e, hv=hv, tr=tr: e.tensor_tensor(out=hv, in0=tmpR[tr][:, 0:ntok], in1=tmpR[tr][:, 0:ntok],
                                                                     op=ALU.mult),
                         reads=[f"tmpR{tr}"], writes=[hid_n(f)])

            ckpt(5)
            ydst = (lambda bi: ys.ap()) if sample else (lambda bi: yp.ap()[(4 * t + bi) * 128:(4 * t + bi + 1) * 128, :])
            yt = 0
            for r in range(2):
                for g8 in range(8):
                    s = next_chunk("down")
                    wv = SLOT_AP[s].rearrange("p (c n) -> p c n", c=8)
                    for fc in range(8):
                        f = g8 * 8 + fc
                        for bi in range(nb):
                            for nh in range(2):
                                pb = bi * 2 + nh
                                hv = hid(f, bi * 128, (bi + 1) * 128)
                                P.op("pe", lambda e, pb=pb, f=f, hv=hv, nh=nh, fc=fc, wv=wv: e.matmul(
                                    bank(pb), lhsT=hv, rhs=wv[:, fc, nh * 512:(nh + 1) * 512],
                                    start=(f == 0), stop=(f == 63)),
                                    reads=[hid_n(f)] + SLOT_N[s], writes=[f"ps{pb}"])
                which = "lo" if r == 0 else "hi"
                for bi in range(nb):
                    ysl = yt % 2
                    yt += 1
                    ytmp = Tr[:, 8 * ysl:8 * ysl + 4, :].rearrange("p c n -> p (c n)").bitcast(F32)
                    for nh in range(2):
                        pb = bi * 2 + nh
                        P.op("dve", lambda e, pb=pb, bi=bi, nh=nh, ytmp=ytmp, r=r: e.tensor_tensor(
                            out=ytmp[:, nh * 512:(nh + 1) * 512], in0=bank(pb), in1=X1[:, bi, r * 1024 + nh * 512:r * 1024 + (nh + 1) * 512],
                            op=ALU.add),
                            reads=[f"ps{pb}", f"X1_{bi}_{which}"], writes=[f"ytmp{ysl}"] + [f"T{i}" for i in range(4)])
                    P.dma("pool", f"yst{ysl}", lambda e, bi=bi, ytmp=ytmp, r=r: e.dma_start(out=ydst(bi)[:, r * 1024:(r + 1) * 1024], in_=ytmp),
                          reads=[f"ytmp{ysl}"])
            for ysl in range(2):
                P.op("dve", lambda e: e.memset(stat[:, 60:61], 0.0), writes=[f"ytmp{ysl}"] + [f"T{i}" for i in range(4)] + ["st60"])

        try:
            for kind, t in (TILES if tiles is None else tiles):
                run_tile(kind, t)
            if tiles is None:
                assert wstate["i"] == len(chunks), (wstate, len(chunks))
        except _Stop:
            pass
        P.finalize(block)
    return nc


def _t5_bucket_static(n):
    import math
    try:
        import jax
        import jax.numpy as jnp
        with jax.default_device(jax.devices("cpu")[0]):
            nn = jnp.asarray(n, dtype=jnp.int32)
            half, max_exact = 16, 8
            offset = jnp.where(nn < 0, half, 0)
            a = jnp.abs(nn)
            af = jnp.maximum(a, 1).astype(jnp.float32)
            large = max_exact + (jnp.log(af / max_exact) / math.log(128 / max_exact) * (half - max_exact)).astype(jnp.int32)
            large = jnp.minimum(large, half - 1)
            return np.asarray(offset + jnp.where(a < max_exact, a, large))
    except Exception:
        nn = np.asarray(n, dtype=np.int32)
        half, max_exact = 16, 8
        offset = np.where(nn < 0, half, 0)
        a = np.abs(nn)
        af = np.maximum(a, 1).astype(np.float32)
        large = max_exact + (np.log(af / np.float32(max_exact)) / np.float32(math.log(128 / max_exact))
                             * np.float32(half - max_exact)).astype(np.int32)
        large = np.minimum(large, half - 1)
        return offset + np.where(a < max_exact, a, large)


_NC_CACHE = {}


def prep_inputs(x_prompt, x_sample, cache_attn_k, cache_attn_v, rel_bias_table, ln_mix_g, w_in,
                q_norm_g, k_norm_g, attn_sinks, sgu_norm_g, sgu_w, sgu_b, out_norm_attn_g,
                out_norm_sgu_g, w_out, ln_ffn_g, w_ffn_up, w_ffn_down):
    f = lambda a: np.ascontiguousarray(np.asarray(a, dtype=np.float32))
    x_prompt, x_sample = f(x_prompt), f(x_sample)
    hk, g, d = np.meshgrid(np.arange(2), np.arange(8), np.arange(64), indexing="ij")
    qcols = ((hk * 8 + g) * 64 + d).transpose(1, 0, 2).reshape(-1)
    perm = np.concatenate([qcols, np.arange(1024, 1280), np.arange(2304, 3328), np.arange(1280, 2304)])
    w_in_p = np.asarray(w_in)[0][:, perm]

    def img_cols(w, col0, wdt):
        return w[:, col0:col0 + wdt].reshape(16, 128, wdt).transpose(1, 0, 2).reshape(128, 16 * wdt)

    IN_COL0 = [0, 512, 1024, 1280, 1792, 2304, 2816]
    IN_W = [512, 512, 256, 512, 512, 512, 512]
    w_in_img = f(np.concatenate([img_cols(w_in_p, c0, wd) for c0, wd in zip(IN_COL0, IN_W)], axis=1))
    wo = np.asarray(w_out)[0]
    w_out_img = f(np.concatenate([img_cols(wo, c * 512, 512) for c in range(4)], axis=1))
    wu = np.asarray(w_ffn_up)[0]
    w_up_img = f(np.concatenate([img_cols(wu, c * 512, 512) for c in range(16)], axis=1))
    wd_ = np.asarray(w_ffn_down)[0]
    w_down_img = f(np.concatenate(
        [wd_[g * 1024:(g + 1) * 1024, r * 1024:(r + 1) * 1024].reshape(8, 128, 1024).transpose(1, 0, 2).reshape(128, 8192)
         for r in range(2) for g in range(8)], axis=1))
    m = np.arange(384)
    bucket = _t5_bucket_static(255 - m)
    oh = np.zeros((32, 384), np.float32)
    oh[bucket, m] = 1.0
    common = {
        "table": f(rel_bias_table), "oh": oh, "ident": np.eye(128, dtype=np.float32),
        "w_in": w_in_img, "w_out": w_out_img, "w_up": w_up_img, "w_down": w_down_img,
        "g_mix": f(ln_mix_g), "g_ffn": f(ln_ffn_g), "g_sgu": f(sgu_norm_g), "g_oa": f(out_norm_attn_g), "g_os": f(out_norm_sgu_g),
        "gq": f(np.asarray(q_norm_g)[0][None, :]), "gk": f(np.asarray(k_norm_g)[0][None, :]), "sinks": f(np.asarray(attn_sinks)[0].reshape(1, 16)),
        "wsT": f(np.asarray(sgu_w)[0].transpose(2, 0, 1)), "bsT": f(np.asarray(sgu_b)[0].T),
    }
    ck = np.asarray(cache_attn_k)[0]
    cvv = np.asarray(cache_attn_v)[0]
    in_maps = []
    for c in range(NCORES):
        mm = dict(common)
        mm["xp"] = x_prompt[c]
        mm["xs"] = f(x_sample[2 * c:2 * c + 2].reshape(128, D))
        mm["ckT"] = f(ck[2 * c:2 * c + 2].reshape(2, 128, 128).transpose(0, 2, 1))
        mm["cv"] = f(cvv[2 * c:2 * c + 2].reshape(2, 128, 128))
        in_maps.append(mm)
    return in_maps


def kernel(**inputs):
    in_maps = prep_inputs(**inputs)
    if "nc" not in _NC_CACHE:
        _NC_CACHE["nc"] = build_program()
    nc = _NC_CACHE["nc"]
    res = run_bass_kernel_spmd(nc, in_maps, core_ids=list(range(NCORES)))
    R = res.results
    y_prompt = np.stack([R[c]["yp"] for c in range(NCORES)]).astype(np.float32)
    y_sample = np.concatenate([R[c]["ys"].reshape(2, 64, D) for c in range(NCORES)]).astype(np.float32)
    kpo = np.stack([R[c]["kp"].reshape(128, 2, 64) for c in range(NCORES)])[None].astype(np.float32)
    vpo = np.stack([R[c]["vp"].reshape(128, 2, 64) for c in range(NCORES)])[None].astype(np.float32)
    kso = np.concatenate([R[c]["ks"].reshape(2, 64, 2, 64) for c in range(NCORES)])[None].astype(np.float32)
    vso = np.concatenate([R[c]["vs"].reshape(2, 64, 2, 64) for c in range(NCORES)])[None].astype(np.float32)
    sgo = np.concatenate([R[c]["sgv"].reshape(2, 64, 1024) for c in range(NCORES)])[None].astype(np.float32)
    return (y_prompt, y_sample, kpo, vpo, kso, vso, sgo)
```

```python
import numpy as np
import concourse.bass as bass
import concourse.mybir as mybir
from concourse.bass_utils import run_bass_kernel_spmd

F32 = mybir.dt.float32
BF16 = mybir.dt.bfloat16
AF = mybir.ActivationFunctionType
ALU = mybir.AluOpType
AX = mybir.AxisListType

D = 2048
DFF = 8192
NCORES = 8
EPS = 1e-6
NEGB = -30000.0
import os
SAMPLE_RING = int(os.environ.get("SAMPLE_RING", "6"))
ENGS = ("pe", "act", "dve", "pool", "sp")


class _Op:
    __slots__ = ("eng", "fn", "reads", "writes", "dma", "deps", "signal", "ev", "idx")

    def __init__(self, eng, fn, reads, writes, dma):
        self.eng, self.fn, self.reads, self.writes, self.dma = eng, fn, reads, writes, dma
        self.deps = set()
        self.signal = False
        self.ev = None


class Prog:
    def __init__(self, nc, same_engine_sync=True):
        self.nc = nc
        self.ops = []
        self.last_w = {}
        self.readers = {}
        self.same_engine_sync = same_engine_sync

    def _add(self, eng, fn, reads, writes, dma=None):
        op = _Op(eng, fn, tuple(reads), tuple(writes), dma)
        op.idx = len(self.ops)
        for r in op.reads:
            w = self.last_w.get(r)
            if w is not None:
                op.deps.add(w)
        for w_ in op.writes:
            w = self.last_w.get(w_)
            if w is not None:
                op.deps.add(w)
            latest = {}
            for rd in self.readers.get(w_, ()):
                ro = self.ops[rd]
                if ro.dma is not None:
                    op.deps.add(rd)
                else:
                    latest[ro.eng] = max(latest.get(ro.eng, -1), rd)
            op.deps.update(latest.values())
        for r in op.reads:
            self.readers.setdefault(r, []).append(op.idx)
        for w_ in op.writes:
            self.last_w[w_] = op.idx
            self.readers[w_] = []
        op.deps.discard(op.idx)
        self.ops.append(op)
        return op

    def op(self, eng, fn, reads=(), writes=()):
        return self._add(eng, fn, reads, writes)

    def dma(self, eng, sem_name, fn, reads=(), writes=()):
        return self._add(eng, fn, reads, writes, dma=sem_name)

    def finalize(self, block):
        import os
        nc = self.nc
        ops = self.ops
        nmax = int(os.environ.get("BISECT_N", "0"))
        if nmax:
            ops = ops[:nmax]
            for i, o in enumerate(ops[-3:]):
                print("last ops:", o.idx, o.eng, o.reads, o.writes, o.dma)
        print("n_ops", len(ops))
        for op in ops:
            for d in op.deps:
                p = ops[d]
                if p.dma is not None:
                    continue
                if p.eng == op.eng and (p.eng == "pe" or not self.same_engine_sync):
                    continue
                p.signal = True
        eng_sem = {e: nc.alloc_semaphore("sem_" + e) for e in ENGS}
        self.all_sems = list(eng_sem.values())
        cnt = {e: 0 for e in ENGS}
        dsem, dcnt = {}, {}
        for op in ops:
            if op.dma is not None:
                if op.dma not in dsem:
                    dsem[op.dma] = nc.alloc_semaphore("dsem_" + op.dma)
                    self.all_sems.append(dsem[op.dma])
                    dcnt[op.dma] = 0
                dcnt[op.dma] += 16
                op.ev = (op.dma, dcnt[op.dma])
            elif op.signal:
                cnt[op.eng] += 1
                op.ev = (op.eng, cnt[op.eng])
        issued = {k: 0 for k in dsem}
        waits_for = []
        for op in ops:
            w = {}
            for d in op.deps:
                p = ops[d]
                if p.dma is not None:
                    w[("d", p.dma)] = max(w.get(("d", p.dma), 0), issued[p.dma])
                elif p.ev is not None:
                    w[("e", p.eng)] = max(w.get(("e", p.eng), 0), p.ev[1])
            waits_for.append(w)
            if op.dma is not None:
                issued[op.dma] += 16
        final = dict(dcnt)

        def semof(k):
            return dsem[k[1]] if k[0] == "d" else eng_sem[k[1]]

        def emit(engname, engobj):
            waited = {}
            for op, w in zip(ops, waits_for):
                if op.eng != engname:
                    continue
                for k, v in w.items():
                    if waited.get(k, 0) >= v:
                        continue
                    engobj.wait_ge(semof(k), v)
                    waited[k] = v
                inst = op.fn(engobj)
                if op.dma is not None:
                    inst.then_inc(dsem[op.dma], 16)
                elif op.signal:
                    inst.then_inc(eng_sem[op.eng], 1)
            if engname == "sp":
                for k, v in final.items():
                    engobj.wait_ge(dsem[k], v)

        for sm in self.all_sems:
            nc.sync.sem_clear(sm)
        nc.all_engine_barrier()
        block = nc.Block().__enter__()
        self._block = block

        @block.tensor
        def _(e):
            emit("pe", e)

        @block.scalar
        def _(e):
            emit("act", e)

        @block.vector
        def _(e):
            emit("dve", e)

        @block.gpsimd
        def _(e):
            emit("pool", e)

        @block.sync
        def _(e):
            emit("sp", e)

        block.__exit__(None, None, None)


class _Stop(Exception):
    pass


def build_program(stage=99, tiles=None):
    nc = bass.Bass("TRN2", target_bir_lowering=False)

    def din(name, shape):
        return nc.dram_tensor(name, list(shape), F32, kind="ExternalInput")

    def dout(name, shape):
        return nc.dram_tensor(name, list(shape), F32, kind="ExternalOutput")

    xp = din("xp", (2048, D))
    xs = din("xs", (128, D))
    ckT = din("ckT", (2, 128, 128))
    cv = din("cv", (2, 128, 128))
    table = din("table", (32, 16))
    oh = din("oh", (32, 384))
    ident = din("ident", (128, 128))
    w_in = din("w_in", (128, 16 * 3328))
    w_out = din("w_out", (128, 4 * 8192))
    w_up = din("w_up", (128, 16 * 8192))
    w_down = din("w_down", (128, 16 * 8192))
    wimg = {"in": w_in, "out": w_out, "up": w_up, "down": w_down}
    wscr = {k: nc.dram_tensor("scr_" + k, list(v.shape), BF16, kind="ExternalOutput") for k, v in wimg.items()}
    g_mix_d = din("g_mix", (1, D))
    g_ffn_d = din("g_ffn", (1, D))
    g_sgu_d = din("g_sgu", (1, 1024))
    g_oa_d = din("g_oa", (1, 1024))
    g_os_d = din("g_os", (1, 1024))
    gq_d = din("gq", (1, 64))
    gk_d = din("gk", (1, 64))
    sinks_d = din("sinks", (1, 16))
    wsT_d = din("wsT", (128, 8, 128))
    bsT_d = din("bsT", (128, 8))
    scr = nc.dram_tensor("scr", [16, 384], F32, kind="Internal")

    yp = dout("yp", (2048, D))
    ys = dout("ys", (128, D))
    kp_o = dout("kp", (128, 128))
    vp_o = dout("vp", (128, 128))
    ks_o = dout("ks", (128, 128))
    vs_o = dout("vs", (128, 128))
    sgv_o = dout("sgv", (128, 1024))

    def sb(name, shape, dt):
        return nc.alloc_sbuf_tensor(name, list(shape), dt)

    identb = sb("identb", (128, 128), BF16)
    g_mix = sb("g_mix_s", (128, D), F32)
    g_ffn = sb("g_ffn_s", (128, D), F32)
    g_sgu = sb("g_sgu_s", (128, 1024), F32)
    g_oa = sb("g_oa_s", (128, 1024), F32)
    g_os = sb("g_os_s", (128, 1024), F32)
    gq_t = sb("gq_s", (128, 64), F32)
    gk_t = sb("gk_s", (128, 64), F32)
    esink = sb("esink", (128, 16), F32)
    epsb = sb("epsb", (128, 1), F32)
    onesb = sb("onesb", (128, 2), BF16)
    WsT = sb("WsT", (128, 8, 128), BF16)
    bsT = sb("bsT_s", (128, 8), F32)
    bsTs = sb("bsTs", (128, 8), F32)
    biasP = sb("biasP", (128, 16, 128), F32)
    biasO = sb("biasO", (128, 16, 128), F32)
    NSLOT = 5
    kTz = [[sb(f"kTz{hk}_{s}", (128, 128), BF16) for s in range(NSLOT)] for hk in range(2)]
    Vb = [sb(f"Vb{s}", (128, 128), BF16) for s in range(NSLOT)]
    qT = sb("qT", (128, 8, 128), BF16)
    stat = sb("stat", (128, 64), F32)
    tmpR = [sb(f"tmpR{i}", (128, 512), F32) for i in range(2)]
    X1 = sb("X1", (128, 4, D), F32)
    Hr = sb("Hr", (128, 16384), F32)
    Hrb = Hr[:, :].bitcast(BF16)
    Tr = sb("Tr", (128, 16, 512), BF16)
    Wr = [sb(f"Wr{i}", (128, 8192), BF16) for i in range(2)]
    ps = nc.alloc_psum_tensor("ps", [128, 4096], F32)

    def bank(i, n=512):
        return ps[:, i * 512:i * 512 + n]

    ptr = ps[:, 0:1024].bitcast(BF16)
    ptr2 = ps[:, 3072:4096].bitcast(BF16)
    tstate = {"n": 0}

    class HAlloc:
        def __init__(self):
            self.off = 0

        def take(self, nbytes, dt, shape=None):
            start = (self.off + 1023) // 1024 * 1024
            self.off = start + nbytes
            assert self.off <= 65536, self.off
            ap = Hr[:, start // 4:(start + nbytes) // 4]
            if dt == BF16:
                ap = Hrb[:, start // 2:(start + nbytes) // 2]
            names = [f"H{i}" for i in range(start // 1024, (start + nbytes + 1023) // 1024)]
            return ap, names

    ha = HAlloc()
    zqk, zqk_n = [], []
    for b in range(4):
        a, n = ha.take(1152 * 4, F32)
        zqk.append(a)
        zqk_n.append(n)
    tmpq, tmpq_n = ha.take(1152 * 4, F32)
    qknb, qknb_n = ha.take(1152 * 2, BF16)
    tmpS, tmpS_n = [], []
    for i in range(2):
        a, n = ha.take(512 * 4, F32)
        tmpS.append(a)
        tmpS_n.append(n)
    PT, PT_n = {}, {}
    for hk in range(2):
        for kt in range(3):
            a, n = ha.take(1024 * 2, BF16)
            PT[(hk, kt)] = a
            PT_n[(hk, kt)] = n
    o32, o32_n = ha.take(1024 * 4, F32)
    vnb, vnb_n = ha.take(1024 * 2, BF16)
    xtmp, xtmp_n = ha.take(D * 4, F32)
    hank, hank_n = xtmp, xtmp_n
    mixb, mixb_n = ha.take(D * 2, BF16)
    mixb_lo_n, mixb_hi_n = mixb_n[:2], mixb_n[2:]
    assert len(mixb_n) == 4
    h_used = ha.off
    tbl = tmpR[1][0:32, 384:400]
    ohs = tmpR[0][0:32, 0:384]
    srow = tmpR[1][0:16, 0:384]
    kn32 = o32[:, 0:128]
    v32 = o32[:, 128:256]
    zq1_start = 5 * 1024
    biasPB = Hr[:, zq1_start // 4:(zq1_start + 8192) // 4]
    wsts_start = 15 * 1024
    WsTs = Hrb[:, wsts_start // 2:(wsts_start + 2048) // 2].rearrange("p (g i) -> p g i", g=8)
    WsTs_n = ["H15", "H16"]
    biasPB_n = [f"H{i}" for i in range(zq1_start // 1024, zq1_start // 1024 + 8)]
    ALLH = [f"H{i}" for i in range(64)]

    hstate = {"compact": False}

    def hid(f, t0=0, t1=512):
        if hstate["compact"]:
            return Hrb[:, f * 128 + t0:f * 128 + t1]
        return Hrb[:, f * 512 + t0:f * 512 + t1]

    def hid_n(f):
        return f"H{f // 4}" if hstate["compact"] else f"H{f}"

    P = Prog(nc)
    _cst = {"n": 0}
    _orig_dma = P.dma

    def _dma(eng, sem_name, fn, reads=(), writes=()):
        if sem_name == "cst":
            _cst["n"] += 1
            sem_name = f"cst{_cst['n']}"
        return _orig_dma(eng, sem_name, fn, reads=reads, writes=writes)

    P.dma = _dma

    TILES = [("p", t) for t in range(4)] + [("s", 0)]
    chunks = []
    for _ in (TILES if tiles is None else tiles):
        chunks += [("in", c) for c in range(7)]
        chunks += [("out", c) for c in range(4)]
        chunks += [("up", c) for c in range(16)]
        chunks += [("down", r, g) for r in range(2) for g in range(8)]
    IN_COL0 = [0, 512, 1024, 1280, 1792, 2304, 2816]
    IN_W = [512, 512, 256, 512, 512, 512, 512]
    wstate = {"i": 0, "issued": 0}

    NCH = 43
    SLOT_AP = [Wr[0][:, :], Wr[1][:, :], X1[:, 1:3, :].rearrange("p b n -> p (b n)").bitcast(BF16),
               Hrb[:, 8192:16384], Hrb[:, 16384:24576], Hrb[:, 24576:32768]]
    SLOT_N = [["w0"], ["w1"], ["X1_1_lo", "X1_1_hi", "X1_2_lo", "X1_2_hi"],
              [f"H{i}" for i in range(16, 32)], [f"H{i}" for i in range(32, 48)], [f"H{i}" for i in range(48, 64)]]
    n_tiles_run = len(TILES if tiles is None else tiles)
    tile_kinds = [k for k, _ in (TILES if tiles is None else tiles)]

    def slot_of(i):
        j = i % NCH
        if tile_kinds[i // NCH] == "s":
            if SAMPLE_RING <= 2:
                return i % 2
            return [0, 1, 2][j % 3] if j < 11 else [2, 0, 1, 3, 4, 5][(j - 11) % 6]
        return i % 2

    def ahead_of(i):
        j = i % NCH
        if tile_kinds[i // NCH] == "s":
            if SAMPLE_RING <= 2:
                return 1
            return 2 if j < 11 else 5
        return 1

    slot_last = {}

    def issue_chunk(i):
        c = chunks[i]
        s = slot_of(i)
        assert slot_last.get(s, -1) < wstate["i"], (i, s, slot_last.get(s), wstate["i"])
        slot_last[s] = i
        tile_i, j = i // NCH, i % NCH
        multi = n_tiles_run >= 3
        if not multi:
            cast, wback = tile_i == 0, tile_i == 0
        elif tile_i == 0:
            cast, wback = True, (j % 2 == 0)
        elif tile_i == 1:
            cast, wback = (j % 2 == 1), (j % 2 == 1)
        else:
            cast, wback = False, False
        kind = c[0]
        if kind == "in":
            off, ln = 16 * IN_COL0[c[1]], 16 * IN_W[c[1]]
        elif kind == "down":
            off, ln = (c[1] * 8 + c[2]) * 8192, 8192
        else:
            off, ln = c[1] * 8192, 8192
        d = SLOT_AP[s][:, 0:ln]
        rname = f"scr_{kind}_{off}"
        if cast:
            src = wimg[kind].ap()[:, off:off + ln]
            P.dma("pool", f"w{s}q", lambda e, d=d, src=src: e.dma_start(out=d, in_=src), writes=SLOT_N[s])
            if wback:
                dsts = wscr[kind].ap()[:, off:off + ln]
                P.dma("sp", "wst", lambda e, d=d, dsts=dsts: e.dma_start(out=dsts, in_=d), reads=SLOT_N[s], writes=[rname])
        else:
            src = wscr[kind].ap()[:, off:off + ln]
            P.dma("sp", f"w{s}", lambda e, d=d, src=src: e.dma_start(out=d, in_=src), reads=[rname], writes=SLOT_N[s])

    def next_chunk(kind):
        i = wstate["i"]
        assert chunks[i][0] == kind, (chunks[i], kind)
        while wstate["issued"] <= min(i + ahead_of(i), len(chunks) - 1):
            nxt = wstate["issued"]
            if nxt // NCH != i // NCH and nxt > i + 1:
                break
            issue_chunk(nxt)
            wstate["issued"] += 1
        wstate["i"] += 1
        return slot_of(i)

    def rstd_of(ss_col, r_col, inv_n, reads, tag):
        P.op("act", lambda e: e.activation(out=r_col, in_=ss_col, func=AF.Sqrt, scale=inv_n, bias=epsb[:]),
             reads=reads + ["epsb"], writes=[tag + "_r"])
        P.op("dve", lambda e: e.reciprocal(out=r_col, in_=r_col), reads=[tag + "_r"], writes=[tag + "_r"])

    def transposes_to(src_tile, nch, dst_ap, src_names, dst_names, evac_eng="act"):
        k = tstate["n"] % 2
        tstate["n"] += 1
        pt = [ptr, ptr2][k]
        pn = [["ps0", "ps1"], ["ps6", "ps7"]][k]
        for c in range(nch):
            P.op("pe", lambda e, c=c, pt=pt: e.matmul(pt[:, c * 128:(c + 1) * 128], lhsT=src_tile[:, c * 128:(c + 1) * 128],
                                                    rhs=identb[:], start=True, stop=True, is_transpose=True),
                 reads=src_names + ["identb"], writes=pn)
        src = pt[:, 0:nch * 128].rearrange("p (c t) -> p c t", c=nch)
        if evac_eng == "act":
            P.op("act", lambda e: e.copy(out=dst_ap, in_=src), reads=pn, writes=dst_names)
        else:
            P.op("dve", lambda e: e.tensor_copy(out=dst_ap, in_=src), reads=pn, writes=dst_names)

    def toeplitz(dst, dst_names, c0):
        hk_ = hank.rearrange("p (h q) -> p h q", h=16)
        P.dma("sp", "cst", lambda e: e.dma_start(out=hk_, in_=bass.AP(tensor=scr, offset=c0, ap=[[1, 128], [384, 16], [1, 128]])),
              reads=["scr"], writes=hank_n)
        t = hank
        rev = bass.AP(tensor=t.tensor, offset=t.offset + 127, ap=[list(t.ap[0]), [128, 16], [-1, 128]])
        P.op("dve", lambda e: e.tensor_copy(out=dst, in_=rev), reads=hank_n, writes=dst_names)

    block = None
    if True:
        P.op("dve", lambda e: e.memset(epsb[:], EPS), writes=["epsb"])
        P.op("dve", lambda e: e.memset(onesb[:], 1.0), writes=["onesb"])
        for hk in range(2):
            for s in range(NSLOT):
                P.op("dve", lambda e, hk=hk, s=s: e.memset(kTz[hk][s][:], 0.0), writes=[f"kTz{hk}_{s}"])

        def bc_load(dst, src, n, name):
            P.dma("sp", "cst", lambda e: e.dma_start(out=dst[:], in_=src.ap().partition_broadcast(128)[:, 0, :]), writes=[name])

        idf = xtmp[:, 0:128]
        P.dma("sp", "cst", lambda e: e.dma_start(out=idf, in_=ident.ap()), writes=xtmp_n)
        P.op("dve", lambda e: e.tensor_copy(out=identb[:], in_=idf), reads=xtmp_n, writes=["identb"])
        bc_load(g_mix, g_mix_d, D, "g_mix")
        P.dma("sp", "cst", lambda e: e.dma_start(out=tbl, in_=table.ap()), writes=["tmpR1"])
        P.dma("sp", "cst", lambda e: e.dma_start(out=ohs, in_=oh.ap()), writes=["tmpR0"])

    def late_setup():
        bc_load(g_ffn, g_ffn_d, D, "g_ffn")
        bc_load(g_sgu, g_sgu_d, 1024, "g_sgu")
        bc_load(g_oa, g_oa_d, 1024, "g_oa")
        bc_load(g_os, g_os_d, 1024, "g_os")
        bc_load(gq_t, gq_d, 64, "gq")
        bc_load(gk_t, gk_d, 64, "gk")
        bc_load(esink, sinks_d, 16, "esink")
        P.op("act", lambda e: e.activation(out=esink[:], in_=esink[:], func=AF.Exp), reads=["esink"], writes=["esink"])
        wst = hank.rearrange("p (g i) -> p g i", g=16)[:, 0:8, :]
        P.dma("sp", "cst", lambda e: e.dma_start(out=wst, in_=wsT_d.ap()), writes=hank_n)
        P.op("dve", lambda e: e.tensor_copy(out=WsT[:], in_=wst), reads=hank_n, writes=["WsT"])
        P.op("dve", lambda e: e.memset(WsT[64:128, :, 0:64], 0.0), reads=[], writes=["WsT"])
        P.dma("sp", "cst", lambda e: e.dma_start(out=bsT[:], in_=bsT_d.ap()), writes=["bsT"])
        P.dma("sp", "cst", lambda e: e.dma_start(out=bsTs[0:64, :], in_=bsT_d.ap()[0:64, :]), writes=["bsTs"])
        P.dma("sp", "cst", lambda e: e.dma_start(out=bsTs[64:128, :], in_=bsT_d.ap()[0:64, :]), writes=["bsTs"])
        P.op("pe", lambda e: e.matmul(bank(2)[0:16, 0:384], lhsT=tbl, rhs=ohs, start=True, stop=True),
             reads=["tmpR0", "tmpR1"], writes=["ps2"])
        P.op("dve", lambda e: e.tensor_copy(out=srow, in_=bank(2)[0:16, 0:384]), reads=["ps2"], writes=["tmpR1"])
        P.dma("sp", "cst", lambda e: e.dma_start(out=scr.ap(), in_=srow), reads=["tmpR1"], writes=["scr"])
        toeplitz(biasP[:], ["biasP"], 0)
        P.op("dve", lambda e: e.memset(biasP[0:64, :, 64:128], NEGB), writes=["biasP"])
        toeplitz(biasO[:], ["biasO"], 128)
        P.op("dve", lambda e: e.memset(biasO[64:128, :, 0:64], NEGB), writes=["biasO"])

    late_done = {"v": False}
    if True:
        def ckpt(n):
            if stage <= n:
                raise _Stop()

        def run_tile(kind, t):
            sample = kind == "s"
            hstate["compact"] = sample and os.environ.get("NO_COMPACT", "0") != "1"
            nb = 1 if sample else 4
            ntok = nb * 128
            gB = [16] if sample else [4 * t + i for i in range(4)]
            xsrc = (lambda bi: xs.ap()) if sample else (lambda bi: xp.ap()[(4 * t + bi) * 128:(4 * t + bi + 1) * 128, :])

            if sample and not late_done["v"]:
                late_done["v"] = True
                late_setup()
            if sample:
                P.op("dve", lambda e: e.memset(biasP[:, :, 64:128], NEGB), writes=["biasP"])
                P.op("dve", lambda e: e.memset(biasO[0:64, :, 64:128], NEGB), writes=["biasO"])
                toeplitz(biasPB.rearrange("p (h q) -> p h q", h=16), biasPB_n, 64)
                P.op("dve", lambda e: e.memset(biasPB.rearrange("p (h q) -> p h q", h=16)[:, :, 0:64], NEGB), writes=biasPB_n)
                wstg = xtmp.rearrange("p (g i) -> p g i", g=16)[:, 0:8, :]
                P.dma("pool", "xld", lambda e: e.dma_start(out=wstg[0:64, :, 0:64], in_=wsT_d.ap()[0:64, :, 0:64]), writes=xtmp_n)
                P.dma("pool", "xld", lambda e: e.dma_start(out=wstg[64:128, :, 64:128], in_=wsT_d.ap()[0:64, :, 0:64]), writes=xtmp_n)
                P.op("dve", lambda e: e.memset(WsTs, 0.0), writes=WsTs_n)
                P.op("dve", lambda e: e.tensor_copy(out=WsTs[0:64, :, 0:64], in_=wstg[0:64, :, 0:64]), reads=xtmp_n, writes=WsTs_n)
                P.op("dve", lambda e: e.tensor_copy(out=WsTs[64:128, :, 64:128], in_=wstg[64:128, :, 64:128]), reads=xtmp_n, writes=WsTs_n)
                for sq in range(2):
                    slot = 2 + sq
                    st_ = xtmp[:, sq * 256:sq * 256 + 128]
                    sv_ = xtmp[:, sq * 256 + 128:sq * 256 + 256]
                    P.dma("pool", "xld", lambda e, st_=st_, sq=sq: e.dma_start(out=st_, in_=ckT.ap()[sq]), writes=xtmp_n)
                    P.dma("pool", "xld", lambda e, sv_=sv_, sq=sq: e.dma_start(out=sv_, in_=cv.ap()[sq]), writes=xtmp_n)
                    P.op("dve", lambda e, st_=st_, slot=slot: e.tensor_copy(out=kTz[0][slot][0:64, :], in_=st_[0:64, :]),
                         reads=xtmp_n, writes=[f"kTz0_{slot}"])
                    P.op("dve", lambda e, st_=st_, slot=slot: e.tensor_copy(out=kTz[1][slot][64:128, :], in_=st_[64:128, :]),
                         reads=xtmp_n, writes=[f"kTz1_{slot}"])
                    P.op("dve", lambda e, sv_=sv_, slot=slot: e.tensor_copy(out=Vb[slot][:], in_=sv_), reads=xtmp_n, writes=[f"Vb{slot}"])

            ckpt(0)
            for bi in range(nb):
                P.dma("pool", f"xld{bi}", lambda e, bi=bi: e.dma_start(out=X1[:, bi, :], in_=xsrc(bi)),
                      writes=[f"X1_{bi}_lo", f"X1_{bi}_hi"])
            def z1_unit(c, s, bi, pb):
                wdt = IN_W[c]
                wv = SLOT_AP[s][:, 0:16 * wdt].rearrange("p (c n) -> p c n", c=16)
                for dc in range(16):
                    P.op("pe", lambda e, pb=pb, dc=dc, bi=bi, wv=wv, wdt=wdt: e.matmul(
                        bank(pb, wdt), lhsT=Tr[:, dc, bi * 128:(bi + 1) * 128], rhs=wv[:, dc, :], start=(dc == 0), stop=(dc == 15)),
                        reads=[f"T{bi}"] + SLOT_N[s], writes=[f"ps{pb}"])
                slot = gB[bi] % NSLOT
                if c < 2:
                    P.op("dve", lambda e, pb=pb, bi=bi, c=c: e.tensor_copy(out=zqk[bi][:, c * 512:(c + 1) * 512], in_=bank(pb)),
                         reads=[f"ps{pb}"], writes=zqk_n[bi])
                elif c == 2:
                    P.op("dve", lambda e, pb=pb, bi=bi: e.tensor_copy(out=zqk[bi][:, 1024:1152], in_=bank(pb, 128)),
                         reads=[f"ps{pb}"], writes=zqk_n[bi])
                    P.op("dve", lambda e, pb=pb, slot=slot: e.tensor_copy(out=Vb[slot][:], in_=ps[:, pb * 512 + 128:pb * 512 + 256]),
                         reads=[f"ps{pb}"], writes=[f"Vb{slot}"])
                    if sample or (t == 3 and bi == 3):
                        P.op("dve", lambda e, pb=pb: e.tensor_copy(out=v32, in_=ps[:, pb * 512 + 128:pb * 512 + 256]),
                             reads=[f"ps{pb}"], writes=o32_n)
                        vo = vs_o if sample else vp_o
                        P.dma("pool", "ost", lambda e, vo=vo: e.dma_start(out=vo.ap(), in_=v32), reads=o32_n)
                else:
                    half = (c - 3) % 2
                    which = "lo" if c < 5 else "hi"
                    col0 = (0 if c < 5 else 1024) + half * 512
                    P.op("act", lambda e, pb=pb, bi=bi, col0=col0: e.activation(out=X1[:, bi, col0:col0 + 512], in_=bank(pb),
                                                                               func=AF.Gelu_apprx_tanh),
                         reads=[f"ps{pb}"], writes=[f"X1_{bi}_{which}"])


            s0 = next_chunk("in")
            for bi in range(nb):
                x1n = [f"X1_{bi}_lo", f"X1_{bi}_hi"]
                P.op("act", lambda e, bi=bi: e.activation(out=mixb, in_=X1[:, bi, :], func=AF.Square, accum_out=stat[:, 0:1]),
                     reads=x1n, writes=mixb_n + ["st0"])
                rstd_of(stat[:, 0:1], stat[:, 1:2], 1.0 / D, ["st0"], "n1")
                P.op("dve", lambda e, bi=bi: e.scalar_tensor_tensor(out=mixb, in0=X1[:, bi, :], scalar=stat[:, 1:2], in1=g_mix[:],
                                                                    op0=ALU.mult, op1=ALU.mult),
                     reads=x1n + ["n1_r", "g_mix"], writes=mixb_n)
                transposes_to(mixb, 16, Tr[:, :, bi * 128:(bi + 1) * 128], mixb_n, [f"T{bi}"])
                if bi >= 1:
                    z1_unit(0, s0, bi - 1, 2 + (bi - 1) % 4)
            z1_unit(0, s0, nb - 1, 2 + (nb - 1) % 4)
            if not late_done["v"]:
                late_done["v"] = True
                late_setup()
            ckpt(1)
            for c in range(1, 7):
                s = next_chunk("in")
                for bi in range(nb):
                    z1_unit(c, s, bi, 2 + (c * nb + bi) % 4)

            ckpt(2)
            for bi in range(nb):
                B = gB[bi]
                slot = B % NSLOT
                zq3 = zqk[bi].rearrange("p (h d) -> p h d", d=64)
                tq3 = tmpq.rearrange("p (h d) -> p h d", d=64)
                P.op("dve", lambda e, bi=bi: e.tensor_tensor(out=tmpq, in0=zqk[bi], in1=zqk[bi], op=ALU.mult),
                     reads=zqk_n[bi], writes=tmpq_n)
                P.op("dve", lambda e: e.tensor_reduce(out=stat[:, 8:26], in_=tq3, op=ALU.add, axis=AX.X),
                     reads=tmpq_n, writes=["qk_r"])
                rstd_of(stat[:, 8:26], stat[:, 8:26], 1.0 / 64, ["qk_r"], "qk")
                P.op("dve", lambda e, zq3=zq3: e.tensor_tensor(out=tq3, in0=zq3, in1=stat[:, 8:26].unsqueeze(2).to_broadcast([128, 18, 64]),
                                                             op=ALU.mult),
                     reads=zqk_n[bi] + ["qk_r"], writes=tmpq_n)
                P.op("dve", lambda e: e.tensor_tensor(out=qknb[:, 0:1024].rearrange("p (h d) -> p h d", d=64), in0=tq3[:, 0:16, :],
                                                      in1=gq_t[:].unsqueeze(1).to_broadcast([128, 16, 64]), op=ALU.mult),
                     reads=tmpq_n + ["gq"], writes=qknb_n)
                P.op("dve", lambda e: e.tensor_tensor(out=qknb[:, 1024:1152].rearrange("p (h d) -> p h d", d=64), in0=tq3[:, 16:18, :],
                                                      in1=gk_t[:].unsqueeze(1).to_broadcast([128, 2, 64]), op=ALU.mult),
                     reads=tmpq_n + ["gk"], writes=qknb_n)
                if sample or (t == 3 and bi == 3):
                    P.op("dve", lambda e: e.tensor_tensor(out=kn32.rearrange("p (h d) -> p h d", d=64), in0=tq3[:, 16:18, :],
                                                          in1=gk_t[:].unsqueeze(1).to_broadcast([128, 2, 64]), op=ALU.mult),
                         reads=tmpq_n + ["gk"], writes=o32_n)
                    ko = ks_o if sample else kp_o
                    P.dma("pool", "ost", lambda e, ko=ko: e.dma_start(out=ko.ap(), in_=kn32), reads=o32_n)
                for c in range(9):
                    P.op("pe", lambda e, c=c: e.matmul(ptr[:, c * 128:(c + 1) * 128], lhsT=qknb[:, c * 128:(c + 1) * 128],
                                                     rhs=identb[:], start=True, stop=True, is_transpose=True),
                         reads=qknb_n + ["identb"], writes=["ps0", "ps1"])
                P.op("act", lambda e: e.copy(out=qT[:], in_=ptr[:, 0:1024].rearrange("p (c t) -> p c t", c=8)),
                     reads=["ps0", "ps1"], writes=["qT"])
                P.op("dve", lambda e, slot=slot: e.tensor_copy(out=kTz[0][slot][0:64, :], in_=ptr[0:64, 1024:1152]),
                     reads=["ps0", "ps1"], writes=[f"kTz0_{slot}"])
                P.op("dve", lambda e, slot=slot: e.tensor_copy(out=kTz[1][slot][64:128, :], in_=ptr[64:128, 1024:1152]),
                     reads=["ps0", "ps1"], writes=[f"kTz1_{slot}"])
                if sample:
                    kts = [(2, biasP[:], ["biasP"]), (3, biasPB.rearrange("p (h q) -> p h q", h=16), biasPB_n), (slot, biasO[:], ["biasO"])]
                elif B == 0:
                    kts = [(slot, biasO[:], ["biasO"])]
                else:
                    kts = [((B - 1) % NSLOT, biasP[:], ["biasP"]), (slot, biasO[:], ["biasO"])]
                zv = X1[:, bi, 0:1024]
                uu = X1[:, bi, 1024:2048]
                P.op("act", lambda e, zv=zv: e.activation(out=vnb, in_=zv, func=AF.Square, accum_out=stat[:, 4:5]),
                     reads=[f"X1_{bi}_lo"], writes=vnb_n + ["st4"])
                rstd_of(stat[:, 4:5], stat[:, 5:6], 1.0 / 1024, ["st4"], "sv")
                if sample:
                    P.op("dve", lambda e, zv=zv: e.scalar_tensor_tensor(out=o32, in0=zv, scalar=stat[:, 5:6], in1=g_sgu[:],
                                                                        op0=ALU.mult, op1=ALU.mult),
                         reads=[f"X1_{bi}_lo", "sv_r", "g_sgu"], writes=o32_n)
                    P.dma("pool", "ost", lambda e: e.dma_start(out=sgv_o.ap(), in_=o32), reads=o32_n)
                P.op("dve", lambda e, zv=zv: e.scalar_tensor_tensor(out=vnb, in0=zv, scalar=stat[:, 5:6], in1=g_sgu[:],
                                                                    op0=ALU.mult, op1=ALU.mult),
                     reads=[f"X1_{bi}_lo", "sv_r", "g_sgu"], writes=vnb_n)
                Wg = WsTs if sample else WsT
                Wgn = WsTs_n if sample else ["WsT"]
                bg = bsTs if sample else bsT
                bgn = "bsTs" if sample else "bsT"
                cnt = 0
                for hk in range(2):
                    for ki, (ks_, bt, btn) in enumerate(kts):
                        for half in range(2):
                            pb = 4 + cnt % 2
                            ts_ = cnt % 2
                            cnt += 1
                            P.op("pe", lambda e, pb=pb, hk=hk, ks_=ks_, half=half: e.matmul(
                                bank(pb), lhsT=kTz[hk][ks_][:], rhs=qT[:, 4 * half:4 * half + 4, :], start=True, stop=True),
                                reads=[f"kTz{hk}_{ks_}", "qT"], writes=[f"ps{pb}"])
                            h0 = hk * 8 + 4 * half
                            P.op("dve", lambda e, pb=pb, ts_=ts_, bt=bt, h0=h0: e.scalar_tensor_tensor(
                                out=tmpS[ts_], in0=bank(pb), scalar=0.125, in1=bt[:, h0:h0 + 4, :].rearrange("p h q -> p (h q)"),
                                op0=ALU.mult, op1=ALU.add),
                                reads=[f"ps{pb}"] + btn, writes=tmpS_n[ts_])
                            P.op("act", lambda e, ts_=ts_, hk=hk, ki=ki, half=half: e.activation(
                                out=PT[(hk, ki)][:, half * 512:(half + 1) * 512], in_=tmpS[ts_], func=AF.Exp),
                                reads=tmpS_n[ts_], writes=PT_n[(hk, ki)])
                for g in range(8):
                    P.op("pe", lambda e, g=g, Wg=Wg: e.matmul(ps[:, (3 - g // 4) * 512 + (g % 4) * 128:(3 - g // 4) * 512 + (g % 4 + 1) * 128], lhsT=Wg[:, g, :],
                                                            rhs=vnb[:, g * 128:(g + 1) * 128], start=True, stop=True),
                         reads=Wgn + vnb_n, writes=[f"ps{3 - g // 4}"])
                for g in range(8):
                    P.op("dve", lambda e, g=g, bg=bg, bi=bi: e.scalar_tensor_tensor(
                        out=X1[:, bi, g * 128:(g + 1) * 128], in0=ps[:, (3 - g // 4) * 512 + (g % 4) * 128:(3 - g // 4) * 512 + (g % 4 + 1) * 128], scalar=bg[:, g:g + 1],
                        in1=X1[:, bi, 1024 + g * 128:1024 + (g + 1) * 128], op0=ALU.add, op1=ALU.mult),
                        reads=[f"ps{3 - g // 4}", bgn, f"X1_{bi}_hi"], writes=[f"X1_{bi}_lo"])
                P.op("act", lambda e, zv=zv: e.activation(out=mixb[:, 1024:2048], in_=zv, func=AF.Square, accum_out=stat[:, 6:7]),
                     reads=[f"X1_{bi}_lo"], writes=mixb_hi_n + ["st6"])
                rstd_of(stat[:, 6:7], stat[:, 7:8], 1.0 / 1024, ["st6"], "os")
                P.op("dve", lambda e, zv=zv: e.scalar_tensor_tensor(out=mixb[:, 1024:2048], in0=zv, scalar=stat[:, 7:8], in1=g_os[:],
                                                                    op0=ALU.mult, op1=ALU.mult),
                     reads=[f"X1_{bi}_lo", "os_r", "g_os"], writes=mixb_hi_n)
                nk = len(kts)
                for hk in range(2):
                    for g in range(8):
                        h = hk * 8 + g
                        for ki, (ks_, bt, btn) in enumerate(kts):
                            P.op("pe", lambda e, hk=hk, g=g, ki=ki, ks_=ks_, nk=nk: e.matmul(
                                ps[:, (6 + hk) * 512 + g * 64:(6 + hk) * 512 + (g + 1) * 64], lhsT=PT[(hk, ki)][:, g * 128:(g + 1) * 128],
                                rhs=Vb[ks_][:, hk * 64:(hk + 1) * 64], start=(ki == 0), stop=(ki == nk - 1)),
                                reads=PT_n[(hk, ki)] + [f"Vb{ks_}"], writes=[f"ps{6 + hk}"])
                        for ki, (ks_, bt, btn) in enumerate(kts):
                            P.op("pe", lambda e, hk=hk, g=g, ki=ki, h=h, nk=nk: e.matmul(
                                ps[:, 2 * 512 + h:2 * 512 + h + 1], lhsT=PT[(hk, ki)][:, g * 128:(g + 1) * 128],
                                rhs=onesb[:, 0:1], start=(ki == 0), stop=(ki == nk - 1)),
                                reads=PT_n[(hk, ki)] + ["onesb"], writes=["ps2"])
                P.op("dve", lambda e: e.tensor_tensor(out=stat[:, 32:48], in0=ps[:, 1024:1040], in1=esink[:], op=ALU.add),
                     reads=["ps2", "esink"], writes=["st_den"])
                P.op("dve", lambda e: e.reciprocal(out=stat[:, 32:48], in_=stat[:, 32:48]), reads=["st_den"], writes=["st_rden"])
                for hk in range(2):
                    P.op("dve", lambda e, hk=hk: e.tensor_tensor(
                        out=o32[:, hk * 512:(hk + 1) * 512].rearrange("p (g d) -> p g d", d=64),
                        in0=ps[:, (6 + hk) * 512:(7 + hk) * 512].rearrange("p (g d) -> p g d", d=64),
                        in1=stat[:, 32 + hk * 8:40 + hk * 8].unsqueeze(2).to_broadcast([128, 8, 64]), op=ALU.mult),
                        reads=[f"ps{6 + hk}", "st_rden"], writes=o32_n)
                P.op("act", lambda e: e.activation(out=mixb[:, 0:1024], in_=o32, func=AF.Square, accum_out=stat[:, 2:3]),
                     reads=o32_n, writes=mixb_lo_n + ["st2"])
                rstd_of(stat[:, 2:3], stat[:, 3:4], 1.0 / 1024, ["st2"], "oa")
                P.op("dve", lambda e: e.scalar_tensor_tensor(out=mixb[:, 0:1024], in0=o32, scalar=stat[:, 3:4], in1=g_oa[:],
                                                             op0=ALU.mult, op1=ALU.mult),
                     reads=o32_n + ["oa_r", "g_oa"], writes=mixb_lo_n)
                transposes_to(mixb, 16, Tr[:, :, bi * 128:(bi + 1) * 128], mixb_n, [f"T{bi}"])
                P.dma("pool", "xld2", lambda e, bi=bi: e.dma_start(out=X1[:, bi, :], in_=xsrc(bi)),
                      writes=[f"X1_{bi}_lo", f"X1_{bi}_hi"])

            ckpt(3)
            for c in range(4):
                s = next_chunk("out")
                wv = SLOT_AP[s].rearrange("p (c n) -> p c n", c=16)
                for bi in range(nb):
                    pb = 2 + (c * nb + bi) % 4
                    for kc in range(16):
                        P.op("pe", lambda e, pb=pb, kc=kc, bi=bi, wv=wv: e.matmul(
                            bank(pb), lhsT=Tr[:, kc, bi * 128:(bi + 1) * 128], rhs=wv[:, kc, :], start=(kc == 0), stop=(kc == 15)),
                            reads=[f"T{bi}"] + SLOT_N[s], writes=[f"ps{pb}"])
                    which = "lo" if c < 2 else "hi"
                    P.op("dve", lambda e, pb=pb, bi=bi, c=c: e.tensor_tensor(out=X1[:, bi, c * 512:(c + 1) * 512], in0=bank(pb),
                                                                           in1=X1[:, bi, c * 512:(c + 1) * 512], op=ALU.add),
                         reads=[f"ps{pb}", f"X1_{bi}_{which}"], writes=[f"X1_{bi}_{which}"])
                    if c == 3:
                        if bi >= 1:
                            transposes_to(mixb, 16, Tr[:, :, (bi - 1) * 128:bi * 128], mixb_n, [f"T{bi - 1}"])
                        x1n = [f"X1_{bi}_lo", f"X1_{bi}_hi"]
                        P.op("act", lambda e, bi=bi: e.activation(out=mixb, in_=X1[:, bi, :], func=AF.Square, accum_out=stat[:, 0:1]),
                             reads=x1n, writes=mixb_n + ["st0"])
                        rstd_of(stat[:, 0:1], stat[:, 1:2], 1.0 / D, ["st0"], "n1")
                        P.op("dve", lambda e, bi=bi: e.scalar_tensor_tensor(out=mixb, in0=X1[:, bi, :], scalar=stat[:, 1:2], in1=g_ffn[:],
                                                                            op0=ALU.mult, op1=ALU.mult),
                             reads=x1n + ["n1_r", "g_ffn"], writes=mixb_n)
            transposes_to(mixb, 16, Tr[:, :, (nb - 1) * 128:nb * 128], mixb_n, [f"T{nb - 1}"])

            ckpt(4)
            Tn = [f"T{bi}" for bi in range(nb)]
            ev = 0
            for fg in range(16):
                s = next_chunk("up")
                wv = SLOT_AP[s].rearrange("p (c n) -> p c n", c=16)
                for fc in range(4):
                    f = fg * 4 + fc
                    pb = f % 4
                    for dc in range(16):
                        P.op("pe", lambda e, pb=pb, dc=dc, fc=fc, wv=wv: e.matmul(
                            bank(pb, ntok), lhsT=wv[:, dc, fc * 128:(fc + 1) * 128], rhs=Tr[:, dc, 0:ntok], start=(dc == 0), stop=(dc == 15)),
                            reads=Tn + SLOT_N[s], writes=[f"ps{pb}"])
                    tr = ev % 2
                    ev += 1
                    P.op("act", lambda e, pb=pb, tr=tr: e.activation(out=tmpR[tr][:, 0:ntok], in_=bank(pb, ntok), func=AF.Relu),
                         reads=[f"ps{pb}"], writes=[f"tmpR{tr}"])
                    hv = hid(f, 0, ntok)
                    P.op("dve", lambda e, hv=hv, tr=tr: e.tensor_tensor(out=hv, in0=tmpR[tr][:, 0:ntok], in1=tmpR[tr][:, 0:ntok],
                                                                     op=ALU.mult),
                         reads=[f"tmpR{tr}"], writes=[hid_n(f)])

            ckpt(5)
            ydst = (lambda bi: ys.ap()) if sample else (lambda bi: yp.ap()[(4 * t + bi) * 128:(4 * t + bi + 1) * 128, :])
            yt = 0
            for r in range(2):
                for g8 in range(8):
                    s = next_chunk("down")
                    wv = SLOT_AP[s].rearrange("p (c n) -> p c n", c=8)
                    for fc in range(8):
                        f = g8 * 8 + fc
                        for bi in range(nb):
                            for nh in range(2):
                                pb = bi * 2 + nh
                                hv = hid(f, bi * 128, (bi + 1) * 128)
                                P.op("pe", lambda e, pb=pb, f=f, hv=hv, nh=nh, fc=fc, wv=wv: e.matmul(
                                    bank(pb), lhsT=hv, rhs=wv[:, fc, nh * 512:(nh + 1) * 512],
                                    start=(f == 0), stop=(f == 63)),
                                    reads=[hid_n(f)] + SLOT_N[s], writes=[f"ps{pb}"])
                which = "lo" if r == 0 else "hi"
                for bi in range(nb):
                    ysl = yt % 2
                    yt += 1
                    ytmp = Tr[:, 8 * ysl:8 * ysl + 4, :].rearrange("p c n -> p (c n)").bitcast(F32)
                    for nh in range(2):
                        pb = bi * 2 + nh
                        P.op("dve", lambda e, pb=pb, bi=bi, nh=nh, ytmp=ytmp, r=r: e.tensor_tensor(
                            out=ytmp[:, nh * 512:(nh + 1) * 512], in0=bank(pb), in1=X1[:, bi, r * 1024 + nh * 512:r * 1024 + (nh + 1) * 512],
                            op=ALU.add),
                            reads=[f"ps{pb}", f"X1_{bi}_{which}"], writes=[f"ytmp{ysl}"] + [f"T{i}" for i in range(4)])
                    P.dma("pool", f"yst{ysl}", lambda e, bi=bi, ytmp=ytmp, r=r: e.dma_start(out=ydst(bi)[:, r * 1024:(r + 1) * 1024], in_=ytmp),
                          reads=[f"ytmp{ysl}"])
            for ysl in range(2):
                P.op("dve", lambda e: e.memset(stat[:, 60:61], 0.0), writes=[f"ytmp{ysl}"] + [f"T{i}" for i in range(4)] + ["st60"])

        try:
            for kind, t in (TILES if tiles is None else tiles):
                run_tile(kind, t)
            if tiles is None:
                assert wstate["i"] == len(chunks), (wstate, len(chunks))
        except _Stop:
            pass
        P.finalize(block)
    return nc


def _t5_bucket_static(n):
    import math
    try:
        import jax
        import jax.numpy as jnp
        with jax.default_device(jax.devices("cpu")[0]):
            nn = jnp.asarray(n, dtype=jnp.int32)
            half, max_exact = 16, 8
            offset = jnp.where(nn < 0, half, 0)
            a = jnp.abs(nn)
            af = jnp.maximum(a, 1).astype(jnp.float32)
            large = max_exact + (jnp.log(af / max_exact) / math.log(128 / max_exact) * (half - max_exact)).astype(jnp.int32)
            large = jnp.minimum(large, half - 1)
            return np.asarray(offset + jnp.where(a < max_exact, a, large))
    except Exception:
        nn = np.asarray(n, dtype=np.int32)
        half, max_exact = 16, 8
        offset = np.where(nn < 0, half, 0)
        a = np.abs(nn)
        af = np.maximum(a, 1).astype(np.float32)
        large = max_exact + (np.log(af / np.float32(max_exact)) / np.float32(math.log(128 / max_exact))
                             * np.float32(half - max_exact)).astype(np.int32)
        large = np.minimum(large, half - 1)
        return offset + np.where(a < max_exact, a, large)


_NC_CACHE = {}


def prep_inputs(x_prompt, x_sample, cache_attn_k, cache_attn_v, rel_bias_table, ln_mix_g, w_in,
                q_norm_g, k_norm_g, attn_sinks, sgu_norm_g, sgu_w, sgu_b, out_norm_attn_g,
                out_norm_sgu_g, w_out, ln_ffn_g, w_ffn_up, w_ffn_down):
    f = lambda a: np.ascontiguousarray(np.asarray(a, dtype=np.float32))
    x_prompt, x_sample = f(x_prompt), f(x_sample)
    hk, g, d = np.meshgrid(np.arange(2), np.arange(8), np.arange(64), indexing="ij")
    qcols = ((hk * 8 + g) * 64 + d).transpose(1, 0, 2).reshape(-1)
    perm = np.concatenate([qcols, np.arange(1024, 1280), np.arange(2304, 3328), np.arange(1280, 2304)])
    w_in_p = np.asarray(w_in)[0][:, perm]

    def img_cols(w, col0, wdt):
        return w[:, col0:col0 + wdt].reshape(16, 128, wdt).transpose(1, 0, 2).reshape(128, 16 * wdt)

    IN_COL0 = [0, 512, 1024, 1280, 1792, 2304, 2816]
    IN_W = [512, 512, 256, 512, 512, 512, 512]
    w_in_img = f(np.concatenate([img_cols(w_in_p, c0, wd) for c0, wd in zip(IN_COL0, IN_W)], axis=1))
    wo = np.asarray(w_out)[0]
    w_out_img = f(np.concatenate([img_cols(wo, c * 512, 512) for c in range(4)], axis=1))
    wu = np.asarray(w_ffn_up)[0]
    w_up_img = f(np.concatenate([img_cols(wu, c * 512, 512) for c in range(16)], axis=1))
    wd_ = np.asarray(w_ffn_down)[0]
    w_down_img = f(np.concatenate(
        [wd_[g * 1024:(g + 1) * 1024, r * 1024:(r + 1) * 1024].reshape(8, 128, 1024).transpose(1, 0, 2).reshape(128, 8192)
         for r in range(2) for g in range(8)], axis=1))
    m = np.arange(384)
    bucket = _t5_bucket_static(255 - m)
    oh = np.zeros((32, 384), np.float32)
    oh[bucket, m] = 1.0
    common = {
        "table": f(rel_bias_table), "oh": oh, "ident": np.eye(128, dtype=np.float32),
        "w_in": w_in_img, "w_out": w_out_img, "w_up": w_up_img, "w_down": w_down_img,
        "g_mix": f(ln_mix_g), "g_ffn": f(ln_ffn_g), "g_sgu": f(sgu_norm_g), "g_oa": f(out_norm_attn_g), "g_os": f(out_norm_sgu_g),
        "gq": f(np.asarray(q_norm_g)[0][None, :]), "gk": f(np.asarray(k_norm_g)[0][None, :]), "sinks": f(np.asarray(attn_sinks)[0].reshape(1, 16)),
        "wsT": f(np.asarray(sgu_w)[0].transpose(2, 0, 1)), "bsT": f(np.asarray(sgu_b)[0].T),
    }
    ck = np.asarray(cache_attn_k)[0]
    cvv = np.asarray(cache_attn_v)[0]
    in_maps = []
    for c in range(NCORES):
        mm = dict(common)
        mm["xp"] = x_prompt[c]
        mm["xs"] = f(x_sample[2 * c:2 * c + 2].reshape(128, D))
        mm["ckT"] = f(ck[2 * c:2 * c + 2].reshape(2, 128, 128).transpose(0, 2, 1))
        mm["cv"] = f(cvv[2 * c:2 * c + 2].reshape(2, 128, 128))
        in_maps.append(mm)
    return in_maps


def kernel(**inputs):
    in_maps = prep_inputs(**inputs)
    if "nc" not in _NC_CACHE:
        _NC_CACHE["nc"] = build_program()
    nc = _NC_CACHE["nc"]
    res = run_bass_kernel_spmd(nc, in_maps, core_ids=list(range(NCORES)))
    R = res.results
    y_prompt = np.stack([R[c]["yp"] for c in range(NCORES)]).astype(np.float32)
    y_sample = np.concatenate([R[c]["ys"].reshape(2, 64, D) for c in range(NCORES)]).astype(np.float32)
    kpo = np.stack([R[c]["kp"].reshape(128, 2, 64) for c in range(NCORES)])[None].astype(np.float32)
    vpo = np.stack([R[c]["vp"].reshape(128, 2, 64) for c in range(NCORES)])[None].astype(np.float32)
    kso = np.concatenate([R[c]["ks"].reshape(2, 64, 2, 64) for c in range(NCORES)])[None].astype(np.float32)
    vso = np.concatenate([R[c]["vs"].reshape(2, 64, 2, 64) for c in range(NCORES)])[None].astype(np.float32)
    sgo = np.concatenate([R[c]["sgv"].reshape(2, 64, 1024) for c in range(NCORES)])[None].astype(np.float32)
    return (y_prompt, y_sample, kpo, vpo, kso, vso, sgo)
```

```python
import numpy as np
import concourse.bass as bass
import concourse.mybir as mybir
from concourse.bass_utils import run_bass_kernel_spmd

F32 = mybir.dt.float32
BF16 = mybir.dt.bfloat16
AF = mybir.ActivationFunctionType
ALU = mybir.AluOpType
AX = mybir.AxisListType

D = 2048
DFF = 8192
NCORES = 8
EPS = 1e-6
NEGB = -30000.0
import os
SAMPLE_RING = int(os.environ.get("SAMPLE_RING", "6"))
ENGS = ("pe", "act", "dve", "pool", "sp")


class _Op:
    __slots__ = ("eng", "fn", "reads", "writes", "dma", "deps", "signal", "ev", "idx")

    def __init__(self, eng, fn, reads, writes, dma):
        self.eng, self.fn, self.reads, self.writes, self.dma = eng, fn, reads, writes, dma
        self.deps = set()
        self.signal = False
        self.ev = None


class Prog:
    def __init__(self, nc, same_engine_sync=True):
        self.nc = nc
        self.ops = []
        self.last_w = {}
        self.readers = {}
        self.same_engine_sync = same_engine_sync

    def _add(self, eng, fn, reads, writes, dma=None):
        op = _Op(eng, fn, tuple(reads), tuple(writes), dma)
        op.idx = len(self.ops)
        for r in op.reads:
            w = self.last_w.get(r)
            if w is not None:
                op.deps.add(w)
        for w_ in op.writes:
            w = self.last_w.get(w_)
            if w is not None:
                op.deps.add(w)
            latest = {}
            for rd in self.readers.get(w_, ()):
                ro = self.ops[rd]
                if ro.dma is not None:
                    op.deps.add(rd)
                else:
                    latest[ro.eng] = max(latest.get(ro.eng, -1), rd)
            op.deps.update(latest.values())
        for r in op.reads:
            self.readers.setdefault(r, []).append(op.idx)
        for w_ in op.writes:
            self.last_w[w_] = op.idx
            self.readers[w_] = []
        op.deps.discard(op.idx)
        self.ops.append(op)
        return op

    def op(self, eng, fn, reads=(), writes=()):
        return self._add(eng, fn, reads, writes)

    def dma(self, eng, sem_name, fn, reads=(), writes=()):
        return self._add(eng, fn, reads, writes, dma=sem_name)

    def finalize(self, block):
        import os
        nc = self.nc
        ops = self.ops
        nmax = int(os.environ.get("BISECT_N", "0"))
        if nmax:
            ops = ops[:nmax]
            for i, o in enumerate(ops[-3:]):
                print("last ops:", o.idx, o.eng, o.reads, o.writes, o.dma)
        print("n_ops", len(ops))
        for op in ops:
            for d in op.deps:
                p = ops[d]
                if p.dma is not None:
                    continue
                if p.eng == op.eng and (p.eng == "pe" or not self.same_engine_sync):
                    continue
                p.signal = True
        eng_sem = {e: nc.alloc_semaphore("sem_" + e) for e in ENGS}
        self.all_sems = list(eng_sem.values())
        cnt = {e: 0 for e in ENGS}
        dsem, dcnt = {}, {}
        for op in ops:
            if op.dma is not None:
                if op.dma not in dsem:
                    dsem[op.dma] = nc.alloc_semaphore("dsem_" + op.dma)
                    self.all_sems.append(dsem[op.dma])
                    dcnt[op.dma] = 0
                dcnt[op.dma] += 16
                op.ev = (op.dma, dcnt[op.dma])
            elif op.signal:
                cnt[op.eng] += 1
                op.ev = (op.eng, cnt[op.eng])
        issued = {k: 0 for k in dsem}
        waits_for = []
        for op in ops:
            w = {}
            for d in op.deps:
                p = ops[d]
                if p.dma is not None:
                    w[("d", p.dma)] = max(w.get(("d", p.dma), 0), issued[p.dma])
                elif p.ev is not None:
                    w[("e", p.eng)] = max(w.get(("e", p.eng), 0), p.ev[1])
            waits_for.append(w)
            if op.dma is not None:
                issued[op.dma] += 16
        final = dict(dcnt)

        def semof(k):
            return dsem[k[1]] if k[0] == "d" else eng_sem[k[1]]

        def emit(engname, engobj):
            waited = {}
            for op, w in zip(ops, waits_for):
                if op.eng != engname:
                    continue
                for k, v in w.items():
                    if waited.get(k, 0) >= v:
                        continue
                    engobj.wait_ge(semof(k), v)
                    waited[k] = v
                inst = op.fn(engobj)
                if op.dma is not None:
                    inst.then_inc(dsem[op.dma], 16)
                elif op.signal:
                    inst.then_inc(eng_sem[op.eng], 1)
            if engname == "sp":
                for k, v in final.items():
                    engobj.wait_ge(dsem[k], v)

        for sm in self.all_sems:
            nc.sync.sem_clear(sm)
        nc.all_engine_barrier()
        block = nc.Block().__enter__()
        self._block = block

        @block.tensor
        def _(e):
            emit("pe", e)

        @block.scalar
        def _(e):
            emit("act", e)

        @block.vector
        def _(e):
            emit("dve", e)

        @block.gpsimd
        def _(e):
            emit("pool", e)

        @block.sync
        def _(e):
            emit("sp", e)

        block.__exit__(None, None, None)


class _Stop(Exception):
    pass


def build_program(stage=99, tiles=None):
    nc = bass.Bass("TRN2", target_bir_lowering=False)

    def din(name, shape):
        return nc.dram_tensor(name, list(shape), F32, kind="ExternalInput")

    def dout(name, shape):
        return nc.dram_tensor(name, list(shape), F32, kind="ExternalOutput")

    xp = din("xp", (2048, D))
    xs = din("xs", (128, D))
    ckT = din("ckT", (2, 128, 128))
    cv = din("cv", (2, 128, 128))
    table = din("table", (32, 16))
    oh = din("oh", (32, 384))
    ident = din("ident", (128, 128))
    w_in = din("w_in", (128, 16 * 3328))
    w_out = din("w_out", (128, 4 * 8192))
    w_up = din("w_up", (128, 16 * 8192))
    w_down = din("w_down", (128, 16 * 8192))
    wimg = {"in": w_in, "out": w_out, "up": w_up, "down": w_down}
    wscr = {k: nc.dram_tensor("scr_" + k, list(v.shape), BF16, kind="ExternalOutput") for k, v in wimg.items()}
    g_mix_d = din("g_mix", (1, D))
    g_ffn_d = din("g_ffn", (1, D))
    g_sgu_d = din("g_sgu", (1, 1024))
    g_oa_d = din("g_oa", (1, 1024))
    g_os_d = din("g_os", (1, 1024))
    gq_d = din("gq", (1, 64))
    gk_d = din("gk", (1, 64))
    sinks_d = din("sinks", (1, 16))
    wsT_d = din("wsT", (128, 8, 128))
    bsT_d = din("bsT", (128, 8))
    scr = nc.dram_tensor("scr", [16, 384], F32, kind="Internal")

    yp = dout("yp", (2048, D))
    ys = dout("ys", (128, D))
    kp_o = dout("kp", (128, 128))
    vp_o = dout("vp", (128, 128))
    ks_o = dout("ks", (128, 128))
    vs_o = dout("vs", (128, 128))
    sgv_o = dout("sgv", (128, 1024))

    def sb(name, shape, dt):
        return nc.alloc_sbuf_tensor(name, list(shape), dt)

    identb = sb("identb", (128, 128), BF16)
    g_mix = sb("g_mix_s", (128, D), F32)
    g_ffn = sb("g_ffn_s", (128, D), F32)
    g_sgu = sb("g_sgu_s", (128, 1024), F32)
    g_oa = sb("g_oa_s", (128, 1024), F32)
    g_os = sb("g_os_s", (128, 1024), F32)
    gq_t = sb("gq_s", (128, 64), F32)
    gk_t = sb("gk_s", (128, 64), F32)
    esink = sb("esink", (128, 16), F32)
    epsb = sb("epsb", (128, 1), F32)
    onesb = sb("onesb", (128, 2), BF16)
    WsT = sb("WsT", (128, 8, 128), BF16)
    bsT = sb("bsT_s", (128, 8), F32)
    bsTs = sb("bsTs", (128, 8), F32)
    biasP = sb("biasP", (128, 16, 128), F32)
    biasO = sb("biasO", (128, 16, 128), F32)
    NSLOT = 5
    kTz = [[sb(f"kTz{hk}_{s}", (128, 128), BF16) for s in range(NSLOT)] for hk in range(2)]
    Vb = [sb(f"Vb{s}", (128, 128), BF16) for s in range(NSLOT)]
    qT = sb("qT", (128, 8, 128), BF16)
    stat = sb("stat", (128, 64), F32)
    tmpR = [sb(f"tmpR{i}", (128, 512), F32) for i in range(2)]
    X1 = sb("X1", (128, 4, D), F32)
    Hr = sb("Hr", (128, 16384), F32)
    Hrb = Hr[:, :].bitcast(BF16)
    Tr = sb("Tr", (128, 16, 512), BF16)
    Wr = [sb(f"Wr{i}", (128, 8192), BF16) for i in range(2)]
    ps = nc.alloc_psum_tensor("ps", [128, 4096], F32)

    def bank(i, n=512):
        return ps[:, i * 512:i * 512 + n]

    ptr = ps[:, 0:1024].bitcast(BF16)
    ptr2 = ps[:, 3072:4096].bitcast(BF16)
    tstate = {"n": 0}

    class HAlloc:
        def __init__(self):
            self.off = 0

        def take(self, nbytes, dt, shape=None):
            start = (self.off + 1023) // 1024 * 1024
            self.off = start + nbytes
            assert self.off <= 65536, self.off
            ap = Hr[:, start // 4:(start + nbytes) // 4]
            if dt == BF16:
                ap = Hrb[:, start // 2:(start + nbytes) // 2]
            names = [f"H{i}" for i in range(start // 1024, (start + nbytes + 1023) // 1024)]
            return ap, names

    ha = HAlloc()
    zqk, zqk_n = [], []
    for b in range(4):
        a, n = ha.take(1152 * 4, F32)
        zqk.append(a)
        zqk_n.append(n)
    tmpq, tmpq_n = ha.take(1152 * 4, F32)
    qknb, qknb_n = ha.take(1152 * 2, BF16)
    tmpS, tmpS_n = [], []
    for i in range(2):
        a, n = ha.take(512 * 4, F32)
        tmpS.append(a)
        tmpS_n.append(n)
    PT, PT_n = {}, {}
    for hk in range(2):
        for kt in range(3):
            a, n = ha.take(1024 * 2, BF16)
            PT[(hk, kt)] = a
            PT_n[(hk, kt)] = n
    o32, o32_n = ha.take(1024 * 4, F32)
    vnb, vnb_n = ha.take(1024 * 2, BF16)
    xtmp, xtmp_n = ha.take(D * 4, F32)
    hank, hank_n = xtmp, xtmp_n
    mixb, mixb_n = ha.take(D * 2, BF16)
    mixb_lo_n, mixb_hi_n = mixb_n[:2], mixb_n[2:]
    assert len(mixb_n) == 4
    h_used = ha.off
    tbl = tmpR[1][0:32, 384:400]
    ohs = tmpR[0][0:32, 0:384]
    srow = tmpR[1][0:16, 0:384]
    kn32 = o32[:, 0:128]
    v32 = o32[:, 128:256]
    zq1_start = 5 * 1024
    biasPB = Hr[:, zq1_start // 4:(zq1_start + 8192) // 4]
    wsts_start = 15 * 1024
    WsTs = Hrb[:, wsts_start // 2:(wsts_start + 2048) // 2].rearrange("p (g i) -> p g i", g=8)
    WsTs_n = ["H15", "H16"]
    biasPB_n = [f"H{i}" for i in range(zq1_start // 1024, zq1_start // 1024 + 8)]
    ALLH = [f"H{i}" for i in range(64)]

    hstate = {"compact": False}

    def hid(f, t0=0, t1=512):
        if hstate["compact"]:
            return Hrb[:, f * 128 + t0:f * 128 + t1]
        return Hrb[:, f * 512 + t0:f * 512 + t1]

    def hid_n(f):
        return f"H{f // 4}" if hstate["compact"] else f"H{f}"

    P = Prog(nc)
    _cst = {"n": 0}
    _orig_dma = P.dma

    def _dma(eng, sem_name, fn, reads=(), writes=()):
        if sem_name == "cst":
            _cst["n"] += 1
            sem_name = f"cst{_cst['n']}"
        return _orig_dma(eng, sem_name, fn, reads=reads, writes=writes)

    P.dma = _dma

    TILES = [("p", t) for t in range(4)] + [("s", 0)]
    chunks = []
    for _ in (TILES if tiles is None else tiles):
        chunks += [("in", c) for c in range(7)]
        chunks += [("out", c) for c in range(4)]
        chunks += [("up", c) for c in range(16)]
        chunks += [("down", r, g) for r in range(2) for g in range(8)]
    IN_COL0 = [0, 512, 1024, 1280, 1792, 2304, 2816]
    IN_W = [512, 512, 256, 512, 512, 512, 512]
    wstate = {"i": 0, "issued": 0}

    NCH = 43
    SLOT_AP = [Wr[0][:, :], Wr[1][:, :], X1[:, 1:3, :].rearrange("p b n -> p (b n)").bitcast(BF16),
               Hrb[:, 8192:16384], Hrb[:, 16384:24576], Hrb[:, 24576:32768]]
    SLOT_N = [["w0"], ["w1"], ["X1_1_lo", "X1_1_hi", "X1_2_lo", "X1_2_hi"],
              [f"H{i}" for i in range(16, 32)], [f"H{i}" for i in range(32, 48)], [f"H{i}" for i in range(48, 64)]]
    n_tiles_run = len(TILES if tiles is None else tiles)
    tile_kinds = [k for k, _ in (TILES if tiles is None else tiles)]

    def slot_of(i):
        j = i % NCH
        if tile_kinds[i // NCH] == "s":
            if SAMPLE_RING <= 2:
                return i % 2
            return [0, 1, 2][j % 3] if j < 11 else [2, 0, 1, 3, 4, 5][(j - 11) % 6]
        return i % 2

    def ahead_of(i):
        j = i % NCH
        if tile_kinds[i // NCH] == "s":
            if SAMPLE_RING <= 2:
                return 1
            return 2 if j < 11 else 5
        return 1

    slot_last = {}

    def issue_chunk(i):
        c = chunks[i]
        s = slot_of(i)
        assert slot_last.get(s, -1) < wstate["i"], (i, s, slot_last.get(s), wstate["i"])
        slot_last[s] = i
        tile_i, j = i // NCH, i % NCH
        multi = n_tiles_run >= 3
        if not multi:
            cast, wback = tile_i == 0, tile_i == 0
        elif tile_i == 0:
            cast, wback = True, (j % 2 == 0)
        elif tile_i == 1:
            cast, wback = (j % 2 == 1), (j % 2 == 1)
        else:
            cast, wback = False, False
        kind = c[0]
        if kind == "in":
            off, ln = 16 * IN_COL0[c[1]], 16 * IN_W[c[1]]
        elif kind == "down":
            off, ln = (c[1] * 8 + c[2]) * 8192, 8192
        else:
            off, ln = c[1] * 8192, 8192
        d = SLOT_AP[s][:, 0:ln]
        rname = f"scr_{kind}_{off}"
        if cast:
            src = wimg[kind].ap()[:, off:off + ln]
            P.dma("pool", f"w{s}q", lambda e, d=d, src=src: e.dma_start(out=d, in_=src), writes=SLOT_N[s])
            if wback:
                dsts = wscr[kind].ap()[:, off:off + ln]
                P.dma("sp", "wst", lambda e, d=d, dsts=dsts: e.dma_start(out=dsts, in_=d), reads=SLOT_N[s], writes=[rname])
        else:
            src = wscr[kind].ap()[:, off:off + ln]
            P.dma("sp", f"w{s}", lambda e, d=d, src=src: e.dma_start(out=d, in_=src), reads=[rname], writes=SLOT_N[s])

    def next_chunk(kind):
        i = wstate["i"]
        assert chunks[i][0] == kind, (chunks[i], kind)
        while wstate["issued"] <= min(i + ahead_of(i), len(chunks) - 1):
            nxt = wstate["issued"]
            if nxt // NCH != i // NCH and nxt > i + 1:
                break
            issue_chunk(nxt)
            wstate["issued"] += 1
        wstate["i"] += 1
        return slot_of(i)

    def rstd_of(ss_col, r_col, inv_n, reads, tag):
        P.op("act", lambda e: e.activation(out=r_col, in_=ss_col, func=AF.Sqrt, scale=inv_n, bias=epsb[:]),
             reads=reads + ["epsb"], writes=[tag + "_r"])
        P.op("dve", lambda e: e.reciprocal(out=r_col, in_=r_col), reads=[tag + "_r"], writes=[tag + "_r"])

    def transposes_to(src_tile, nch, dst_ap, src_names, dst_names, evac_eng="act"):
        k = tstate["n"] % 2
        tstate["n"] += 1
        pt = [ptr, ptr2][k]
        pn = [["ps0", "ps1"], ["ps6", "ps7"]][k]
        for c in range(nch):
            P.op("pe", lambda e, c=c, pt=pt: e.matmul(pt[:, c * 128:(c + 1) * 128], lhsT=src_tile[:, c * 128:(c + 1) * 128],
                                                    rhs=identb[:], start=True, stop=True, is_transpose=True),
                 reads=src_names + ["identb"], writes=pn)
        src = pt[:, 0:nch * 128].rearrange("p (c t) -> p c t", c=nch)
        if evac_eng == "act":
            P.op("act", lambda e: e.copy(out=dst_ap, in_=src), reads=pn, writes=dst_names)
        else:
            P.op("dve", lambda e: e.tensor_copy(out=dst_ap, in_=src), reads=pn, writes=dst_names)

    def toeplitz(dst, dst_names, c0):
        hk_ = hank.rearrange("p (h q) -> p h q", h=16)
        P.dma("sp", "cst", lambda e: e.dma_start(out=hk_, in_=bass.AP(tensor=scr, offset=c0, ap=[[1, 128], [384, 16], [1, 128]])),
              reads=["scr"], writes=hank_n)
        t = hank
        rev = bass.AP(tensor=t.tensor, offset=t.offset + 127, ap=[list(t.ap[0]), [128, 16], [-1, 128]])
        P.op("dve", lambda e: e.tensor_copy(out=dst, in_=rev), reads=hank_n, writes=dst_names)

    block = None
    if True:
        P.op("dve", lambda e: e.memset(epsb[:], EPS), writes=["epsb"])
        P.op("dve", lambda e: e.memset(onesb[:], 1.0), writes=["onesb"])
        for hk in range(2):
            for s in range(NSLOT):
                P.op("dve", lambda e, hk=hk, s=s: e.memset(kTz[hk][s][:], 0.0), writes=[f"kTz{hk}_{s}"])

        def bc_load(dst, src, n, name):
            P.dma("sp", "cst", lambda e: e.dma_start(out=dst[:], in_=src.ap().partition_broadcast(128)[:, 0, :]), writes=[name])

        idf = xtmp[:, 0:128]
        P.dma("sp", "cst", lambda e: e.dma_start(out=idf, in_=ident.ap()), writes=xtmp_n)
        P.op("dve", lambda e: e.tensor_copy(out=identb[:], in_=idf), reads=xtmp_n, writes=["identb"])
        bc_load(g_mix, g_mix_d, D, "g_mix")
        P.dma("sp", "cst", lambda e: e.dma_start(out=tbl, in_=table.ap()), writes=["tmpR1"])
        P.dma("sp", "cst", lambda e: e.dma_start(out=ohs, in_=oh.ap()), writes=["tmpR0"])

    def late_setup():
        bc_load(g_ffn, g_ffn_d, D, "g_ffn")
        bc_load(g_sgu, g_sgu_d, 1024, "g_sgu")
        bc_load(g_oa, g_oa_d, 1024, "g_oa")
        bc_load(g_os, g_os_d, 1024, "g_os")
        bc_load(gq_t, gq_d, 64, "gq")
        bc_load(gk_t, gk_d, 64, "gk")
        bc_load(esink, sinks_d, 16, "esink")
        P.op("act", lambda e: e.activation(out=esink[:], in_=esink[:], func=AF.Exp), reads=["esink"], writes=["esink"])
        wst = hank.rearrange("p (g i) -> p g i", g=16)[:, 0:8, :]
        P.dma("sp", "cst", lambda e: e.dma_start(out=wst, in_=wsT_d.ap()), writes=hank_n)
        P.op("dve", lambda e: e.tensor_copy(out=WsT[:], in_=wst), reads=hank_n, writes=["WsT"])
        P.op("dve", lambda e: e.memset(WsT[64:128, :, 0:64], 0.0), reads=[], writes=["WsT"])
        P.dma("sp", "cst", lambda e: e.dma_start(out=bsT[:], in_=bsT_d.ap()), writes=["bsT"])
        P.dma("sp", "cst", lambda e: e.dma_start(out=bsTs[0:64, :], in_=bsT_d.ap()[0:64, :]), writes=["bsTs"])
        P.dma("sp", "cst", lambda e: e.dma_start(out=bsTs[64:128, :], in_=bsT_d.ap()[0:64, :]), writes=["bsTs"])
        P.op("pe", lambda e: e.matmul(bank(2)[0:16, 0:384], lhsT=tbl, rhs=ohs, start=True, stop=True),
             reads=["tmpR0", "tmpR1"], writes=["ps2"])
        P.op("dve", lambda e: e.tensor_copy(out=srow, in_=bank(2)[0:16, 0:384]), reads=["ps2"], writes=["tmpR1"])
        P.dma("sp", "cst", lambda e: e.dma_start(out=scr.ap(), in_=srow), reads=["tmpR1"], writes=["scr"])
        toeplitz(biasP[:], ["biasP"], 0)
        P.op("dve", lambda e: e.memset(biasP[0:64, :, 64:128], NEGB), writes=["biasP"])
        toeplitz(biasO[:], ["biasO"], 128)
        P.op("dve", lambda e: e.memset(biasO[64:128, :, 0:64], NEGB), writes=["biasO"])

    late_done = {"v": False}
    if True:
        def ckpt(n):
            if stage <= n:
                raise _Stop()

        def run_tile(kind, t):
            sample = kind == "s"
            hstate["compact"] = sample and os.environ.get("NO_COMPACT", "0") != "1"
            nb = 1 if sample else 4
            ntok = nb * 128
            gB = [16] if sample else [4 * t + i for i in range(4)]
            xsrc = (lambda bi: xs.ap()) if sample else (lambda bi: xp.ap()[(4 * t + bi) * 128:(4 * t + bi + 1) * 128, :])

            if sample and not late_done["v"]:
                late_done["v"] = True
                late_setup()
            if sample:
                P.op("dve", lambda e: e.memset(biasP[:, :, 64:128], NEGB), writes=["biasP"])
                P.op("dve", lambda e: e.memset(biasO[0:64, :, 64:128], NEGB), writes=["biasO"])
                toeplitz(biasPB.rearrange("p (h q) -> p h q", h=16), biasPB_n, 64)
                P.op("dve", lambda e: e.memset(biasPB.rearrange("p (h q) -> p h q", h=16)[:, :, 0:64], NEGB), writes=biasPB_n)
                wstg = xtmp.rearrange("p (g i) -> p g i", g=16)[:, 0:8, :]
                P.dma("pool", "xld", lambda e: e.dma_start(out=wstg[0:64, :, 0:64], in_=wsT_d.ap()[0:64, :, 0:64]), writes=xtmp_n)
                P.dma("pool", "xld", lambda e: e.dma_start(out=wstg[64:128, :, 64:128], in_=wsT_d.ap()[0:64, :, 0:64]), writes=xtmp_n)
                P.op("dve", lambda e: e.memset(WsTs, 0.0), writes=WsTs_n)
                P.op("dve", lambda e: e.tensor_copy(out=WsTs[0:64, :, 0:64], in_=wstg[0:64, :, 0:64]), reads=xtmp_n, writes=WsTs_n)
                P.op("dve", lambda e: e.tensor_copy(out=WsTs[64:128, :, 64:128], in_=wstg[64:128, :, 64:128]), reads=xtmp_n, writes=WsTs_n)
                for sq in range(2):
                    slot = 2 + sq
                    st_ = xtmp[:, sq * 256:sq * 256 + 128]
                    sv_ = xtmp[:, sq * 256 + 128:sq * 256 + 256]
                    P.dma("pool", "xld", lambda e, st_=st_, sq=sq: e.dma_start(out=st_, in_=ckT.ap()[sq]), writes=xtmp_n)
                    P.dma("pool", "xld", lambda e, sv_=sv_, sq=sq: e.dma_start(out=sv_, in_=cv.ap()[sq]), writes=xtmp_n)
                    P.op("dve", lambda e, st_=st_, slot=slot: e.tensor_copy(out=kTz[0][slot][0:64, :], in_=st_[0:64, :]),
                         reads=xtmp_n, writes=[f"kTz0_{slot}"])
                    P.op("dve", lambda e, st_=st_, slot=slot: e.tensor_copy(out=kTz[1][slot][64:128, :], in_=st_[64:128, :]),
                         reads=xtmp_n, writes=[f"kTz1_{slot}"])
                    P.op("dve", lambda e, sv_=sv_, slot=slot: e.tensor_copy(out=Vb[slot][:], in_=sv_), reads=xtmp_n, writes=[f"Vb{slot}"])

            ckpt(0)
            for bi in range(nb):
                P.dma("pool", f"xld{bi}", lambda e, bi=bi: e.dma_start(out=X1[:, bi, :], in_=xsrc(bi)),
                      writes=[f"X1_{bi}_lo", f"X1_{bi}_hi"])
            def a1_elem(bi):
                zq3 = zqk[bi].rearrange("p (h d) -> p h d", d=64)
                tq3 = tmpq.rearrange("p (h d) -> p h d", d=64)
                P.op("dve", lambda e, bi=bi: e.tensor_tensor(out=tmpq, in0=zqk[bi], in1=zqk[bi], op=ALU.mult),
                     reads=zqk_n[bi], writes=tmpq_n)
                P.op("dve", lambda e: e.tensor_reduce(out=stat[:, 8:26], in_=tq3, op=ALU.add, axis=AX.X),
                     reads=tmpq_n, writes=["qk_r"])
                rstd_of(stat[:, 8:26], stat[:, 8:26], 1.0 / 64, ["qk_r"], "qk")
                P.op("dve", lambda e, zq3=zq3: e.tensor_tensor(out=tq3, in0=zq3, in1=stat[:, 8:26].unsqueeze(2).to_broadcast([128, 18, 64]),
                                                             op=ALU.mult),
                     reads=zqk_n[bi] + ["qk_r"], writes=tmpq_n)
                P.op("dve", lambda e: e.tensor_tensor(out=qknb[:, 0:1024].rearrange("p (h d) -> p h d", d=64), in0=tq3[:, 0:16, :],
                                                      in1=gq_t[:].unsqueeze(1).to_broadcast([128, 16, 64]), op=ALU.mult),
                     reads=tmpq_n + ["gq"], writes=qknb_n)
                P.op("dve", lambda e: e.tensor_tensor(out=qknb[:, 1024:1152].rearrange("p (h d) -> p h d", d=64), in0=tq3[:, 16:18, :],
                                                      in1=gk_t[:].unsqueeze(1).to_broadcast([128, 2, 64]), op=ALU.mult),
                     reads=tmpq_n + ["gk"], writes=qknb_n)
                if sample or (t == 3 and bi == 3):
                    P.op("dve", lambda e: e.tensor_tensor(out=kn32.rearrange("p (h d) -> p h d", d=64), in0=tq3[:, 16:18, :],
                                                          in1=gk_t[:].unsqueeze(1).to_broadcast([128, 2, 64]), op=ALU.mult),
                         reads=tmpq_n + ["gk"], writes=o32_n)
                    ko = ks_o if sample else kp_o
                    P.dma("pool", "ost", lambda e, ko=ko: e.dma_start(out=ko.ap(), in_=kn32), reads=o32_n)

            def a1_pe(bi):
                slot = gB[bi] % NSLOT
                for c in range(9):
                    P.op("pe", lambda e, c=c: e.matmul(ptr[:, c * 128:(c + 1) * 128], lhsT=qknb[:, c * 128:(c + 1) * 128],
                                                     rhs=identb[:], start=True, stop=True, is_transpose=True),
                         reads=qknb_n + ["identb"], writes=["ps0", "ps1"])
                P.op("act", lambda e: e.copy(out=qT[:], in_=ptr[:, 0:1024].rearrange("p (c t) -> p c t", c=8)),
                     reads=["ps0", "ps1"], writes=["qT"])
                P.op("dve", lambda e, slot=slot: e.tensor_copy(out=kTz[0][slot][0:64, :], in_=ptr[0:64, 1024:1152]),
                     reads=["ps0", "ps1"], writes=[f"kTz0_{slot}"])
                P.op("dve", lambda e, slot=slot: e.tensor_copy(out=kTz[1][slot][64:128, :], in_=ptr[64:128, 1024:1152]),
                     reads=["ps0", "ps1"], writes=[f"kTz1_{slot}"])

            def z1_unit(c, s, bi, pb):
                wdt = IN_W[c]
                wv = SLOT_AP[s][:, 0:16 * wdt].rearrange("p (c n) -> p c n", c=16)
                for dc in range(16):
                    P.op("pe", lambda e, pb=pb, dc=dc, bi=bi, wv=wv, wdt=wdt: e.matmul(
                        bank(pb, wdt), lhsT=Tr[:, dc, bi * 128:(bi + 1) * 128], rhs=wv[:, dc, :], start=(dc == 0), stop=(dc == 15)),
                        reads=[f"T{bi}"] + SLOT_N[s], writes=[f"ps{pb}"])
                slot = gB[bi] % NSLOT
                if c < 2:
                    P.op("dve", lambda e, pb=pb, bi=bi, c=c: e.tensor_copy(out=zqk[bi][:, c * 512:(c + 1) * 512], in_=bank(pb)),
                         reads=[f"ps{pb}"], writes=zqk_n[bi])
                elif c == 2:
                    P.op("dve", lambda e, pb=pb, bi=bi: e.tensor_copy(out=zqk[bi][:, 1024:1152], in_=bank(pb, 128)),
                         reads=[f"ps{pb}"], writes=zqk_n[bi])
                    P.op("dve", lambda e, pb=pb, slot=slot: e.tensor_copy(out=Vb[slot][:], in_=ps[:, pb * 512 + 128:pb * 512 + 256]),
                         reads=[f"ps{pb}"], writes=[f"Vb{slot}"])
                    if sample or (t == 3 and bi == 3):
                        P.op("dve", lambda e, pb=pb: e.tensor_copy(out=v32, in_=ps[:, pb * 512 + 128:pb * 512 + 256]),
                             reads=[f"ps{pb}"], writes=o32_n)
                        vo = vs_o if sample else vp_o
                        P.dma("pool", "ost", lambda e, vo=vo: e.dma_start(out=vo.ap(), in_=v32), reads=o32_n)
                else:
                    half = (c - 3) % 2
                    which = "lo" if c < 5 else "hi"
                    col0 = (0 if c < 5 else 1024) + half * 512
                    P.op("act", lambda e, pb=pb, bi=bi, col0=col0: e.activation(out=X1[:, bi, col0:col0 + 512], in_=bank(pb),
                                                                               func=AF.Gelu_apprx_tanh),
                         reads=[f"ps{pb}"], writes=[f"X1_{bi}_{which}"])


            s0 = next_chunk("in")
            for bi in range(nb):
                x1n = [f"X1_{bi}_lo", f"X1_{bi}_hi"]
                P.op("act", lambda e, bi=bi: e.activation(out=mixb, in_=X1[:, bi, :], func=AF.Square, accum_out=stat[:, 0:1]),
                     reads=x1n, writes=mixb_n + ["st0"])
                rstd_of(stat[:, 0:1], stat[:, 1:2], 1.0 / D, ["st0"], "n1")
                P.op("dve", lambda e, bi=bi: e.scalar_tensor_tensor(out=mixb, in0=X1[:, bi, :], scalar=stat[:, 1:2], in1=g_mix[:],
                                                                    op0=ALU.mult, op1=ALU.mult),
                     reads=x1n + ["n1_r", "g_mix"], writes=mixb_n)
                transposes_to(mixb, 16, Tr[:, :, bi * 128:(bi + 1) * 128], mixb_n, [f"T{bi}"])
                if bi >= 1:
                    z1_unit(0, s0, bi - 1, 2 + (bi - 1) % 4)
            z1_unit(0, s0, nb - 1, 2 + (nb - 1) % 4)
            if not late_done["v"]:
                late_done["v"] = True
                late_setup()
            ckpt(1)
            for c in range(1, 7):
                s = next_chunk("in")
                for bi in range(nb):
                    z1_unit(c, s, bi, 2 + (c * nb + bi) % 4)
                if c == 2:
                    a1_elem(0)
                if c == 3:
                    a1_pe(0)

            ckpt(2)
            o_early = {"s": None, "done": set()}

            def o_unit(c, s, bi, pb):
                wv = SLOT_AP[s].rearrange("p (c n) -> p c n", c=16)
                for kc in range(16):
                    P.op("pe", lambda e, pb=pb, kc=kc, bi=bi, wv=wv: e.matmul(
                        bank(pb), lhsT=Tr[:, kc, bi * 128:(bi + 1) * 128], rhs=wv[:, kc, :], start=(kc == 0), stop=(kc == 15)),
                        reads=[f"T{bi}"] + SLOT_N[s], writes=[f"ps{pb}"])
                which = "lo" if c < 2 else "hi"
                P.op("dve", lambda e, pb=pb, bi=bi, c=c: e.tensor_tensor(out=X1[:, bi, c * 512:(c + 1) * 512], in0=bank(pb),
                                                                       in1=X1[:, bi, c * 512:(c + 1) * 512], op=ALU.add),
                     reads=[f"ps{pb}", f"X1_{bi}_{which}"], writes=[f"X1_{bi}_{which}"])

            for bi in range(nb):
                B = gB[bi]
                slot = B % NSLOT
                if bi > 0:
                    a1_elem(bi)
                    a1_pe(bi)
                if sample:
                    kts = [(2, biasP[:], ["biasP"]), (3, biasPB.rearrange("p (h q) -> p h q", h=16), biasPB_n), (slot, biasO[:], ["biasO"])]
                elif B == 0:
                    kts = [(slot, biasO[:], ["biasO"])]
                else:
                    kts = [((B - 1) % NSLOT, biasP[:], ["biasP"]), (slot, biasO[:], ["biasO"])]
                zv = X1[:, bi, 0:1024]
                uu = X1[:, bi, 1024:2048]
                P.op("act", lambda e, zv=zv: e.activation(out=vnb, in_=zv, func=AF.Square, accum_out=stat[:, 4:5]),
                     reads=[f"X1_{bi}_lo"], writes=vnb_n + ["st4"])
                rstd_of(stat[:, 4:5], stat[:, 5:6], 1.0 / 1024, ["st4"], "sv")
                if sample:
                    P.op("dve", lambda e, zv=zv: e.scalar_tensor_tensor(out=o32, in0=zv, scalar=stat[:, 5:6], in1=g_sgu[:],
                                                                        op0=ALU.mult, op1=ALU.mult),
                         reads=[f"X1_{bi}_lo", "sv_r", "g_sgu"], writes=o32_n)
                    P.dma("pool", "ost", lambda e: e.dma_start(out=sgv_o.ap(), in_=o32), reads=o32_n)
                P.op("dve", lambda e, zv=zv: e.scalar_tensor_tensor(out=vnb, in0=zv, scalar=stat[:, 5:6], in1=g_sgu[:],
                                                                    op0=ALU.mult, op1=ALU.mult),
                     reads=[f"X1_{bi}_lo", "sv_r", "g_sgu"], writes=vnb_n)
                Wg = WsTs if sample else WsT
                Wgn = WsTs_n if sample else ["WsT"]
                bg = bsTs if sample else bsT
                bgn = "bsTs" if sample else "bsT"
                cnt = 0
                for hk in range(2):
                    for ki, (ks_, bt, btn) in enumerate(kts):
                        for half in range(2):
                            pb = 4 + cnt % 2
                            ts_ = cnt % 2
                            cnt += 1
                            P.op("pe", lambda e, pb=pb, hk=hk, ks_=ks_, half=half: e.matmul(
                                bank(pb), lhsT=kTz[hk][ks_][:], rhs=qT[:, 4 * half:4 * half + 4, :], start=True, stop=True),
                                reads=[f"kTz{hk}_{ks_}", "qT"], writes=[f"ps{pb}"])
                            h0 = hk * 8 + 4 * half
                            P.op("dve", lambda e, pb=pb, ts_=ts_, bt=bt, h0=h0: e.scalar_tensor_tensor(
                                out=tmpS[ts_], in0=bank(pb), scalar=0.125, in1=bt[:, h0:h0 + 4, :].rearrange("p h q -> p (h q)"),
                                op0=ALU.mult, op1=ALU.add),
                                reads=[f"ps{pb}"] + btn, writes=tmpS_n[ts_])
                            P.op("act", lambda e, ts_=ts_, hk=hk, ki=ki, half=half: e.activation(
                                out=PT[(hk, ki)][:, half * 512:(half + 1) * 512], in_=tmpS[ts_], func=AF.Exp),
                                reads=tmpS_n[ts_], writes=PT_n[(hk, ki)])
                for g in range(8):
                    P.op("pe", lambda e, g=g, Wg=Wg: e.matmul(ps[:, (3 - g // 4) * 512 + (g % 4) * 128:(3 - g // 4) * 512 + (g % 4 + 1) * 128], lhsT=Wg[:, g, :],
                                                            rhs=vnb[:, g * 128:(g + 1) * 128], start=True, stop=True),
                         reads=Wgn + vnb_n, writes=[f"ps{3 - g // 4}"])
                for g in range(8):
                    P.op("dve", lambda e, g=g, bg=bg, bi=bi: e.scalar_tensor_tensor(
                        out=X1[:, bi, g * 128:(g + 1) * 128], in0=ps[:, (3 - g // 4) * 512 + (g % 4) * 128:(3 - g // 4) * 512 + (g % 4 + 1) * 128], scalar=bg[:, g:g + 1],
                        in1=X1[:, bi, 1024 + g * 128:1024 + (g + 1) * 128], op0=ALU.add, op1=ALU.mult),
                        reads=[f"ps{3 - g // 4}", bgn, f"X1_{bi}_hi"], writes=[f"X1_{bi}_lo"])
                P.op("act", lambda e, zv=zv: e.activation(out=mixb[:, 1024:2048], in_=zv, func=AF.Square, accum_out=stat[:, 6:7]),
                     reads=[f"X1_{bi}_lo"], writes=mixb_hi_n + ["st6"])
                rstd_of(stat[:, 6:7], stat[:, 7:8], 1.0 / 1024, ["st6"], "os")
                P.op("dve", lambda e, zv=zv: e.scalar_tensor_tensor(out=mixb[:, 1024:2048], in0=zv, scalar=stat[:, 7:8], in1=g_os[:],
                                                                    op0=ALU.mult, op1=ALU.mult),
                     reads=[f"X1_{bi}_lo", "os_r", "g_os"], writes=mixb_hi_n)
                nk = len(kts)
                for hk in range(2):
                    for g in range(8):
                        h = hk * 8 + g
                        for ki, (ks_, bt, btn) in enumerate(kts):
                            P.op("pe", lambda e, hk=hk, g=g, ki=ki, ks_=ks_, nk=nk: e.matmul(
                                ps[:, (6 + hk) * 512 + g * 64:(6 + hk) * 512 + (g + 1) * 64], lhsT=PT[(hk, ki)][:, g * 128:(g + 1) * 128],
                                rhs=Vb[ks_][:, hk * 64:(hk + 1) * 64], start=(ki == 0), stop=(ki == nk - 1)),
                                reads=PT_n[(hk, ki)] + [f"Vb{ks_}"], writes=[f"ps{6 + hk}"])
                        for ki, (ks_, bt, btn) in enumerate(kts):
                            P.op("pe", lambda e, hk=hk, g=g, ki=ki, h=h, nk=nk: e.matmul(
                                ps[:, 2 * 512 + h:2 * 512 + h + 1], lhsT=PT[(hk, ki)][:, g * 128:(g + 1) * 128],
                                rhs=onesb[:, 0:1], start=(ki == 0), stop=(ki == nk - 1)),
                                reads=PT_n[(hk, ki)] + ["onesb"], writes=["ps2"])
                if bi == nb - 1 and nb > 1:
                    o_early["s"] = next_chunk("out")
                    for b2 in range(nb - 1):
                        o_unit(0, o_early["s"], b2, 3 + b2 % 3)
                        o_early["done"].add(b2)
                P.op("dve", lambda e: e.tensor_tensor(out=stat[:, 32:48], in0=ps[:, 1024:1040], in1=esink[:], op=ALU.add),
                     reads=["ps2", "esink"], writes=["st_den"])
                P.op("dve", lambda e: e.reciprocal(out=stat[:, 32:48], in_=stat[:, 32:48]), reads=["st_den"], writes=["st_rden"])
                for hk in range(2):
                    P.op("dve", lambda e, hk=hk: e.tensor_tensor(
                        out=o32[:, hk * 512:(hk + 1) * 512].rearrange("p (g d) -> p g d", d=64),
                        in0=ps[:, (6 + hk) * 512:(7 + hk) * 512].rearrange("p (g d) -> p g d", d=64),
                        in1=stat[:, 32 + hk * 8:40 + hk * 8].unsqueeze(2).to_broadcast([128, 8, 64]), op=ALU.mult),
                        reads=[f"ps{6 + hk}", "st_rden"], writes=o32_n)
                P.op("act", lambda e: e.activation(out=mixb[:, 0:1024], in_=o32, func=AF.Square, accum_out=stat[:, 2:3]),
                     reads=o32_n, writes=mixb_lo_n + ["st2"])
                rstd_of(stat[:, 2:3], stat[:, 3:4], 1.0 / 1024, ["st2"], "oa")
                P.op("dve", lambda e: e.scalar_tensor_tensor(out=mixb[:, 0:1024], in0=o32, scalar=stat[:, 3:4], in1=g_oa[:],
                                                             op0=ALU.mult, op1=ALU.mult),
                     reads=o32_n + ["oa_r", "g_oa"], writes=mixb_lo_n)
                transposes_to(mixb, 16, Tr[:, :, bi * 128:(bi + 1) * 128], mixb_n, [f"T{bi}"])
                P.dma("pool", "xld2", lambda e, bi=bi: e.dma_start(out=X1[:, bi, :], in_=xsrc(bi)),
                      writes=[f"X1_{bi}_lo", f"X1_{bi}_hi"])

            ckpt(3)
            for c in range(4):
                s = o_early["s"] if (c == 0 and o_early["s"] is not None) else next_chunk("out")
                for bi in range(nb):
                    if c == 0 and bi in o_early["done"]:
                        continue
                    pb = 2 + (c * nb + bi) % 4
                    o_unit(c, s, bi, pb)
                    if c == 3:
                        if bi >= 1:
                            transposes_to(mixb, 16, Tr[:, :, (bi - 1) * 128:bi * 128], mixb_n, [f"T{bi - 1}"])
                        x1n = [f"X1_{bi}_lo", f"X1_{bi}_hi"]
                        P.op("act", lambda e, bi=bi: e.activation(out=mixb, in_=X1[:, bi, :], func=AF.Square, accum_out=stat[:, 0:1]),
                             reads=x1n, writes=mixb_n + ["st0"])
                        rstd_of(stat[:, 0:1], stat[:, 1:2], 1.0 / D, ["st0"], "n1")
                        P.op("dve", lambda e, bi=bi: e.scalar_tensor_tensor(out=mixb, in0=X1[:, bi, :], scalar=stat[:, 1:2], in1=g_ffn[:],
                                                                            op0=ALU.mult, op1=ALU.mult),
                             reads=x1n + ["n1_r", "g_ffn"], writes=mixb_n)
            transposes_to(mixb, 16, Tr[:, :, (nb - 1) * 128:nb * 128], mixb_n, [f"T{nb - 1}"])

            ckpt(4)
            Tn = [f"T{bi}" for bi in range(nb)]
            ev = 0
            for fg in range(16):
                s = next_chunk("up")
                wv = SLOT_AP[s].rearrange("p (c n) -> p c n", c=16)
                for fc in range(4):
                    f = fg * 4 + fc
                    pb = f % 4
                    for dc in range(16):
                        P.op("pe", lambda e, pb=pb, dc=dc, fc=fc, wv=wv: e.matmul(
                            bank(pb, ntok), lhsT=wv[:, dc, fc * 128:(fc + 1) * 128], rhs=Tr[:, dc, 0:ntok], start=(dc == 0), stop=(dc == 15)),
                            reads=Tn + SLOT_N[s], writes=[f"ps{pb}"])
                    tr = ev % 2
                    ev += 1
                    P.op("act", lambda e, pb=pb, tr=tr: e.activation(out=tmpR[tr][:, 0:ntok], in_=bank(pb, ntok), func=AF.Relu),
                         reads=[f"ps{pb}"], writes=[f"tmpR{tr}"])
                    hv = hid(f, 0, ntok)
                    P.op("dve", lambda e, hv=hv, tr=tr: e.tensor_tensor(out=hv, in0=tmpR[tr][:, 0:ntok], in1=tmpR[tr][:, 0:ntok],
                                                                     op=ALU.mult),
                         reads=[f"tmpR{tr}"], writes=[hid_n(f)])

            ckpt(5)
            ydst = (lambda bi: ys.ap()) if sample else (lambda bi: yp.ap()[(4 * t + bi) * 128:(4 * t + bi + 1) * 128, :])
            yt = 0
            for r in range(2):
                for g8 in range(8):
                    s = next_chunk("down")
                    wv = SLOT_AP[s].rearrange("p (c n) -> p c n", c=8)
                    for fc in range(8):
                        f = g8 * 8 + fc
                        for bi in range(nb):
                            for nh in range(2):
                                pb = bi * 2 + nh
                                hv = hid(f, bi * 128, (bi + 1) * 128)
                                P.op("pe", lambda e, pb=pb, f=f, hv=hv, nh=nh, fc=fc, wv=wv: e.matmul(
                                    bank(pb), lhsT=hv, rhs=wv[:, fc, nh * 512:(nh + 1) * 512],
                                    start=(f == 0), stop=(f == 63)),
                                    reads=[hid_n(f)] + SLOT_N[s], writes=[f"ps{pb}"])
                which = "lo" if r == 0 else "hi"
                for bi in range(nb):
                    ysl = yt % 2
                    yt += 1
                    ytmp = Tr[:, 8 * ysl:8 * ysl + 4, :].rearrange("p c n -> p (c n)").bitcast(F32)
                    for nh in range(2):
                        pb = bi * 2 + nh
                        P.op("dve", lambda e, pb=pb, bi=bi, nh=nh, ytmp=ytmp, r=r: e.tensor_tensor(
                            out=ytmp[:, nh * 512:(nh + 1) * 512], in0=bank(pb), in1=X1[:, bi, r * 1024 + nh * 512:r * 1024 + (nh + 1) * 512],
                            op=ALU.add),
                            reads=[f"ps{pb}", f"X1_{bi}_{which}"], writes=[f"ytmp{ysl}"] + [f"T{i}" for i in range(4)])
                    P.dma("pool", f"yst{ysl}", lambda e, bi=bi, ytmp=ytmp, r=r: e.dma_start(out=ydst(bi)[:, r * 1024:(r + 1) * 1024], in_=ytmp),
                          reads=[f"ytmp{ysl}"])
            for ysl in range(2):
                P.op("dve", lambda e: e.memset(stat[:, 60:61], 0.0), writes=[f"ytmp{ysl}"] + [f"T{i}" for i in range(4)] + ["st60"])

        try:
            for kind, t in (TILES if tiles is None else tiles):
                run_tile(kind, t)
            if tiles is None:
                assert wstate["i"] == len(chunks), (wstate, len(chunks))
        except _Stop:
            pass
        P.finalize(block)
    return nc


def _t5_bucket_static(n):
    import math
    try:
        import jax
        import jax.numpy as jnp
        with jax.default_device(jax.devices("cpu")[0]):
            nn = jnp.asarray(n, dtype=jnp.int32)
            half, max_exact = 16, 8
            offset = jnp.where(nn < 0, half, 0)
            a = jnp.abs(nn)
            af = jnp.maximum(a, 1).astype(jnp.float32)
            large = max_exact + (jnp.log(af / max_exact) / math.log(128 / max_exact) * (half - max_exact)).astype(jnp.int32)
            large = jnp.minimum(large, half - 1)
            return np.asarray(offset + jnp.where(a < max_exact, a, large))
    except Exception:
        nn = np.asarray(n, dtype=np.int32)
        half, max_exact = 16, 8
        offset = np.where(nn < 0, half, 0)
        a = np.abs(nn)
        af = np.maximum(a, 1).astype(np.float32)
        large = max_exact + (np.log(af / np.float32(max_exact)) / np.float32(math.log(128 / max_exact))
                             * np.float32(half - max_exact)).astype(np.int32)
        large = np.minimum(large, half - 1)
        return offset + np.where(a < max_exact, a, large)


_NC_CACHE = {}


def prep_inputs(x_prompt, x_sample, cache_attn_k, cache_attn_v, rel_bias_table, ln_mix_g, w_in,
                q_norm_g, k_norm_g, attn_sinks, sgu_norm_g, sgu_w, sgu_b, out_norm_attn_g,
                out_norm_sgu_g, w_out, ln_ffn_g, w_ffn_up, w_ffn_down):
    f = lambda a: np.ascontiguousarray(np.asarray(a, dtype=np.float32))
    x_prompt, x_sample = f(x_prompt), f(x_sample)
    hk, g, d = np.meshgrid(np.arange(2), np.arange(8), np.arange(64), indexing="ij")
    qcols = ((hk * 8 + g) * 64 + d).transpose(1, 0, 2).reshape(-1)
    perm = np.concatenate([qcols, np.arange(1024, 1280), np.arange(2304, 3328), np.arange(1280, 2304)])
    w_in_p = np.asarray(w_in)[0][:, perm]

    def img_cols(w, col0, wdt):
        return w[:, col0:col0 + wdt].reshape(16, 128, wdt).transpose(1, 0, 2).reshape(128, 16 * wdt)

    IN_COL0 = [0, 512, 1024, 1280, 1792, 2304, 2816]
    IN_W = [512, 512, 256, 512, 512, 512, 512]
    w_in_img = f(np.concatenate([img_cols(w_in_p, c0, wd) for c0, wd in zip(IN_COL0, IN_W)], axis=1))
    wo = np.asarray(w_out)[0]
    w_out_img = f(np.concatenate([img_cols(wo, c * 512, 512) for c in range(4)], axis=1))
    wu = np.asarray(w_ffn_up)[0]
    w_up_img = f(np.concatenate([img_cols(wu, c * 512, 512) for c in range(16)], axis=1))
    wd_ = np.asarray(w_ffn_down)[0]
    w_down_img = f(np.concatenate(
        [wd_[g * 1024:(g + 1) * 1024, r * 1024:(r + 1) * 1024].reshape(8, 128, 1024).transpose(1, 0, 2).reshape(128, 8192)
         for r in range(2) for g in range(8)], axis=1))
    m = np.arange(384)
    bucket = _t5_bucket_static(255 - m)
    oh = np.zeros((32, 384), np.float32)
    oh[bucket, m] = 1.0
    common = {
        "table": f(rel_bias_table), "oh": oh, "ident": np.eye(128, dtype=np.float32),
        "w_in": w_in_img, "w_out": w_out_img, "w_up": w_up_img, "w_down": w_down_img,
        "g_mix": f(ln_mix_g), "g_ffn": f(ln_ffn_g), "g_sgu": f(sgu_norm_g), "g_oa": f(out_norm_attn_g), "g_os": f(out_norm_sgu_g),
        "gq": f(np.asarray(q_norm_g)[0][None, :]), "gk": f(np.asarray(k_norm_g)[0][None, :]), "sinks": f(np.asarray(attn_sinks)[0].reshape(1, 16)),
        "wsT": f(np.asarray(sgu_w)[0].transpose(2, 0, 1)), "bsT": f(np.asarray(sgu_b)[0].T),
    }
    ck = np.asarray(cache_attn_k)[0]
    cvv = np.asarray(cache_attn_v)[0]
    in_maps = []
    for c in range(NCORES):
        mm = dict(common)
        mm["xp"] = x_prompt[c]
        mm["xs"] = f(x_sample[2 * c:2 * c + 2].reshape(128, D))
        mm["ckT"] = f(ck[2 * c:2 * c + 2].reshape(2, 128, 128).transpose(0, 2, 1))
        mm["cv"] = f(cvv[2 * c:2 * c + 2].reshape(2, 128, 128))
        in_maps.append(mm)
    return in_maps


def kernel(**inputs):
    in_maps = prep_inputs(**inputs)
    if "nc" not in _NC_CACHE:
        _NC_CACHE["nc"] = build_program()
    nc = _NC_CACHE["nc"]
    res = run_bass_kernel_spmd(nc, in_maps, core_ids=list(range(NCORES)))
    R = res.results
    y_prompt = np.stack([R[c]["yp"] for c in range(NCORES)]).astype(np.float32)
    y_sample = np.concatenate([R[c]["ys"].reshape(2, 64, D) for c in range(NCORES)]).astype(np.float32)
    kpo = np.stack([R[c]["kp"].reshape(128, 2, 64) for c in range(NCORES)])[None].astype(np.float32)
    vpo = np.stack([R[c]["vp"].reshape(128, 2, 64) for c in range(NCORES)])[None].astype(np.float32)
    kso = np.concatenate([R[c]["ks"].reshape(2, 64, 2, 64) for c in range(NCORES)])[None].astype(np.float32)
    vso = np.concatenate([R[c]["vs"].reshape(2, 64, 2, 64) for c in range(NCORES)])[None].astype(np.float32)
    sgo = np.concatenate([R[c]["sgv"].reshape(2, 64, 1024) for c in range(NCORES)])[None].astype(np.float32)
    return (y_prompt, y_sample, kpo, vpo, kso, vso, sgo)
```

```python
import numpy as np
import concourse.bass as bass
import concourse.mybir as mybir
from concourse.bass_utils import run_bass_kernel_spmd

F32 = mybir.dt.float32
BF16 = mybir.dt.bfloat16
AF = mybir.ActivationFunctionType
ALU = mybir.AluOpType
AX = mybir.AxisListType

D = 2048
DFF = 8192
NCORES = 8
EPS = 1e-6
NEGB = -30000.0
import os
SAMPLE_RING = int(os.environ.get("SAMPLE_RING", "6"))
ENGS = ("pe", "act", "dve", "pool", "sp")


class _Op:
    __slots__ = ("eng", "fn", "reads", "writes", "dma", "deps", "signal", "ev", "idx")

    def __init__(self, eng, fn, reads, writes, dma):
        self.eng, self.fn, self.reads, self.writes, self.dma = eng, fn, reads, writes, dma
        self.deps = set()
        self.signal = False
        self.ev = None


class Prog:
    def __init__(self, nc, same_engine_sync=True):
        self.nc = nc
        self.ops = []
        self.last_w = {}
        self.readers = {}
        self.same_engine_sync = same_engine_sync

    def _add(self, eng, fn, reads, writes, dma=None):
        op = _Op(eng, fn, tuple(reads), tuple(writes), dma)
        op.idx = len(self.ops)
        for r in op.reads:
            w = self.last_w.get(r)
            if w is not None:
                op.deps.add(w)
        for w_ in op.writes:
            w = self.last_w.get(w_)
            if w is not None:
                op.deps.add(w)
            latest = {}
            for rd in self.readers.get(w_, ()):
                ro = self.ops[rd]
                if ro.dma is not None:
                    op.deps.add(rd)
                else:
                    latest[ro.eng] = max(latest.get(ro.eng, -1), rd)
            op.deps.update(latest.values())
        for r in op.reads:
            self.readers.setdefault(r, []).append(op.idx)
        for w_ in op.writes:
            self.last_w[w_] = op.idx
            self.readers[w_] = []
        op.deps.discard(op.idx)
        self.ops.append(op)
        return op

    def op(self, eng, fn, reads=(), writes=()):
        return self._add(eng, fn, reads, writes)

    def dma(self, eng, sem_name, fn, reads=(), writes=()):
        return self._add(eng, fn, reads, writes, dma=sem_name)

    def finalize(self, block):
        import os
        nc = self.nc
        ops = self.ops
        nmax = int(os.environ.get("BISECT_N", "0"))
        if nmax:
            ops = ops[:nmax]
            for i, o in enumerate(ops[-3:]):
                print("last ops:", o.idx, o.eng, o.reads, o.writes, o.dma)
        print("n_ops", len(ops))
        for op in ops:
            for d in op.deps:
                p = ops[d]
                if p.dma is not None:
                    continue
                if p.eng == op.eng and (p.eng == "pe" or not self.same_engine_sync):
                    continue
                p.signal = True
        eng_sem = {e: nc.alloc_semaphore("sem_" + e) for e in ENGS}
        self.all_sems = list(eng_sem.values())
        cnt = {e: 0 for e in ENGS}
        dsem, dcnt = {}, {}
        for op in ops:
            if op.dma is not None:
                if op.dma not in dsem:
                    dsem[op.dma] = nc.alloc_semaphore("dsem_" + op.dma)
                    self.all_sems.append(dsem[op.dma])
                    dcnt[op.dma] = 0
                dcnt[op.dma] += 16
                op.ev = (op.dma, dcnt[op.dma])
            elif op.signal:
                cnt[op.eng] += 1
                op.ev = (op.eng, cnt[op.eng])
        issued = {k: 0 for k in dsem}
        waits_for = []
        for op in ops:
            w = {}
            for d in op.deps:
                p = ops[d]
                if p.dma is not None:
                    w[("d", p.dma)] = max(w.get(("d", p.dma), 0), issued[p.dma])
                elif p.ev is not None:
                    w[("e", p.eng)] = max(w.get(("e", p.eng), 0), p.ev[1])
            waits_for.append(w)
            if op.dma is not None:
                issued[op.dma] += 16
        final = dict(dcnt)

        def semof(k):
            return dsem[k[1]] if k[0] == "d" else eng_sem[k[1]]

        def emit(engname, engobj):
            waited = {}
            for op, w in zip(ops, waits_for):
                if op.eng != engname:
                    continue
                for k, v in w.items():
                    if waited.get(k, 0) >= v:
                        continue
                    engobj.wait_ge(semof(k), v)
                    waited[k] = v
                inst = op.fn(engobj)
                if op.dma is not None:
                    inst.then_inc(dsem[op.dma], 16)
                elif op.signal:
                    inst.then_inc(eng_sem[op.eng], 1)
            if engname == "sp":
                for k, v in final.items():
                    engobj.wait_ge(dsem[k], v)

        for sm in self.all_sems:
            nc.sync.sem_clear(sm)
        nc.all_engine_barrier()
        block = nc.Block().__enter__()
        self._block = block

        @block.tensor
        def _(e):
            emit("pe", e)

        @block.scalar
        def _(e):
            emit("act", e)

        @block.vector
        def _(e):
            emit("dve", e)

        @block.gpsimd
        def _(e):
            emit("pool", e)

        @block.sync
        def _(e):
            emit("sp", e)

        block.__exit__(None, None, None)


class _Stop(Exception):
    pass


def build_program(stage=99, tiles=None):
    nc = bass.Bass("TRN2", target_bir_lowering=False)

    def din(name, shape):
        return nc.dram_tensor(name, list(shape), F32, kind="ExternalInput")

    def dout(name, shape):
        return nc.dram_tensor(name, list(shape), F32, kind="ExternalOutput")

    xp = din("xp", (2048, D))
    xs = din("xs", (128, D))
    ckT = din("ckT", (2, 128, 128))
    cv = din("cv", (2, 128, 128))
    table = din("table", (32, 16))
    oh = din("oh", (32, 384))
    ident = din("ident", (128, 128))
    w_in = din("w_in", (128, 16 * 3328))
    w_out = din("w_out", (128, 4 * 8192))
    w_up = din("w_up", (128, 16 * 8192))
    w_down = din("w_down", (128, 16 * 8192))
    wimg = {"in": w_in, "out": w_out, "up": w_up, "down": w_down}
    wscr = {k: nc.dram_tensor("scr_" + k, list(v.shape), BF16, kind="ExternalOutput") for k, v in wimg.items()}
    g_mix_d = din("g_mix", (1, D))
    g_ffn_d = din("g_ffn", (1, D))
    g_sgu_d = din("g_sgu", (1, 1024))
    g_oa_d = din("g_oa", (1, 1024))
    g_os_d = din("g_os", (1, 1024))
    gq_d = din("gq", (1, 64))
    gk_d = din("gk", (1, 64))
    sinks_d = din("sinks", (1, 16))
    wsT_d = din("wsT", (128, 8, 128))
    bsT_d = din("bsT", (128, 8))
    scr = nc.dram_tensor("scr", [16, 384], F32, kind="Internal")

    yp = dout("yp", (2048, D))
    ys = dout("ys", (128, D))
    kp_o = dout("kp", (128, 128))
    vp_o = dout("vp", (128, 128))
    ks_o = dout("ks", (128, 128))
    vs_o = dout("vs", (128, 128))
    sgv_o = dout("sgv", (128, 1024))

    def sb(name, shape, dt):
        return nc.alloc_sbuf_tensor(name, list(shape), dt)

    identb = sb("identb", (128, 128), BF16)
    g_mix = sb("g_mix_s", (128, D), F32)
    g_ffn = sb("g_ffn_s", (128, D), F32)
    g_sgu = sb("g_sgu_s", (128, 1024), F32)
    g_oa = sb("g_oa_s", (128, 1024), F32)
    g_os = sb("g_os_s", (128, 1024), F32)
    gq_t = sb("gq_s", (128, 64), F32)
    gk_t = sb("gk_s", (128, 64), F32)
    esink = sb("esink", (128, 16), F32)
    epsb = sb("epsb", (128, 1), F32)
    onesb = sb("onesb", (128, 2), BF16)
    WsT = sb("WsT", (128, 8, 128), BF16)
    bsT = sb("bsT_s", (128, 8), F32)
    bsTs = sb("bsTs", (128, 8), F32)
    biasP = sb("biasP", (128, 16, 128), F32)
    biasO = sb("biasO", (128, 16, 128), F32)
    NSLOT = 5
    kTz = [[sb(f"kTz{hk}_{s}", (128, 128), BF16) for s in range(NSLOT)] for hk in range(2)]
    Vb = [sb(f"Vb{s}", (128, 128), BF16) for s in range(NSLOT)]
    qT = sb("qT", (128, 8, 128), BF16)
    stat = sb("stat", (128, 64), F32)
    tmpR = [sb(f"tmpR{i}", (128, 512), F32) for i in range(2)]
    X1 = sb("X1", (128, 4, D), F32)
    Hr = sb("Hr", (128, 16384), F32)
    Hrb = Hr[:, :].bitcast(BF16)
    Tr = sb("Tr", (128, 16, 512), BF16)
    Wr = [sb(f"Wr{i}", (128, 8192), BF16) for i in range(2)]
    ps = nc.alloc_psum_tensor("ps", [128, 4096], F32)

    def bank(i, n=512):
        return ps[:, i * 512:i * 512 + n]

    ptr = ps[:, 0:1024].bitcast(BF16)
    ptr2 = ps[:, 3072:4096].bitcast(BF16)
    tstate = {"n": 0}

    class HAlloc:
        def __init__(self):
            self.off = 0

        def take(self, nbytes, dt, shape=None):
            start = (self.off + 1023) // 1024 * 1024
            self.off = start + nbytes
            assert self.off <= 65536, self.off
            ap = Hr[:, start // 4:(start + nbytes) // 4]
            if dt == BF16:
                ap = Hrb[:, start // 2:(start + nbytes) // 2]
            names = [f"H{i}" for i in range(start // 1024, (start + nbytes + 1023) // 1024)]
            return ap, names

    ha = HAlloc()
    zqk, zqk_n = [], []
    for b in range(4):
        a, n = ha.take(1152 * 4, F32)
        zqk.append(a)
        zqk_n.append(n)
    tmpq, tmpq_n = ha.take(1152 * 4, F32)
    qknb, qknb_n = ha.take(1152 * 2, BF16)
    tmpS, tmpS_n = [], []
    for i in range(2):
        a, n = ha.take(512 * 4, F32)
        tmpS.append(a)
        tmpS_n.append(n)
    PT, PT_n = {}, {}
    for hk in range(2):
        for kt in range(3):
            a, n = ha.take(1024 * 2, BF16)
            PT[(hk, kt)] = a
            PT_n[(hk, kt)] = n
    o32, o32_n = ha.take(1024 * 4, F32)
    vnb, vnb_n = ha.take(1024 * 2, BF16)
    xtmp, xtmp_n = ha.take(D * 4, F32)
    hank, hank_n = xtmp, xtmp_n
    mixb, mixb_n = ha.take(D * 2, BF16)
    mixb_lo_n, mixb_hi_n = mixb_n[:2], mixb_n[2:]
    assert len(mixb_n) == 4
    h_used = ha.off
    tbl = tmpR[1][0:32, 384:400]
    ohs = tmpR[0][0:32, 0:384]
    srow = tmpR[1][0:16, 0:384]
    kn32 = o32[:, 0:128]
    v32 = o32[:, 128:256]
    zq1_start = 5 * 1024
    biasPB = Hr[:, zq1_start // 4:(zq1_start + 8192) // 4]
    wsts_start = 15 * 1024
    WsTs = Hrb[:, wsts_start // 2:(wsts_start + 2048) // 2].rearrange("p (g i) -> p g i", g=8)
    WsTs_n = ["H15", "H16"]
    biasPB_n = [f"H{i}" for i in range(zq1_start // 1024, zq1_start // 1024 + 8)]
    ALLH = [f"H{i}" for i in range(64)]

    hstate = {"compact": False}

    def hid(f, t0=0, t1=512):
        if hstate["compact"]:
            return Hrb[:, f * 128 + t0:f * 128 + t1]
        return Hrb[:, f * 512 + t0:f * 512 + t1]

    def hid_n(f):
        return f"H{f // 4}" if hstate["compact"] else f"H{f}"

    P = Prog(nc)
    _cst = {"n": 0}
    _orig_dma = P.dma

    def _dma(eng, sem_name, fn, reads=(), writes=()):
        if sem_name == "cst":
            _cst["n"] += 1
            sem_name = f"cst{_cst['n']}"
        return _orig_dma(eng, sem_name, fn, reads=reads, writes=writes)

    P.dma = _dma

    TILES = [("p", t) for t in range(4)] + [("s", 0)]
    chunks = []
    for _ in (TILES if tiles is None else tiles):
        chunks += [("in", c) for c in range(7)]
        chunks += [("out", c) for c in range(4)]
        chunks += [("up", c) for c in range(16)]
        chunks += [("down", r, g) for r in range(2) for g in range(8)]
    IN_COL0 = [0, 512, 1024, 1280, 1792, 2304, 2816]
    IN_W = [512, 512, 256, 512, 512, 512, 512]
    wstate = {"i": 0, "issued": 0}

    NCH = 43
    SLOT_AP = [Wr[0][:, :], Wr[1][:, :], X1[:, 1:3, :].rearrange("p b n -> p (b n)").bitcast(BF16),
               Hrb[:, 8192:16384], Hrb[:, 16384:24576], Hrb[:, 24576:32768]]
    SLOT_N = [["w0"], ["w1"], ["X1_1_lo", "X1_1_hi", "X1_2_lo", "X1_2_hi"],
              [f"H{i}" for i in range(16, 32)], [f"H{i}" for i in range(32, 48)], [f"H{i}" for i in range(48, 64)]]
    n_tiles_run = len(TILES if tiles is None else tiles)
    tile_kinds = [k for k, _ in (TILES if tiles is None else tiles)]

    def slot_of(i):
        j = i % NCH
        if tile_kinds[i // NCH] == "s":
            if SAMPLE_RING <= 2:
                return i % 2
            return [0, 1, 2][j % 3] if j < 11 else [2, 0, 1, 3, 4, 5][(j - 11) % 6]
        return i % 2

    def ahead_of(i):
        j = i % NCH
        if tile_kinds[i // NCH] == "s":
            if SAMPLE_RING <= 2:
                return 1
            return 2 if j < 11 else 5
        return 1

    slot_last = {}

    def issue_chunk(i):
        c = chunks[i]
        s = slot_of(i)
        assert slot_last.get(s, -1) < wstate["i"], (i, s, slot_last.get(s), wstate["i"])
        slot_last[s] = i
        tile_i, j = i // NCH, i % NCH
        multi = n_tiles_run >= 3
        if not multi:
            cast, wback = tile_i == 0, tile_i == 0
        elif tile_i == 0:
            cast, wback = True, (j % 2 == 0)
        elif tile_i == 1:
            cast, wback = (j % 2 == 1), (j % 2 == 1)
        else:
            cast, wback = False, False
        kind = c[0]
        if kind == "in":
            off, ln = 16 * IN_COL0[c[1]], 16 * IN_W[c[1]]
        elif kind == "down":
            off, ln = (c[1] * 8 + c[2]) * 8192, 8192
        else:
            off, ln = c[1] * 8192, 8192
        d = SLOT_AP[s][:, 0:ln]
        rname = f"scr_{kind}_{off}"
        if cast:
            src = wimg[kind].ap()[:, off:off + ln]
            P.dma("pool", f"w{s}q", lambda e, d=d, src=src: e.dma_start(out=d, in_=src), writes=SLOT_N[s])
            if wback:
                dsts = wscr[kind].ap()[:, off:off + ln]
                P.dma("sp", "wst", lambda e, d=d, dsts=dsts: e.dma_start(out=dsts, in_=d), reads=SLOT_N[s], writes=[rname])
        else:
            src = wscr[kind].ap()[:, off:off + ln]
            P.dma("sp", f"w{s}", lambda e, d=d, src=src: e.dma_start(out=d, in_=src), reads=[rname], writes=SLOT_N[s])

    def next_chunk(kind):
        i = wstate["i"]
        assert chunks[i][0] == kind, (chunks[i], kind)
        while wstate["issued"] <= min(i + ahead_of(i), len(chunks) - 1):
            nxt = wstate["issued"]
            if nxt // NCH != i // NCH and nxt > i + 1:
                break
            issue_chunk(nxt)
            wstate["issued"] += 1
        wstate["i"] += 1
        return slot_of(i)

    def rstd_of(ss_col, r_col, inv_n, reads, tag):
        P.op("act", lambda e: e.activation(out=r_col, in_=ss_col, func=AF.Sqrt, scale=inv_n, bias=epsb[:]),
             reads=reads + ["epsb"], writes=[tag + "_r"])
        P.op("dve", lambda e: e.reciprocal(out=r_col, in_=r_col), reads=[tag + "_r"], writes=[tag + "_r"])

    def transposes_to(src_tile, nch, dst_ap, src_names, dst_names, evac_eng="act"):
        k = tstate["n"] % 2
        tstate["n"] += 1
        pt = [ptr, ptr2][k]
        pn = [["ps0", "ps1"], ["ps6", "ps7"]][k]
        for c in range(nch):
            P.op("pe", lambda e, c=c, pt=pt: e.matmul(pt[:, c * 128:(c + 1) * 128], lhsT=src_tile[:, c * 128:(c + 1) * 128],
                                                    rhs=identb[:], start=True, stop=True, is_transpose=True),
                 reads=src_names + ["identb"], writes=pn)
        src = pt[:, 0:nch * 128].rearrange("p (c t) -> p c t", c=nch)
        if evac_eng == "act":
            P.op("act", lambda e: e.copy(out=dst_ap, in_=src), reads=pn, writes=dst_names)
        else:
            P.op("dve", lambda e: e.tensor_copy(out=dst_ap, in_=src), reads=pn, writes=dst_names)

    def toeplitz(dst, dst_names, c0):
        hk_ = hank.rearrange("p (h q) -> p h q", h=16)
        P.dma("sp", "cst", lambda e: e.dma_start(out=hk_, in_=bass.AP(tensor=scr, offset=c0, ap=[[1, 128], [384, 16], [1, 128]])),
              reads=["scr"], writes=hank_n)
        t = hank
        rev = bass.AP(tensor=t.tensor, offset=t.offset + 127, ap=[list(t.ap[0]), [128, 16], [-1, 128]])
        P.op("dve", lambda e: e.tensor_copy(out=dst, in_=rev), reads=hank_n, writes=dst_names)

    block = None
    if True:
        P.op("dve", lambda e: e.memset(epsb[:], EPS), writes=["epsb"])
        P.op("dve", lambda e: e.memset(onesb[:], 1.0), writes=["onesb"])
        for hk in range(2):
            for s in range(NSLOT):
                P.op("dve", lambda e, hk=hk, s=s: e.memset(kTz[hk][s][:], 0.0), writes=[f"kTz{hk}_{s}"])

        def bc_load(dst, src, n, name):
            P.dma("sp", "cst", lambda e: e.dma_start(out=dst[:], in_=src.ap().partition_broadcast(128)[:, 0, :]), writes=[name])

        idf = xtmp[:, 0:128]
        P.dma("sp", "cst", lambda e: e.dma_start(out=idf, in_=ident.ap()), writes=xtmp_n)
        P.op("dve", lambda e: e.tensor_copy(out=identb[:], in_=idf), reads=xtmp_n, writes=["identb"])
        bc_load(g_mix, g_mix_d, D, "g_mix")
        P.dma("sp", "cst", lambda e: e.dma_start(out=tbl, in_=table.ap()), writes=["tmpR1"])
        P.dma("sp", "cst", lambda e: e.dma_start(out=ohs, in_=oh.ap()), writes=["tmpR0"])

    def late_setup():
        bc_load(g_ffn, g_ffn_d, D, "g_ffn")
        bc_load(g_sgu, g_sgu_d, 1024, "g_sgu")
        bc_load(g_oa, g_oa_d, 1024, "g_oa")
        bc_load(g_os, g_os_d, 1024, "g_os")
        bc_load(gq_t, gq_d, 64, "gq")
        bc_load(gk_t, gk_d, 64, "gk")
        bc_load(esink, sinks_d, 16, "esink")
        P.op("act", lambda e: e.activation(out=esink[:], in_=esink[:], func=AF.Exp), reads=["esink"], writes=["esink"])
        wst = hank.rearrange("p (g i) -> p g i", g=16)[:, 0:8, :]
        P.dma("sp", "cst", lambda e: e.dma_start(out=wst, in_=wsT_d.ap()), writes=hank_n)
        P.op("dve", lambda e: e.tensor_copy(out=WsT[:], in_=wst), reads=hank_n, writes=["WsT"])
        P.op("dve", lambda e: e.memset(WsT[64:128, :, 0:64], 0.0), reads=[], writes=["WsT"])
        P.dma("sp", "cst", lambda e: e.dma_start(out=bsT[:], in_=bsT_d.ap()), writes=["bsT"])
        P.dma("sp", "cst", lambda e: e.dma_start(out=bsTs[0:64, :], in_=bsT_d.ap()[0:64, :]), writes=["bsTs"])
        P.dma("sp", "cst", lambda e: e.dma_start(out=bsTs[64:128, :], in_=bsT_d.ap()[0:64, :]), writes=["bsTs"])
        P.op("pe", lambda e: e.matmul(bank(2)[0:16, 0:384], lhsT=tbl, rhs=ohs, start=True, stop=True),
             reads=["tmpR0", "tmpR1"], writes=["ps2"])
        P.op("dve", lambda e: e.tensor_copy(out=srow, in_=bank(2)[0:16, 0:384]), reads=["ps2"], writes=["tmpR1"])
        P.dma("sp", "cst", lambda e: e.dma_start(out=scr.ap(), in_=srow), reads=["tmpR1"], writes=["scr"])
        toeplitz(biasP[:], ["biasP"], 0)
        P.op("dve", lambda e: e.memset(biasP[0:64, :, 64:128], NEGB), writes=["biasP"])
        toeplitz(biasO[:], ["biasO"], 128)
        P.op("dve", lambda e: e.memset(biasO[64:128, :, 0:64], NEGB), writes=["biasO"])

    late_done = {"v": False}
    if True:
        def ckpt(n):
            if stage <= n:
                raise _Stop()

        def run_tile(kind, t):
            sample = kind == "s"
            hstate["compact"] = sample and os.environ.get("NO_COMPACT", "0") != "1"
            nb = 1 if sample else 4
            ntok = nb * 128
            gB = [16] if sample else [4 * t + i for i in range(4)]
            xsrc = (lambda bi: xs.ap()) if sample else (lambda bi: xp.ap()[(4 * t + bi) * 128:(4 * t + bi + 1) * 128, :])

            if sample and not late_done["v"]:
                late_done["v"] = True
                late_setup()
            if sample:
                P.op("dve", lambda e: e.memset(biasP[:, :, 64:128], NEGB), writes=["biasP"])
                P.op("dve", lambda e: e.memset(biasO[0:64, :, 64:128], NEGB), writes=["biasO"])
                toeplitz(biasPB.rearrange("p (h q) -> p h q", h=16), biasPB_n, 64)
                P.op("dve", lambda e: e.memset(biasPB.rearrange("p (h q) -> p h q", h=16)[:, :, 0:64], NEGB), writes=biasPB_n)
                wstg = xtmp.rearrange("p (g i) -> p g i", g=16)[:, 0:8, :]
                P.dma("pool", "xld", lambda e: e.dma_start(out=wstg[0:64, :, 0:64], in_=wsT_d.ap()[0:64, :, 0:64]), writes=xtmp_n)
                P.dma("pool", "xld", lambda e: e.dma_start(out=wstg[64:128, :, 64:128], in_=wsT_d.ap()[0:64, :, 0:64]), writes=xtmp_n)
                P.op("dve", lambda e: e.memset(WsTs, 0.0), writes=WsTs_n)
                P.op("dve", lambda e: e.tensor_copy(out=WsTs[0:64, :, 0:64], in_=wstg[0:64, :, 0:64]), reads=xtmp_n, writes=WsTs_n)
                P.op("dve", lambda e: e.tensor_copy(out=WsTs[64:128, :, 64:128], in_=wstg[64:128, :, 64:128]), reads=xtmp_n, writes=WsTs_n)
                for sq in range(2):
                    slot = 2 + sq
                    st_ = xtmp[:, sq * 256:sq * 256 + 128]
                    sv_ = xtmp[:, sq * 256 + 128:sq * 256 + 256]
                    P.dma("pool", "xld", lambda e, st_=st_, sq=sq: e.dma_start(out=st_, in_=ckT.ap()[sq]), writes=xtmp_n)
                    P.dma("pool", "xld", lambda e, sv_=sv_, sq=sq: e.dma_start(out=sv_, in_=cv.ap()[sq]), writes=xtmp_n)
                    P.op("dve", lambda e, st_=st_, slot=slot: e.tensor_copy(out=kTz[0][slot][0:64, :], in_=st_[0:64, :]),
                         reads=xtmp_n, writes=[f"kTz0_{slot}"])
                    P.op("dve", lambda e, st_=st_, slot=slot: e.tensor_copy(out=kTz[1][slot][64:128, :], in_=st_[64:128, :]),
                         reads=xtmp_n, writes=[f"kTz1_{slot}"])
                    P.op("dve", lambda e, sv_=sv_, slot=slot: e.tensor_copy(out=Vb[slot][:], in_=sv_), reads=xtmp_n, writes=[f"Vb{slot}"])

            ckpt(0)
            for bi in range(nb):
                P.dma("pool", f"xld{bi}", lambda e, bi=bi: e.dma_start(out=X1[:, bi, :], in_=xsrc(bi)),
                      writes=[f"X1_{bi}_lo", f"X1_{bi}_hi"])
            def a1_elem(bi):
                zq3 = zqk[bi].rearrange("p (h d) -> p h d", d=64)
                tq3 = tmpq.rearrange("p (h d) -> p h d", d=64)
                P.op("dve", lambda e, bi=bi: e.tensor_tensor(out=tmpq, in0=zqk[bi], in1=zqk[bi], op=ALU.mult),
                     reads=zqk_n[bi], writes=tmpq_n)
                P.op("dve", lambda e: e.tensor_reduce(out=stat[:, 8:26], in_=tq3, op=ALU.add, axis=AX.X),
                     reads=tmpq_n, writes=["qk_r"])
                rstd_of(stat[:, 8:26], stat[:, 8:26], 1.0 / 64, ["qk_r"], "qk")
                P.op("dve", lambda e, zq3=zq3: e.tensor_tensor(out=tq3, in0=zq3, in1=stat[:, 8:26].unsqueeze(2).to_broadcast([128, 18, 64]),
                                                             op=ALU.mult),
                     reads=zqk_n[bi] + ["qk_r"], writes=tmpq_n)
                P.op("dve", lambda e: e.tensor_tensor(out=qknb[:, 0:1024].rearrange("p (h d) -> p h d", d=64), in0=tq3[:, 0:16, :],
                                                      in1=gq_t[:].unsqueeze(1).to_broadcast([128, 16, 64]), op=ALU.mult),
                     reads=tmpq_n + ["gq"], writes=qknb_n)
                P.op("dve", lambda e: e.tensor_tensor(out=qknb[:, 1024:1152].rearrange("p (h d) -> p h d", d=64), in0=tq3[:, 16:18, :],
                                                      in1=gk_t[:].unsqueeze(1).to_broadcast([128, 2, 64]), op=ALU.mult),
                     reads=tmpq_n + ["gk"], writes=qknb_n)
                if sample or (t == 3 and bi == 3):
                    P.op("dve", lambda e: e.tensor_tensor(out=kn32.rearrange("p (h d) -> p h d", d=64), in0=tq3[:, 16:18, :],
                                                          in1=gk_t[:].unsqueeze(1).to_broadcast([128, 2, 64]), op=ALU.mult),
                         reads=tmpq_n + ["gk"], writes=o32_n)
                    ko = ks_o if sample else kp_o
                    P.dma("pool", "ost", lambda e, ko=ko: e.dma_start(out=ko.ap(), in_=kn32), reads=o32_n)

            def a1_pe(bi):
                slot = gB[bi] % NSLOT
                for c in range(9):
                    P.op("pe", lambda e, c=c: e.matmul(ptr[:, c * 128:(c + 1) * 128], lhsT=qknb[:, c * 128:(c + 1) * 128],
                                                     rhs=identb[:], start=True, stop=True, is_transpose=True),
                         reads=qknb_n + ["identb"], writes=["ps0", "ps1"])
                P.op("act", lambda e: e.copy(out=qT[:], in_=ptr[:, 0:1024].rearrange("p (c t) -> p c t", c=8)),
                     reads=["ps0", "ps1"], writes=["qT"])
                P.op("dve", lambda e, slot=slot: e.tensor_copy(out=kTz[0][slot][0:64, :], in_=ptr[0:64, 1024:1152]),
                     reads=["ps0", "ps1"], writes=[f"kTz0_{slot}"])
                P.op("dve", lambda e, slot=slot: e.tensor_copy(out=kTz[1][slot][64:128, :], in_=ptr[64:128, 1024:1152]),
                     reads=["ps0", "ps1"], writes=[f"kTz1_{slot}"])

            def z1_unit(c, s, bi, pb):
                wdt = IN_W[c]
                wv = SLOT_AP[s][:, 0:16 * wdt].rearrange("p (c n) -> p c n", c=16)
                for dc in range(16):
                    P.op("pe", lambda e, pb=pb, dc=dc, bi=bi, wv=wv, wdt=wdt: e.matmul(
                        bank(pb, wdt), lhsT=Tr[:, dc, bi * 128:(bi + 1) * 128], rhs=wv[:, dc, :], start=(dc == 0), stop=(dc == 15)),
                        reads=[f"T{bi}"] + SLOT_N[s], writes=[f"ps{pb}"])
                slot = gB[bi] % NSLOT
                if c < 2:
                    P.op("dve", lambda e, pb=pb, bi=bi, c=c: e.tensor_copy(out=zqk[bi][:, c * 512:(c + 1) * 512], in_=bank(pb)),
                         reads=[f"ps{pb}"], writes=zqk_n[bi])
                elif c == 2:
                    P.op("dve", lambda e, pb=pb, bi=bi: e.tensor_copy(out=zqk[bi][:, 1024:1152], in_=bank(pb, 128)),
                         reads=[f"ps{pb}"], writes=zqk_n[bi])
                    P.op("dve", lambda e, pb=pb, slot=slot: e.tensor_copy(out=Vb[slot][:], in_=ps[:, pb * 512 + 128:pb * 512 + 256]),
                         reads=[f"ps{pb}"], writes=[f"Vb{slot}"])
                    if sample or (t == 3 and bi == 3):
                        P.op("dve", lambda e, pb=pb: e.tensor_copy(out=v32, in_=ps[:, pb * 512 + 128:pb * 512 + 256]),
                             reads=[f"ps{pb}"], writes=o32_n)
                        vo = vs_o if sample else vp_o
                        P.dma("pool", "ost", lambda e, vo=vo: e.dma_start(out=vo.ap(), in_=v32), reads=o32_n)
                else:
                    half = (c - 3) % 2
                    which = "lo" if c < 5 else "hi"
                    col0 = (0 if c < 5 else 1024) + half * 512
                    P.op("act", lambda e, pb=pb, bi=bi, col0=col0: e.activation(out=X1[:, bi, col0:col0 + 512], in_=bank(pb),
                                                                               func=AF.Gelu_apprx_tanh),
                         reads=[f"ps{pb}"], writes=[f"X1_{bi}_{which}"])


            s0 = next_chunk("in")
            for bi in range(nb):
                x1n = [f"X1_{bi}_lo", f"X1_{bi}_hi"]
                P.op("act", lambda e, bi=bi: e.activation(out=mixb, in_=X1[:, bi, :], func=AF.Square, accum_out=stat[:, 0:1]),
                     reads=x1n, writes=mixb_n + ["st0"])
                rstd_of(stat[:, 0:1], stat[:, 1:2], 1.0 / D, ["st0"], "n1")
                P.op("dve", lambda e, bi=bi: e.scalar_tensor_tensor(out=mixb, in0=X1[:, bi, :], scalar=stat[:, 1:2], in1=g_mix[:],
                                                                    op0=ALU.mult, op1=ALU.mult),
                     reads=x1n + ["n1_r", "g_mix"], writes=mixb_n)
                transposes_to(mixb, 16, Tr[:, :, bi * 128:(bi + 1) * 128], mixb_n, [f"T{bi}"])
                if bi >= 1:
                    z1_unit(0, s0, bi - 1, 2 + (bi - 1) % 4)
            z1_unit(0, s0, nb - 1, 2 + (nb - 1) % 4)
            if not late_done["v"]:
                late_done["v"] = True
                late_setup()
            ckpt(1)
            for c in range(1, 7):
                s = next_chunk("in")
                for bi in range(nb):
                    z1_unit(c, s, bi, 2 + (c * nb + bi) % 4)
                if c == 2:
                    a1_elem(0)
                if c == 3:
                    a1_pe(0)

            ckpt(2)
            o_early = {"s": None, "done": set()}

            def o_unit(c, s, bi, pb):
                wv = SLOT_AP[s].rearrange("p (c n) -> p c n", c=16)
                for kc in range(16):
                    P.op("pe", lambda e, pb=pb, kc=kc, bi=bi, wv=wv: e.matmul(
                        bank(pb), lhsT=Tr[:, kc, bi * 128:(bi + 1) * 128], rhs=wv[:, kc, :], start=(kc == 0), stop=(kc == 15)),
                        reads=[f"T{bi}"] + SLOT_N[s], writes=[f"ps{pb}"])
                which = "lo" if c < 2 else "hi"
                P.op("dve", lambda e, pb=pb, bi=bi, c=c: e.tensor_tensor(out=X1[:, bi, c * 512:(c + 1) * 512], in0=bank(pb),
                                                                       in1=X1[:, bi, c * 512:(c + 1) * 512], op=ALU.add),
                     reads=[f"ps{pb}", f"X1_{bi}_{which}"], writes=[f"X1_{bi}_{which}"])

            for bi in range(nb):
                B = gB[bi]
                slot = B % NSLOT
                if bi > 0:
                    a1_elem(bi)
                    a1_pe(bi)
                if sample:
                    kts = [(2, biasP[:], ["biasP"]), (3, biasPB.rearrange("p (h q) -> p h q", h=16), biasPB_n), (slot, biasO[:], ["biasO"])]
                elif B == 0:
                    kts = [(slot, biasO[:], ["biasO"])]
                else:
                    kts = [((B - 1) % NSLOT, biasP[:], ["biasP"]), (slot, biasO[:], ["biasO"])]
                zv = X1[:, bi, 0:1024]
                uu = X1[:, bi, 1024:2048]
                P.op("act", lambda e, zv=zv: e.activation(out=vnb, in_=zv, func=AF.Square, accum_out=stat[:, 4:5]),
                     reads=[f"X1_{bi}_lo"], writes=vnb_n + ["st4"])
                rstd_of(stat[:, 4:5], stat[:, 5:6], 1.0 / 1024, ["st4"], "sv")
                if sample:
                    P.op("dve", lambda e, zv=zv: e.scalar_tensor_tensor(out=o32, in0=zv, scalar=stat[:, 5:6], in1=g_sgu[:],
                                                                        op0=ALU.mult, op1=ALU.mult),
                         reads=[f"X1_{bi}_lo", "sv_r", "g_sgu"], writes=o32_n)
                    P.dma("pool", "ost", lambda e: e.dma_start(out=sgv_o.ap(), in_=o32), reads=o32_n)
                P.op("dve", lambda e, zv=zv: e.scalar_tensor_tensor(out=vnb, in0=zv, scalar=stat[:, 5:6], in1=g_sgu[:],
                                                                    op0=ALU.mult, op1=ALU.mult),
                     reads=[f"X1_{bi}_lo", "sv_r", "g_sgu"], writes=vnb_n)
                Wg = WsTs if sample else WsT
                Wgn = WsTs_n if sample else ["WsT"]
                bg = bsTs if sample else bsT
                bgn = "bsTs" if sample else "bsT"
                cnt = 0
                for hk in range(2):
                    for ki, (ks_, bt, btn) in enumerate(kts):
                        for half in range(2):
                            pb = 4 + cnt % 2
                            ts_ = cnt % 2
                            cnt += 1
                            P.op("pe", lambda e, pb=pb, hk=hk, ks_=ks_, half=half: e.matmul(
                                bank(pb), lhsT=kTz[hk][ks_][:], rhs=qT[:, 4 * half:4 * half + 4, :], start=True, stop=True),
                                reads=[f"kTz{hk}_{ks_}", "qT"], writes=[f"ps{pb}"])
                            h0 = hk * 8 + 4 * half
                            P.op("dve", lambda e, pb=pb, ts_=ts_, bt=bt, h0=h0: e.scalar_tensor_tensor(
                                out=tmpS[ts_], in0=bank(pb), scalar=0.125, in1=bt[:, h0:h0 + 4, :].rearrange("p h q -> p (h q)"),
                                op0=ALU.mult, op1=ALU.add),
                                reads=[f"ps{pb}"] + btn, writes=tmpS_n[ts_])
                            P.op("act", lambda e, ts_=ts_, hk=hk, ki=ki, half=half: e.activation(
                                out=PT[(hk, ki)][:, half * 512:(half + 1) * 512], in_=tmpS[ts_], func=AF.Exp),
                                reads=tmpS_n[ts_], writes=PT_n[(hk, ki)])
                for g in range(8):
                    P.op("pe", lambda e, g=g, Wg=Wg: e.matmul(ps[:, (3 - g // 4) * 512 + (g % 4) * 128:(3 - g // 4) * 512 + (g % 4 + 1) * 128], lhsT=Wg[:, g, :],
                                                            rhs=vnb[:, g * 128:(g + 1) * 128], start=True, stop=True),
                         reads=Wgn + vnb_n, writes=[f"ps{3 - g // 4}"])
                for g in range(8):
                    P.op("dve", lambda e, g=g, bg=bg, bi=bi: e.scalar_tensor_tensor(
                        out=X1[:, bi, g * 128:(g + 1) * 128], in0=ps[:, (3 - g // 4) * 512 + (g % 4) * 128:(3 - g // 4) * 512 + (g % 4 + 1) * 128], scalar=bg[:, g:g + 1],
                        in1=X1[:, bi, 1024 + g * 128:1024 + (g + 1) * 128], op0=ALU.add, op1=ALU.mult),
                        reads=[f"ps{3 - g // 4}", bgn, f"X1_{bi}_hi"], writes=[f"X1_{bi}_lo"])
                P.op("act", lambda e, zv=zv: e.activation(out=mixb[:, 1024:2048], in_=zv, func=AF.Square, accum_out=stat[:, 6:7]),
                     reads=[f"X1_{bi}_lo"], writes=mixb_hi_n + ["st6"])
                rstd_of(stat[:, 6:7], stat[:, 7:8], 1.0 / 1024, ["st6"], "os")
                P.op("dve", lambda e, zv=zv: e.scalar_tensor_tensor(out=mixb[:, 1024:2048], in0=zv, scalar=stat[:, 7:8], in1=g_os[:],
                                                                    op0=ALU.mult, op1=ALU.mult),
                     reads=[f"X1_{bi}_lo", "os_r", "g_os"], writes=mixb_hi_n)
                nk = len(kts)
                for hk in range(2):
                    for g in range(8):
                        h = hk * 8 + g
                        for ki, (ks_, bt, btn) in enumerate(kts):
                            P.op("pe", lambda e, hk=hk, g=g, ki=ki, ks_=ks_, nk=nk: e.matmul(
                                ps[:, (6 + hk) * 512 + g * 64:(6 + hk) * 512 + (g + 1) * 64], lhsT=PT[(hk, ki)][:, g * 128:(g + 1) * 128],
                                rhs=Vb[ks_][:, hk * 64:(hk + 1) * 64], start=(ki == 0), stop=(ki == nk - 1)),
                                reads=PT_n[(hk, ki)] + [f"Vb{ks_}"], writes=[f"ps{6 + hk}"])
                        for ki, (ks_, bt, btn) in enumerate(kts):
                            P.op("pe", lambda e, hk=hk, g=g, ki=ki, h=h, nk=nk: e.matmul(
                                ps[:, 2 * 512 + h:2 * 512 + h + 1], lhsT=PT[(hk, ki)][:, g * 128:(g + 1) * 128],
                                rhs=onesb[:, 0:1], start=(ki == 0), stop=(ki == nk - 1)),
                                reads=PT_n[(hk, ki)] + ["onesb"], writes=["ps2"])
                if bi == nb - 1 and nb > 1:
                    o_early["s"] = next_chunk("out")
                    for b2 in range(nb - 1):
                        o_unit(0, o_early["s"], b2, 3 + b2 % 3)
                        o_early["done"].add(b2)
                P.op("dve", lambda e: e.tensor_tensor(out=stat[:, 32:48], in0=ps[:, 1024:1040], in1=esink[:], op=ALU.add),
                     reads=["ps2", "esink"], writes=["st_den"])
                P.op("dve", lambda e: e.reciprocal(out=stat[:, 32:48], in_=stat[:, 32:48]), reads=["st_den"], writes=["st_rden"])
                for hk in range(2):
                    P.op("dve", lambda e, hk=hk: e.tensor_tensor(
                        out=o32[:, hk * 512:(hk + 1) * 512].rearrange("p (g d) -> p g d", d=64),
                        in0=ps[:, (6 + hk) * 512:(7 + hk) * 512].rearrange("p (g d) -> p g d", d=64),
                        in1=stat[:, 32 + hk * 8:40 + hk * 8].unsqueeze(2).to_broadcast([128, 8, 64]), op=ALU.mult),
                        reads=[f"ps{6 + hk}", "st_rden"], writes=o32_n)
                P.op("act", lambda e: e.activation(out=mixb[:, 0:1024], in_=o32, func=AF.Square, accum_out=stat[:, 2:3]),
                     reads=o32_n, writes=mixb_lo_n + ["st2"])
                rstd_of(stat[:, 2:3], stat[:, 3:4], 1.0 / 1024, ["st2"], "oa")
                P.op("dve", lambda e: e.scalar_tensor_tensor(out=mixb[:, 0:1024], in0=o32, scalar=stat[:, 3:4], in1=g_oa[:],
                                                             op0=ALU.mult, op1=ALU.mult),
                     reads=o32_n + ["oa_r", "g_oa"], writes=mixb_lo_n)
                transposes_to(mixb, 16, Tr[:, :, bi * 128:(bi + 1) * 128], mixb_n, [f"T{bi}"])
                P.dma("pool", "xld2", lambda e, bi=bi: e.dma_start(out=X1[:, bi, :], in_=xsrc(bi)),
                      writes=[f"X1_{bi}_lo", f"X1_{bi}_hi"])

            ckpt(3)
            for c in range(4):
                s = o_early["s"] if (c == 0 and o_early["s"] is not None) else next_chunk("out")
                for bi in range(nb):
                    if c == 0 and bi in o_early["done"]:
                        continue
                    pb = 2 + (c * nb + bi) % 4
                    o_unit(c, s, bi, pb)
                    if c == 3:
                        if bi >= 1:
                            transposes_to(mixb, 16, Tr[:, :, (bi - 1) * 128:bi * 128], mixb_n, [f"T{bi - 1}"])
                        x1n = [f"X1_{bi}_lo", f"X1_{bi}_hi"]
                        P.op("act", lambda e, bi=bi: e.activation(out=mixb, in_=X1[:, bi, :], func=AF.Square, accum_out=stat[:, 0:1]),
                             reads=x1n, writes=mixb_n + ["st0"])
                        rstd_of(stat[:, 0:1], stat[:, 1:2], 1.0 / D, ["st0"], "n1")
                        P.op("dve", lambda e, bi=bi: e.scalar_tensor_tensor(out=mixb, in0=X1[:, bi, :], scalar=stat[:, 1:2], in1=g_ffn[:],
                                                                            op0=ALU.mult, op1=ALU.mult),
                             reads=x1n + ["n1_r", "g_ffn"], writes=mixb_n)
            transposes_to(mixb, 16, Tr[:, :, (nb - 1) * 128:nb * 128], mixb_n, [f"T{nb - 1}"])

            ckpt(4)
            Tn = [f"T{bi}" for bi in range(nb)]
            ev = 0
            for fg in range(16):
                s = next_chunk("up")
                wv = SLOT_AP[s].rearrange("p (c n) -> p c n", c=16)
                for fc in range(4):
                    f = fg * 4 + fc
                    pb = f % 4
                    for dc in range(16):
                        P.op("pe", lambda e, pb=pb, dc=dc, fc=fc, wv=wv: e.matmul(
                            bank(pb, ntok), lhsT=wv[:, dc, fc * 128:(fc + 1) * 128], rhs=Tr[:, dc, 0:ntok], start=(dc == 0), stop=(dc == 15)),
                            reads=Tn + SLOT_N[s], writes=[f"ps{pb}"])
                    tr = ev % 2
                    ev += 1
                    P.op("act", lambda e, pb=pb, tr=tr: e.activation(out=tmpR[tr][:, 0:ntok], in_=bank(pb, ntok), func=AF.Relu),
                         reads=[f"ps{pb}"], writes=[f"tmpR{tr}"])
                    hv = hid(f, 0, ntok)
                    P.op("dve", lambda e, hv=hv, tr=tr: e.tensor_tensor(out=hv, in0=tmpR[tr][:, 0:ntok], in1=tmpR[tr][:, 0:ntok],
                                                                     op=ALU.mult),
                         reads=[f"tmpR{tr}"], writes=[hid_n(f)])

            ckpt(5)
            ydst = (lambda bi: ys.ap()) if sample else (lambda bi: yp.ap()[(4 * t + bi) * 128:(4 * t + bi + 1) * 128, :])
            yt = 0
            for r in range(2):
                for g8 in range(8):
                    s = next_chunk("down")
                    wv = SLOT_AP[s].rearrange("p (c n) -> p c n", c=8)
                    order = [(fc, bi, nh) for fc in range(8) for bi in range(nb) for nh in range(2)]
                    if g8 == 7:
                        order = [(fc, bi, nh) for bi in range(nb) for nh in range(2) for fc in range(8)]
                    for fc, bi, nh in order:
                        f = g8 * 8 + fc
                        pb = bi * 2 + nh
                        hv = hid(f, bi * 128, (bi + 1) * 128)
                        P.op("pe", lambda e, pb=pb, f=f, hv=hv, nh=nh, fc=fc, wv=wv: e.matmul(
                            bank(pb), lhsT=hv, rhs=wv[:, fc, nh * 512:(nh + 1) * 512],
                            start=(f == 0), stop=(f == 63)),
                            reads=[hid_n(f)] + SLOT_N[s], writes=[f"ps{pb}"])
                which = "lo" if r == 0 else "hi"
                for bi in range(nb):
                    ysl = yt % 2
                    yt += 1
                    ytmp = Tr[:, 8 * ysl:8 * ysl + 4, :].rearrange("p c n -> p (c n)").bitcast(F32)
                    for nh in range(2):
                        pb = bi * 2 + nh
                        P.op("dve", lambda e, pb=pb, bi=bi, nh=nh, ytmp=ytmp, r=r: e.tensor_tensor(
                            out=ytmp[:, nh * 512:(nh + 1) * 512], in0=bank(pb), in1=X1[:, bi, r * 1024 + nh * 512:r * 1024 + (nh + 1) * 512],
                            op=ALU.add),
                            reads=[f"ps{pb}", f"X1_{bi}_{which}"], writes=[f"ytmp{ysl}"] + [f"T{i}" for i in range(4)])
                    P.dma("pool", f"yst{ysl}", lambda e, bi=bi, ytmp=ytmp, r=r: e.dma_start(out=ydst(bi)[:, r * 1024:(r + 1) * 1024], in_=ytmp),
                          reads=[f"ytmp{ysl}"])
            for ysl in range(2):
                P.op("dve", lambda e: e.memset(stat[:, 60:61], 0.0), writes=[f"ytmp{ysl}"] + [f"T{i}" for i in range(4)] + ["st60"])

        try:
            for kind, t in (TILES if tiles is None else tiles):
                run_tile(kind, t)
            if tiles is None:
                assert wstate["i"] == len(chunks), (wstate, len(chunks))
        except _Stop:
            pass
        P.finalize(block)
    return nc


def _t5_bucket_static(n):
    import math
    try:
        import jax
        import jax.numpy as jnp
        with jax.default_device(jax.devices("cpu")[0]):
            nn = jnp.asarray(n, dtype=jnp.int32)
            half, max_exact = 16, 8
            offset = jnp.where(nn < 0, half, 0)
            a = jnp.abs(nn)
            af = jnp.maximum(a, 1).astype(jnp.float32)
            large = max_exact + (jnp.log(af / max_exact) / math.log(128 / max_exact) * (half - max_exact)).astype(jnp.int32)
            large = jnp.minimum(large, half - 1)
            return np.asarray(offset + jnp.where(a < max_exact, a, large))
    except Exception:
        nn = np.asarray(n, dtype=np.int32)
        half, max_exact = 16, 8
        offset = np.where(nn < 0, half, 0)
        a = np.abs(nn)
        af = np.maximum(a, 1).astype(np.float32)
        large = max_exact + (np.log(af / np.float32(max_exact)) / np.float32(math.log(128 / max_exact))
                             * np.float32(half - max_exact)).astype(np.int32)
        large = np.minimum(large, half - 1)
        return offset + np.where(a < max_exact, a, large)


_NC_CACHE = {}


def prep_inputs(x_prompt, x_sample, cache_attn_k, cache_attn_v, rel_bias_table, ln_mix_g, w_in,
                q_norm_g, k_norm_g, attn_sinks, sgu_norm_g, sgu_w, sgu_b, out_norm_attn_g,
                out_norm_sgu_g, w_out, ln_ffn_g, w_ffn_up, w_ffn_down):
    f = lambda a: np.ascontiguousarray(np.asarray(a, dtype=np.float32))
    x_prompt, x_sample = f(x_prompt), f(x_sample)
    hk, g, d = np.meshgrid(np.arange(2), np.arange(8), np.arange(64), indexing="ij")
    qcols = ((hk * 8 + g) * 64 + d).transpose(1, 0, 2).reshape(-1)
    perm = np.concatenate([qcols, np.arange(1024, 1280), np.arange(2304, 3328), np.arange(1280, 2304)])
    w_in_p = np.asarray(w_in)[0][:, perm]

    def img_cols(w, col0, wdt):
        return w[:, col0:col0 + wdt].reshape(16, 128, wdt).transpose(1, 0, 2).reshape(128, 16 * wdt)

    IN_COL0 = [0, 512, 1024, 1280, 1792, 2304, 2816]
    IN_W = [512, 512, 256, 512, 512, 512, 512]
    w_in_img = f(np.concatenate([img_cols(w_in_p, c0, wd) for c0, wd in zip(IN_COL0, IN_W)], axis=1))
    wo = np.asarray(w_out)[0]
    w_out_img = f(np.concatenate([img_cols(wo, c * 512, 512) for c in range(4)], axis=1))
    wu = np.asarray(w_ffn_up)[0]
    w_up_img = f(np.concatenate([img_cols(wu, c * 512, 512) for c in range(16)], axis=1))
    wd_ = np.asarray(w_ffn_down)[0]
    w_down_img = f(np.concatenate(
        [wd_[g * 1024:(g + 1) * 1024, r * 1024:(r + 1) * 1024].reshape(8, 128, 1024).transpose(1, 0, 2).reshape(128, 8192)
         for r in range(2) for g in range(8)], axis=1))
    m = np.arange(384)
    bucket = _t5_bucket_static(255 - m)
    oh = np.zeros((32, 384), np.float32)
    oh[bucket, m] = 1.0
    common = {
        "table": f(rel_bias_table), "oh": oh, "ident": np.eye(128, dtype=np.float32),
        "w_in": w_in_img, "w_out": w_out_img, "w_up": w_up_img, "w_down": w_down_img,
        "g_mix": f(ln_mix_g), "g_ffn": f(ln_ffn_g), "g_sgu": f(sgu_norm_g), "g_oa": f(out_norm_attn_g), "g_os": f(out_norm_sgu_g),
        "gq": f(np.asarray(q_norm_g)[0][None, :]), "gk": f(np.asarray(k_norm_g)[0][None, :]), "sinks": f(np.asarray(attn_sinks)[0].reshape(1, 16)),
        "wsT": f(np.asarray(sgu_w)[0].transpose(2, 0, 1)), "bsT": f(np.asarray(sgu_b)[0].T),
    }
    ck = np.asarray(cache_attn_k)[0]
    cvv = np.asarray(cache_attn_v)[0]
    in_maps = []
    for c in range(NCORES):
        mm = dict(common)
        mm["xp"] = x_prompt[c]
        mm["xs"] = f(x_sample[2 * c:2 * c + 2].reshape(128, D))
        mm["ckT"] = f(ck[2 * c:2 * c + 2].reshape(2, 128, 128).transpose(0, 2, 1))
        mm["cv"] = f(cvv[2 * c:2 * c + 2].reshape(2, 128, 128))
        in_maps.append(mm)
    return in_maps


def kernel(**inputs):
    in_maps = prep_inputs(**inputs)
    if "nc" not in _NC_CACHE:
        _NC_CACHE["nc"] = build_program()
    nc = _NC_CACHE["nc"]
    res = run_bass_kernel_spmd(nc, in_maps, core_ids=list(range(NCORES)))
    R = res.results
    y_prompt = np.stack([R[c]["yp"] for c in range(NCORES)]).astype(np.float32)
    y_sample = np.concatenate([R[c]["ys"].reshape(2, 64, D) for c in range(NCORES)]).astype(np.float32)
    kpo = np.stack([R[c]["kp"].reshape(128, 2, 64) for c in range(NCORES)])[None].astype(np.float32)
    vpo = np.stack([R[c]["vp"].reshape(128, 2, 64) for c in range(NCORES)])[None].astype(np.float32)
    kso = np.concatenate([R[c]["ks"].reshape(2, 64, 2, 64) for c in range(NCORES)])[None].astype(np.float32)
    vso = np.concatenate([R[c]["vs"].reshape(2, 64, 2, 64) for c in range(NCORES)])[None].astype(np.float32)
    sgo = np.concatenate([R[c]["sgv"].reshape(2, 64, 1024) for c in range(NCORES)])[None].astype(np.float32)
    return (y_prompt, y_sample, kpo, vpo, kso, vso, sgo)
```

```python
import numpy as np
import concourse.bass as bass
import concourse.mybir as mybir
from concourse.bass_utils import run_bass_kernel_spmd

F32 = mybir.dt.float32
BF16 = mybir.dt.bfloat16
AF = mybir.ActivationFunctionType
ALU = mybir.AluOpType
AX = mybir.AxisListType

D = 2048
DFF = 8192
NCORES = 8
EPS = 1e-6
NEGB = -30000.0
import os
SAMPLE_RING = int(os.environ.get("SAMPLE_RING", "6"))
ENGS = ("pe", "act", "dve", "pool", "sp")


class _Op:
    __slots__ = ("eng", "fn", "reads", "writes", "dma", "deps", "signal", "ev", "idx")

    def __init__(self, eng, fn, reads, writes, dma):
        self.eng, self.fn, self.reads, self.writes, self.dma = eng, fn, reads, writes, dma
        self.deps = set()
        self.signal = False
        self.ev = None


class Prog:
    def __init__(self, nc, same_engine_sync=True):
        self.nc = nc
        self.ops = []
        self.last_w = {}
        self.readers = {}
        self.same_engine_sync = same_engine_sync

    def _add(self, eng, fn, reads, writes, dma=None):
        op = _Op(eng, fn, tuple(reads), tuple(writes), dma)
        op.idx = len(self.ops)
        for r in op.reads:
            w = self.last_w.get(r)
            if w is not None:
                op.deps.add(w)
        for w_ in op.writes:
            w = self.last_w.get(w_)
            if w is not None:
                op.deps.add(w)
            latest = {}
            for rd in self.readers.get(w_, ()):
                ro = self.ops[rd]
                if ro.dma is not None:
                    op.deps.add(rd)
                else:
                    latest[ro.eng] = max(latest.get(ro.eng, -1), rd)
            op.deps.update(latest.values())
        for r in op.reads:
            self.readers.setdefault(r, []).append(op.idx)
        for w_ in op.writes:
            self.last_w[w_] = op.idx
            self.readers[w_] = []
        op.deps.discard(op.idx)
        self.ops.append(op)
        return op

    def op(self, eng, fn, reads=(), writes=()):
        return self._add(eng, fn, reads, writes)

    def dma(self, eng, sem_name, fn, reads=(), writes=()):
        return self._add(eng, fn, reads, writes, dma=sem_name)

    def finalize(self, block):
        import os
        nc = self.nc
        ops = self.ops
        nmax = int(os.environ.get("BISECT_N", "0"))
        if nmax:
            ops = ops[:nmax]
            for i, o in enumerate(ops[-3:]):
                print("last ops:", o.idx, o.eng, o.reads, o.writes, o.dma)
        print("n_ops", len(ops))
        for op in ops:
            for d in op.deps:
                p = ops[d]
                if p.dma is not None:
                    continue
                if p.eng == op.eng and (p.eng == "pe" or not self.same_engine_sync):
                    continue
                p.signal = True
        eng_sem = {e: nc.alloc_semaphore("sem_" + e) for e in ENGS}
        self.all_sems = list(eng_sem.values())
        cnt = {e: 0 for e in ENGS}
        dsem, dcnt = {}, {}
        for op in ops:
            if op.dma is not None:
                if op.dma not in dsem:
                    dsem[op.dma] = nc.alloc_semaphore("dsem_" + op.dma)
                    self.all_sems.append(dsem[op.dma])
                    dcnt[op.dma] = 0
                dcnt[op.dma] += 16
                op.ev = (op.dma, dcnt[op.dma])
            elif op.signal:
                cnt[op.eng] += 1
                op.ev = (op.eng, cnt[op.eng])
        issued = {k: 0 for k in dsem}
        waits_for = []
        for op in ops:
            w = {}
            for d in op.deps:
                p = ops[d]
                if p.dma is not None:
                    w[("d", p.dma)] = max(w.get(("d", p.dma), 0), issued[p.dma])
                elif p.ev is not None:
                    w[("e", p.eng)] = max(w.get(("e", p.eng), 0), p.ev[1])
            waits_for.append(w)
            if op.dma is not None:
                issued[op.dma] += 16
        final = dict(dcnt)

        def semof(k):
            return dsem[k[1]] if k[0] == "d" else eng_sem[k[1]]

        def emit(engname, engobj):
            waited = {}
            for op, w in zip(ops, waits_for):
                if op.eng != engname:
                    continue
                for k, v in w.items():
                    if waited.get(k, 0) >= v:
                        continue
                    engobj.wait_ge(semof(k), v)
                    waited[k] = v
                inst = op.fn(engobj)
                if op.dma is not None:
                    inst.then_inc(dsem[op.dma], 16)
                elif op.signal:
                    inst.then_inc(eng_sem[op.eng], 1)
            if engname == "sp":
                for k, v in final.items():
                    engobj.wait_ge(dsem[k], v)

        for sm in self.all_sems:
            nc.sync.sem_clear(sm)
        nc.all_engine_barrier()
        block = nc.Block().__enter__()
        self._block = block

        @block.tensor
        def _(e):
            emit("pe", e)

        @block.scalar
        def _(e):
            emit("act", e)

        @block.vector
        def _(e):
            emit("dve", e)

        @block.gpsimd
        def _(e):
            emit("pool", e)

        @block.sync
        def _(e):
            emit("sp", e)

        block.__exit__(None, None, None)


class _Stop(Exception):
    pass


def build_program(stage=99, tiles=None):
    nc = bass.Bass("TRN2", target_bir_lowering=False)

    def din(name, shape):
        return nc.dram_tensor(name, list(shape), F32, kind="ExternalInput")

    def dout(name, shape):
        return nc.dram_tensor(name, list(shape), F32, kind="ExternalOutput")

    xp = din("xp", (2048, D))
    xs = din("xs", (128, D))
    ckT = din("ckT", (2, 128, 128))
    cv = din("cv", (2, 128, 128))
    table = din("table", (32, 16))
    oh = din("oh", (32, 384))
    ident = din("ident", (128, 128))
    w_in = din("w_in", (128, 16 * 3328))
    w_out = din("w_out", (128, 4 * 8192))
    w_up = din("w_up", (128, 16 * 8192))
    w_down = din("w_down", (128, 16 * 8192))
    wimg = {"in": w_in, "out": w_out, "up": w_up, "down": w_down}
    wscr = {k: nc.dram_tensor("scr_" + k, list(v.shape), BF16, kind="ExternalOutput") for k, v in wimg.items()}
    g_mix_d = din("g_mix", (1, D))
    g_ffn_d = din("g_ffn", (1, D))
    g_sgu_d = din("g_sgu", (1, 1024))
    g_oa_d = din("g_oa", (1, 1024))
    g_os_d = din("g_os", (1, 1024))
    gq_d = din("gq", (1, 64))
    gk_d = din("gk", (1, 64))
    sinks_d = din("sinks", (1, 16))
    wsT_d = din("wsT", (128, 8, 128))
    bsT_d = din("bsT", (128, 8))
    scr = nc.dram_tensor("scr", [16, 384], F32, kind="Internal")

    yp = dout("yp", (2048, D))
    ys = dout("ys", (128, D))
    kp_o = dout("kp", (128, 128))
    vp_o = dout("vp", (128, 128))
    ks_o = dout("ks", (128, 128))
    vs_o = dout("vs", (128, 128))
    sgv_o = dout("sgv", (128, 1024))

    def sb(name, shape, dt):
        return nc.alloc_sbuf_tensor(name, list(shape), dt)

    identb = sb("identb", (128, 128), BF16)
    g_mix = sb("g_mix_s", (128, D), F32)
    g_ffn = sb("g_ffn_s", (128, D), F32)
    g_sgu = sb("g_sgu_s", (128, 1024), F32)
    g_oa = sb("g_oa_s", (128, 1024), F32)
    g_os = sb("g_os_s", (128, 1024), F32)
    gq_t = sb("gq_s", (128, 64), F32)
    gk_t = sb("gk_s", (128, 64), F32)
    esink = sb("esink", (128, 16), F32)
    epsb = sb("epsb", (128, 1), F32)
    onesb = sb("onesb", (128, 2), BF16)
    WsT = sb("WsT", (128, 8, 128), BF16)
    bsT = sb("bsT_s", (128, 8), F32)
    bsTs = sb("bsTs", (128, 8), F32)
    biasP = sb("biasP", (128, 16, 128), F32)
    biasO = sb("biasO", (128, 16, 128), F32)
    NSLOT = 5
    kTz = [[sb(f"kTz{hk}_{s}", (128, 128), BF16) for s in range(NSLOT)] for hk in range(2)]
    Vb = [sb(f"Vb{s}", (128, 128), BF16) for s in range(NSLOT)]
    qT = sb("qT", (128, 8, 128), BF16)
    stat = sb("stat", (128, 64), F32)
    tmpR = [sb(f"tmpR{i}", (128, 512), F32) for i in range(2)]
    X1 = sb("X1", (128, 4, D), F32)
    Hr = sb("Hr", (128, 16384), F32)
    Hrb = Hr[:, :].bitcast(BF16)
    Tr = sb("Tr", (128, 16, 512), BF16)
    Wr = [sb(f"Wr{i}", (128, 8192), BF16) for i in range(2)]
    ps = nc.alloc_psum_tensor("ps", [128, 4096], F32)

    def bank(i, n=512):
        return ps[:, i * 512:i * 512 + n]

    ptr = ps[:, 0:1024].bitcast(BF16)
    ptr2 = ps[:, 3072:4096].bitcast(BF16)
    tstate = {"n": 0}

    class HAlloc:
        def __init__(self):
            self.off = 0

        def take(self, nbytes, dt, shape=None):
            start = (self.off + 1023) // 1024 * 1024
            self.off = start + nbytes
            assert self.off <= 65536, self.off
            ap = Hr[:, start // 4:(start + nbytes) // 4]
            if dt == BF16:
                ap = Hrb[:, start // 2:(start + nbytes) // 2]
            names = [f"H{i}" for i in range(start // 1024, (start + nbytes + 1023) // 1024)]
            return ap, names

    ha = HAlloc()
    zqk, zqk_n = [], []
    for b in range(4):
        a, n = ha.take(1152 * 4, F32)
        zqk.append(a)
        zqk_n.append(n)
    tmpq, tmpq_n = ha.take(1152 * 4, F32)
    qknb, qknb_n = ha.take(1152 * 2, BF16)
    tmpS, tmpS_n = [], []
    for i in range(2):
        a, n = ha.take(512 * 4, F32)
        tmpS.append(a)
        tmpS_n.append(n)
    PT, PT_n = {}, {}
    for hk in range(2):
        for kt in range(3):
            a, n = ha.take(1024 * 2, BF16)
            PT[(hk, kt)] = a
            PT_n[(hk, kt)] = n
    o32, o32_n = ha.take(1024 * 4, F32)
    vnb, vnb_n = ha.take(1024 * 2, BF16)
    xtmp, xtmp_n = ha.take(D * 4, F32)
    hank, hank_n = xtmp, xtmp_n
    mixb, mixb_n = ha.take(D * 2, BF16)
    mixb_lo_n, mixb_hi_n = mixb_n[:2], mixb_n[2:]
    assert len(mixb_n) == 4
    h_used = ha.off
    tbl = tmpR[1][0:32, 384:400]
    ohs = tmpR[0][0:32, 0:384]
    srow = tmpR[1][0:16, 0:384]
    kn32 = o32[:, 0:128]
    v32 = o32[:, 128:256]
    zq1_start = 5 * 1024
    biasPB = Hr[:, zq1_start // 4:(zq1_start + 8192) // 4]
    wsts_start = 15 * 1024
    WsTs = Hrb[:, wsts_start // 2:(wsts_start + 2048) // 2].rearrange("p (g i) -> p g i", g=8)
    WsTs_n = ["H15", "H16"]
    biasPB_n = [f"H{i}" for i in range(zq1_start // 1024, zq1_start // 1024 + 8)]
    ALLH = [f"H{i}" for i in range(64)]

    hstate = {"compact": False}

    def hid(f, t0=0, t1=512):
        if hstate["compact"]:
            return Hrb[:, f * 128 + t0:f * 128 + t1]
        return Hrb[:, f * 512 + t0:f * 512 + t1]

    def hid_n(f):
        return f"H{f // 4}" if hstate["compact"] else f"H{f}"

    P = Prog(nc)
    _cst = {"n": 0}
    _orig_dma = P.dma

    def _dma(eng, sem_name, fn, reads=(), writes=()):
        if sem_name == "cst":
            _cst["n"] += 1
            sem_name = f"cst{_cst['n']}"
        return _orig_dma(eng, sem_name, fn, reads=reads, writes=writes)

    P.dma = _dma

    TILES = [("p", t) for t in range(4)] + [("s", 0)]
    chunks = []
    for _ in (TILES if tiles is None else tiles):
        chunks += [("in", c) for c in range(7)]
        chunks += [("out", c) for c in range(4)]
        chunks += [("up", c) for c in range(16)]
        chunks += [("down", r, g) for r in range(2) for g in range(8)]
    IN_COL0 = [0, 512, 1024, 1280, 1792, 2304, 2816]
    IN_W = [512, 512, 256, 512, 512, 512, 512]
    wstate = {"i": 0, "issued": 0}

    NCH = 43
    SLOT_AP = [Wr[0][:, :], Wr[1][:, :], X1[:, 1:3, :].rearrange("p b n -> p (b n)").bitcast(BF16),
               Hrb[:, 8192:16384], Hrb[:, 16384:24576], Hrb[:, 24576:32768]]
    SLOT_N = [["w0"], ["w1"], ["X1_1_lo", "X1_1_hi", "X1_2_lo", "X1_2_hi"],
              [f"H{i}" for i in range(16, 32)], [f"H{i}" for i in range(32, 48)], [f"H{i}" for i in range(48, 64)]]
    n_tiles_run = len(TILES if tiles is None else tiles)
    tile_kinds = [k for k, _ in (TILES if tiles is None else tiles)]

    def slot_of(i):
        j = i % NCH
        if tile_kinds[i // NCH] == "s":
            if SAMPLE_RING <= 2:
                return i % 2
            return [0, 1, 2][j % 3] if j < 11 else [2, 0, 1, 3, 4, 5][(j - 11) % 6]
        return i % 2

    def ahead_of(i):
        j = i % NCH
        if tile_kinds[i // NCH] == "s":
            if SAMPLE_RING <= 2:
                return 1
            return 2 if j < 11 else 5
        return 1

    slot_last = {}

    def issue_chunk(i):
        c = chunks[i]
        s = slot_of(i)
        assert slot_last.get(s, -1) < wstate["i"], (i, s, slot_last.get(s), wstate["i"])
        slot_last[s] = i
        tile_i, j = i // NCH, i % NCH
        multi = n_tiles_run >= 3
        if not multi:
            cast, wback = tile_i == 0, tile_i == 0
        elif tile_i == 0:
            cast, wback = True, (j % 2 == 0)
        elif tile_i == 1:
            cast, wback = (j % 2 == 1), (j % 2 == 1)
        else:
            cast, wback = False, False
        kind = c[0]
        if kind == "in":
            off, ln = 16 * IN_COL0[c[1]], 16 * IN_W[c[1]]
        elif kind == "down":
            off, ln = (c[1] * 8 + c[2]) * 8192, 8192
        else:
            off, ln = c[1] * 8192, 8192
        d = SLOT_AP[s][:, 0:ln]
        rname = f"scr_{kind}_{off}"
        if cast:
            src = wimg[kind].ap()[:, off:off + ln]
            P.dma("pool", f"w{s}q", lambda e, d=d, src=src: e.dma_start(out=d, in_=src), writes=SLOT_N[s])
            if wback:
                dsts = wscr[kind].ap()[:, off:off + ln]
                P.dma("sp", "wst", lambda e, d=d, dsts=dsts: e.dma_start(out=dsts, in_=d), reads=SLOT_N[s], writes=[rname])
        else:
            src = wscr[kind].ap()[:, off:off + ln]
            P.dma("sp", f"w{s}", lambda e, d=d, src=src: e.dma_start(out=d, in_=src), reads=[rname], writes=SLOT_N[s])

    def next_chunk(kind):
        i = wstate["i"]
        assert chunks[i][0] == kind, (chunks[i], kind)
        while wstate["issued"] <= min(i + ahead_of(i), len(chunks) - 1):
            nxt = wstate["issued"]
            if nxt // NCH != i // NCH and nxt > i + 1:
                break
            issue_chunk(nxt)
            wstate["issued"] += 1
        wstate["i"] += 1
        return slot_of(i)

    def rstd_of(ss_col, r_col, inv_n, reads, tag):
        P.op("act", lambda e: e.activation(out=r_col, in_=ss_col, func=AF.Sqrt, scale=inv_n, bias=epsb[:]),
             reads=reads + ["epsb"], writes=[tag + "_r"])
        P.op("dve", lambda e: e.reciprocal(out=r_col, in_=r_col), reads=[tag + "_r"], writes=[tag + "_r"])

    def transposes_to(src_tile, nch, dst_ap, src_names, dst_names, evac_eng="act"):
        k = tstate["n"] % 2
        tstate["n"] += 1
        pt = [ptr, ptr2][k]
        pn = [["ps0", "ps1"], ["ps6", "ps7"]][k]
        for c in range(nch):
            P.op("pe", lambda e, c=c, pt=pt: e.matmul(pt[:, c * 128:(c + 1) * 128], lhsT=src_tile[:, c * 128:(c + 1) * 128],
                                                    rhs=identb[:], start=True, stop=True, is_transpose=True),
                 reads=src_names + ["identb"], writes=pn)
        src = pt[:, 0:nch * 128].rearrange("p (c t) -> p c t", c=nch)
        if evac_eng == "act":
            P.op("act", lambda e: e.copy(out=dst_ap, in_=src), reads=pn, writes=dst_names)
        else:
            P.op("dve", lambda e: e.tensor_copy(out=dst_ap, in_=src), reads=pn, writes=dst_names)

    def toeplitz(dst, dst_names, c0):
        hk_ = hank.rearrange("p (h q) -> p h q", h=16)
        P.dma("sp", "cst", lambda e: e.dma_start(out=hk_, in_=bass.AP(tensor=scr, offset=c0, ap=[[1, 128], [384, 16], [1, 128]])),
              reads=["scr"], writes=hank_n)
        t = hank
        rev = bass.AP(tensor=t.tensor, offset=t.offset + 127, ap=[list(t.ap[0]), [128, 16], [-1, 128]])
        P.op("dve", lambda e: e.tensor_copy(out=dst, in_=rev), reads=hank_n, writes=dst_names)

    block = None
    if True:
        P.op("dve", lambda e: e.memset(epsb[:], EPS), writes=["epsb"])
        P.op("dve", lambda e: e.memset(onesb[:], 1.0), writes=["onesb"])
        for hk in range(2):
            for s in range(NSLOT):
                P.op("dve", lambda e, hk=hk, s=s: e.memset(kTz[hk][s][:], 0.0), writes=[f"kTz{hk}_{s}"])

        def bc_load(dst, src, n, name):
            P.dma("sp", "cst", lambda e: e.dma_start(out=dst[:], in_=src.ap().partition_broadcast(128)[:, 0, :]), writes=[name])

        idf = xtmp[:, 0:128]
        P.dma("sp", "cst", lambda e: e.dma_start(out=idf, in_=ident.ap()), writes=xtmp_n)
        P.op("dve", lambda e: e.tensor_copy(out=identb[:], in_=idf), reads=xtmp_n, writes=["identb"])
        bc_load(g_mix, g_mix_d, D, "g_mix")
        P.dma("sp", "cst", lambda e: e.dma_start(out=tbl, in_=table.ap()), writes=["tmpR1"])
        P.dma("sp", "cst", lambda e: e.dma_start(out=ohs, in_=oh.ap()), writes=["tmpR0"])

    def late_setup():
        bc_load(g_ffn, g_ffn_d, D, "g_ffn")
        bc_load(g_sgu, g_sgu_d, 1024, "g_sgu")
        bc_load(g_oa, g_oa_d, 1024, "g_oa")
        bc_load(g_os, g_os_d, 1024, "g_os")
        bc_load(gq_t, gq_d, 64, "gq")
        bc_load(gk_t, gk_d, 64, "gk")
        bc_load(esink, sinks_d, 16, "esink")
        P.op("act", lambda e: e.activation(out=esink[:], in_=esink[:], func=AF.Exp), reads=["esink"], writes=["esink"])
        wst = hank.rearrange("p (g i) -> p g i", g=16)[:, 0:8, :]
        P.dma("sp", "cst", lambda e: e.dma_start(out=wst, in_=wsT_d.ap()), writes=hank_n)
        P.op("dve", lambda e: e.tensor_copy(out=WsT[:], in_=wst), reads=hank_n, writes=["WsT"])
        P.op("dve", lambda e: e.memset(WsT[64:128, :, 0:64], 0.0), reads=[], writes=["WsT"])
        P.dma("sp", "cst", lambda e: e.dma_start(out=bsT[:], in_=bsT_d.ap()), writes=["bsT"])
        P.dma("sp", "cst", lambda e: e.dma_start(out=bsTs[0:64, :], in_=bsT_d.ap()[0:64, :]), writes=["bsTs"])
        P.dma("sp", "cst", lambda e: e.dma_start(out=bsTs[64:128, :], in_=bsT_d.ap()[0:64, :]), writes=["bsTs"])
        P.op("pe", lambda e: e.matmul(bank(2)[0:16, 0:384], lhsT=tbl, rhs=ohs, start=True, stop=True),
             reads=["tmpR0", "tmpR1"], writes=["ps2"])
        P.op("dve", lambda e: e.tensor_copy(out=srow, in_=bank(2)[0:16, 0:384]), reads=["ps2"], writes=["tmpR1"])
        P.dma("sp", "cst", lambda e: e.dma_start(out=scr.ap(), in_=srow), reads=["tmpR1"], writes=["scr"])
        toeplitz(biasP[:], ["biasP"], 0)
        P.op("dve", lambda e: e.memset(biasP[0:64, :, 64:128], NEGB), writes=["biasP"])
        toeplitz(biasO[:], ["biasO"], 128)
        P.op("dve", lambda e: e.memset(biasO[64:128, :, 0:64], NEGB), writes=["biasO"])

    late_done = {"v": False}
    if True:
        def ckpt(n):
            if stage <= n:
                raise _Stop()

        def run_tile(kind, t):
            sample = kind == "s"
            hstate["compact"] = sample and os.environ.get("NO_COMPACT", "0") != "1"
            nb = 1 if sample else 4
            ntok = nb * 128
            gB = [16] if sample else [4 * t + i for i in range(4)]
            xsrc = (lambda bi: xs.ap()) if sample else (lambda bi: xp.ap()[(4 * t + bi) * 128:(4 * t + bi + 1) * 128, :])

            if sample and not late_done["v"]:
                late_done["v"] = True
                late_setup()
            if sample:
                P.op("dve", lambda e: e.memset(biasP[:, :, 64:128], NEGB), writes=["biasP"])
                P.op("dve", lambda e: e.memset(biasO[0:64, :, 64:128], NEGB), writes=["biasO"])
                toeplitz(biasPB.rearrange("p (h q) -> p h q", h=16), biasPB_n, 64)
                P.op("dve", lambda e: e.memset(biasPB.rearrange("p (h q) -> p h q", h=16)[:, :, 0:64], NEGB), writes=biasPB_n)
                wstg = xtmp.rearrange("p (g i) -> p g i", g=16)[:, 0:8, :]
                P.dma("pool", "xld", lambda e: e.dma_start(out=wstg[0:64, :, 0:64], in_=wsT_d.ap()[0:64, :, 0:64]), writes=xtmp_n)
                P.dma("pool", "xld", lambda e: e.dma_start(out=wstg[64:128, :, 64:128], in_=wsT_d.ap()[0:64, :, 0:64]), writes=xtmp_n)
                P.op("dve", lambda e: e.memset(WsTs, 0.0), writes=WsTs_n)
                P.op("dve", lambda e: e.tensor_copy(out=WsTs[0:64, :, 0:64], in_=wstg[0:64, :, 0:64]), reads=xtmp_n, writes=WsTs_n)
                P.op("dve", lambda e: e.tensor_copy(out=WsTs[64:128, :, 64:128], in_=wstg[64:128, :, 64:128]), reads=xtmp_n, writes=WsTs_n)
                for sq in range(2):
                    slot = 2 + sq
                    st_ = xtmp[:, sq * 256:sq * 256 + 128]
                    sv_ = xtmp[:, sq * 256 + 128:sq * 256 + 256]
                    P.dma("pool", "xld", lambda e, st_=st_, sq=sq: e.dma_start(out=st_, in_=ckT.ap()[sq]), writes=xtmp_n)
                    P.dma("pool", "xld", lambda e, sv_=sv_, sq=sq: e.dma_start(out=sv_, in_=cv.ap()[sq]), writes=xtmp_n)
                    P.op("dve", lambda e, st_=st_, slot=slot: e.tensor_copy(out=kTz[0][slot][0:64, :], in_=st_[0:64, :]),
                         reads=xtmp_n, writes=[f"kTz0_{slot}"])
                    P.op("dve", lambda e, st_=st_, slot=slot: e.tensor_copy(out=kTz[1][slot][64:128, :], in_=st_[64:128, :]),
                         reads=xtmp_n, writes=[f"kTz1_{slot}"])
                    P.op("dve", lambda e, sv_=sv_, slot=slot: e.tensor_copy(out=Vb[slot][:], in_=sv_), reads=xtmp_n, writes=[f"Vb{slot}"])

            ckpt(0)
            for bi in range(nb):
                P.dma("pool", f"xld{bi}", lambda e, bi=bi: e.dma_start(out=X1[:, bi, :], in_=xsrc(bi)),
                      writes=[f"X1_{bi}_lo", f"X1_{bi}_hi"])
            def a1_elem(bi):
                zq3 = zqk[bi].rearrange("p (h d) -> p h d", d=64)
                tq3 = tmpq.rearrange("p (h d) -> p h d", d=64)
                P.op("dve", lambda e, bi=bi: e.tensor_tensor(out=tmpq, in0=zqk[bi], in1=zqk[bi], op=ALU.mult),
                     reads=zqk_n[bi], writes=tmpq_n)
                P.op("dve", lambda e: e.tensor_reduce(out=stat[:, 8:26], in_=tq3, op=ALU.add, axis=AX.X),
                     reads=tmpq_n, writes=["qk_r"])
                rstd_of(stat[:, 8:26], stat[:, 8:26], 1.0 / 64, ["qk_r"], "qk")
                P.op("dve", lambda e, zq3=zq3: e.tensor_tensor(out=tq3, in0=zq3, in1=stat[:, 8:26].unsqueeze(2).to_broadcast([128, 18, 64]),
                                                             op=ALU.mult),
                     reads=zqk_n[bi] + ["qk_r"], writes=tmpq_n)
                P.op("dve", lambda e: e.tensor_tensor(out=qknb[:, 0:1024].rearrange("p (h d) -> p h d", d=64), in0=tq3[:, 0:16, :],
                                                      in1=gq_t[:].unsqueeze(1).to_broadcast([128, 16, 64]), op=ALU.mult),
                     reads=tmpq_n + ["gq"], writes=qknb_n)
                P.op("dve", lambda e: e.tensor_tensor(out=qknb[:, 1024:1152].rearrange("p (h d) -> p h d", d=64), in0=tq3[:, 16:18, :],
                                                      in1=gk_t[:].unsqueeze(1).to_broadcast([128, 2, 64]), op=ALU.mult),
                     reads=tmpq_n + ["gk"], writes=qknb_n)
                if sample or (t == 3 and bi == 3):
                    P.op("dve", lambda e: e.tensor_tensor(out=kn32.rearrange("p (h d) -> p h d", d=64), in0=tq3[:, 16:18, :],
                                                          in1=gk_t[:].unsqueeze(1).to_broadcast([128, 2, 64]), op=ALU.mult),
                         reads=tmpq_n + ["gk"], writes=o32_n)
                    ko = ks_o if sample else kp_o
                    P.dma("pool", "ost", lambda e, ko=ko: e.dma_start(out=ko.ap(), in_=kn32), reads=o32_n)

            def a1_pe(bi):
                slot = gB[bi] % NSLOT
                for c in range(9):
                    P.op("pe", lambda e, c=c: e.matmul(ptr[:, c * 128:(c + 1) * 128], lhsT=qknb[:, c * 128:(c + 1) * 128],
                                                     rhs=identb[:], start=True, stop=True, is_transpose=True),
                         reads=qknb_n + ["identb"], writes=["ps0", "ps1"])
                P.op("act", lambda e: e.copy(out=qT[:], in_=ptr[:, 0:1024].rearrange("p (c t) -> p c t", c=8)),
                     reads=["ps0", "ps1"], writes=["qT"])
                P.op("dve", lambda e, slot=slot: e.tensor_copy(out=kTz[0][slot][0:64, :], in_=ptr[0:64, 1024:1152]),
                     reads=["ps0", "ps1"], writes=[f"kTz0_{slot}"])
                P.op("dve", lambda e, slot=slot: e.tensor_copy(out=kTz[1][slot][64:128, :], in_=ptr[64:128, 1024:1152]),
                     reads=["ps0", "ps1"], writes=[f"kTz1_{slot}"])

            def z1_unit(c, s, bi, pb):
                wdt = IN_W[c]
                wv = SLOT_AP[s][:, 0:16 * wdt].rearrange("p (c n) -> p c n", c=16)
                for dc in range(16):
                    P.op("pe", lambda e, pb=pb, dc=dc, bi=bi, wv=wv, wdt=wdt: e.matmul(
                        bank(pb, wdt), lhsT=Tr[:, dc, bi * 128:(bi + 1) * 128], rhs=wv[:, dc, :], start=(dc == 0), stop=(dc == 15)),
                        reads=[f"T{bi}"] + SLOT_N[s], writes=[f"ps{pb}"])
                slot = gB[bi] % NSLOT
                if c < 2:
                    P.op("dve", lambda e, pb=pb, bi=bi, c=c: e.tensor_copy(out=zqk[bi][:, c * 512:(c + 1) * 512], in_=bank(pb)),
                         reads=[f"ps{pb}"], writes=zqk_n[bi])
                elif c == 2:
                    P.op("dve", lambda e, pb=pb, bi=bi: e.tensor_copy(out=zqk[bi][:, 1024:1152], in_=bank(pb, 128)),
                         reads=[f"ps{pb}"], writes=zqk_n[bi])
                    P.op("dve", lambda e, pb=pb, slot=slot: e.tensor_copy(out=Vb[slot][:], in_=ps[:, pb * 512 + 128:pb * 512 + 256]),
                         reads=[f"ps{pb}"], writes=[f"Vb{slot}"])
                    if sample or (t == 3 and bi == 3):
                        P.op("dve", lambda e, pb=pb: e.tensor_copy(out=v32, in_=ps[:, pb * 512 + 128:pb * 512 + 256]),
                             reads=[f"ps{pb}"], writes=o32_n)
                        vo = vs_o if sample else vp_o
                        P.dma("pool", "ost", lambda e, vo=vo: e.dma_start(out=vo.ap(), in_=v32), reads=o32_n)
                else:
                    half = (c - 3) % 2
                    which = "lo" if c < 5 else "hi"
                    col0 = (0 if c < 5 else 1024) + half * 512
                    P.op("act", lambda e, pb=pb, bi=bi, col0=col0: e.activation(out=X1[:, bi, col0:col0 + 512], in_=bank(pb),
                                                                               func=AF.Gelu_apprx_tanh),
                         reads=[f"ps{pb}"], writes=[f"X1_{bi}_{which}"])


            s0 = next_chunk("in")
            for bi in range(nb):
                x1n = [f"X1_{bi}_lo", f"X1_{bi}_hi"]
                P.op("act", lambda e, bi=bi: e.activation(out=mixb, in_=X1[:, bi, :], func=AF.Square, accum_out=stat[:, 0:1]),
                     reads=x1n, writes=mixb_n + ["st0"])
                rstd_of(stat[:, 0:1], stat[:, 1:2], 1.0 / D, ["st0"], "n1")
                P.op("dve", lambda e, bi=bi: e.scalar_tensor_tensor(out=mixb, in0=X1[:, bi, :], scalar=stat[:, 1:2], in1=g_mix[:],
                                                                    op0=ALU.mult, op1=ALU.mult),
                     reads=x1n + ["n1_r", "g_mix"], writes=mixb_n)
                transposes_to(mixb, 16, Tr[:, :, bi * 128:(bi + 1) * 128], mixb_n, [f"T{bi}"])
                if bi >= 1:
                    z1_unit(0, s0, bi - 1, 2 + (bi - 1) % 4)
            z1_unit(0, s0, nb - 1, 2 + (nb - 1) % 4)
            if not late_done["v"]:
                late_done["v"] = True
                late_setup()
            ckpt(1)
            for c in range(1, 7):
                s = next_chunk("in")
                for bi in range(nb):
                    z1_unit(c, s, bi, 2 + (c * nb + bi) % 4)
                if c == 2:
                    a1_elem(0)
                if c == 3:
                    a1_pe(0)

            ckpt(2)
            o_early = {"s": None, "done": set()}

            def o_unit(c, s, bi, pb):
                wv = SLOT_AP[s].rearrange("p (c n) -> p c n", c=16)
                for kc in range(16):
                    P.op("pe", lambda e, pb=pb, kc=kc, bi=bi, wv=wv: e.matmul(
                        bank(pb), lhsT=Tr[:, kc, bi * 128:(bi + 1) * 128], rhs=wv[:, kc, :], start=(kc == 0), stop=(kc == 15)),
                        reads=[f"T{bi}"] + SLOT_N[s], writes=[f"ps{pb}"])
                which = "lo" if c < 2 else "hi"
                P.op("dve", lambda e, pb=pb, bi=bi, c=c: e.tensor_tensor(out=X1[:, bi, c * 512:(c + 1) * 512], in0=bank(pb),
                                                                       in1=X1[:, bi, c * 512:(c + 1) * 512], op=ALU.add),
                     reads=[f"ps{pb}", f"X1_{bi}_{which}"], writes=[f"X1_{bi}_{which}"])

            for bi in range(nb):
                B = gB[bi]
                slot = B % NSLOT
                if sample:
                    kts = [(2, biasP[:], ["biasP"]), (3, biasPB.rearrange("p (h q) -> p h q", h=16), biasPB_n), (slot, biasO[:], ["biasO"])]
                elif B == 0:
                    kts = [(slot, biasO[:], ["biasO"])]
                else:
                    kts = [((B - 1) % NSLOT, biasP[:], ["biasP"]), (slot, biasO[:], ["biasO"])]
                zv = X1[:, bi, 0:1024]
                uu = X1[:, bi, 1024:2048]
                P.op("act", lambda e, zv=zv: e.activation(out=vnb, in_=zv, func=AF.Square, accum_out=stat[:, 4:5]),
                     reads=[f"X1_{bi}_lo"], writes=vnb_n + ["st4"])
                rstd_of(stat[:, 4:5], stat[:, 5:6], 1.0 / 1024, ["st4"], "sv")
                if sample:
                    P.op("dve", lambda e, zv=zv: e.scalar_tensor_tensor(out=o32, in0=zv, scalar=stat[:, 5:6], in1=g_sgu[:],
                                                                        op0=ALU.mult, op1=ALU.mult),
                         reads=[f"X1_{bi}_lo", "sv_r", "g_sgu"], writes=o32_n)
                    P.dma("pool", "ost", lambda e: e.dma_start(out=sgv_o.ap(), in_=o32), reads=o32_n)
                P.op("dve", lambda e, zv=zv: e.scalar_tensor_tensor(out=vnb, in0=zv, scalar=stat[:, 5:6], in1=g_sgu[:],
                                                                    op0=ALU.mult, op1=ALU.mult),
                     reads=[f"X1_{bi}_lo", "sv_r", "g_sgu"], writes=vnb_n)
                Wg = WsTs if sample else WsT
                Wgn = WsTs_n if sample else ["WsT"]
                bg = bsTs if sample else bsT
                bgn = "bsTs" if sample else "bsT"
                cnt = 0
                for hk in range(2):
                    for ki, (ks_, bt, btn) in enumerate(kts):
                        for half in range(2):
                            pb = 4 + cnt % 2
                            ts_ = cnt % 2
                            cnt += 1
                            P.op("pe", lambda e, pb=pb, hk=hk, ks_=ks_, half=half: e.matmul(
                                bank(pb), lhsT=kTz[hk][ks_][:], rhs=qT[:, 4 * half:4 * half + 4, :], start=True, stop=True),
                                reads=[f"kTz{hk}_{ks_}", "qT"], writes=[f"ps{pb}"])
                            h0 = hk * 8 + 4 * half
                            P.op("dve", lambda e, pb=pb, ts_=ts_, bt=bt, h0=h0: e.scalar_tensor_tensor(
                                out=tmpS[ts_], in0=bank(pb), scalar=0.125, in1=bt[:, h0:h0 + 4, :].rearrange("p h q -> p (h q)"),
                                op0=ALU.mult, op1=ALU.add),
                                reads=[f"ps{pb}"] + btn, writes=tmpS_n[ts_])
                            P.op("act", lambda e, ts_=ts_, hk=hk, ki=ki, half=half: e.activation(
                                out=PT[(hk, ki)][:, half * 512:(half + 1) * 512], in_=tmpS[ts_], func=AF.Exp),
                                reads=tmpS_n[ts_], writes=PT_n[(hk, ki)])
                for g in range(8):
                    P.op("pe", lambda e, g=g, Wg=Wg: e.matmul(ps[:, (3 - g // 4) * 512 + (g % 4) * 128:(3 - g // 4) * 512 + (g % 4 + 1) * 128], lhsT=Wg[:, g, :],
                                                            rhs=vnb[:, g * 128:(g + 1) * 128], start=True, stop=True),
                         reads=Wgn + vnb_n, writes=[f"ps{3 - g // 4}"])
                for g in range(8):
                    P.op("dve", lambda e, g=g, bg=bg, bi=bi: e.scalar_tensor_tensor(
                        out=X1[:, bi, g * 128:(g + 1) * 128], in0=ps[:, (3 - g // 4) * 512 + (g % 4) * 128:(3 - g // 4) * 512 + (g % 4 + 1) * 128], scalar=bg[:, g:g + 1],
                        in1=X1[:, bi, 1024 + g * 128:1024 + (g + 1) * 128], op0=ALU.add, op1=ALU.mult),
                        reads=[f"ps{3 - g // 4}", bgn, f"X1_{bi}_hi"], writes=[f"X1_{bi}_lo"])
                P.op("act", lambda e, zv=zv: e.activation(out=mixb[:, 1024:2048], in_=zv, func=AF.Square, accum_out=stat[:, 6:7]),
                     reads=[f"X1_{bi}_lo"], writes=mixb_hi_n + ["st6"])
                rstd_of(stat[:, 6:7], stat[:, 7:8], 1.0 / 1024, ["st6"], "os")
                P.op("dve", lambda e, zv=zv: e.scalar_tensor_tensor(out=mixb[:, 1024:2048], in0=zv, scalar=stat[:, 7:8], in1=g_os[:],
                                                                    op0=ALU.mult, op1=ALU.mult),
                     reads=[f"X1_{bi}_lo", "os_r", "g_os"], writes=mixb_hi_n)
                if bi + 1 < nb:
                    a1_elem(bi + 1)
                nk = len(kts)
                for hk in range(2):
                    for g in range(8):
                        h = hk * 8 + g
                        for ki, (ks_, bt, btn) in enumerate(kts):
                            P.op("pe", lambda e, hk=hk, g=g, ki=ki, ks_=ks_, nk=nk: e.matmul(
                                ps[:, (6 + hk) * 512 + g * 64:(6 + hk) * 512 + (g + 1) * 64], lhsT=PT[(hk, ki)][:, g * 128:(g + 1) * 128],
                                rhs=Vb[ks_][:, hk * 64:(hk + 1) * 64], start=(ki == 0), stop=(ki == nk - 1)),
                                reads=PT_n[(hk, ki)] + [f"Vb{ks_}"], writes=[f"ps{6 + hk}"])
                        for ki, (ks_, bt, btn) in enumerate(kts):
                            P.op("pe", lambda e, hk=hk, g=g, ki=ki, h=h, nk=nk: e.matmul(
                                ps[:, 2 * 512 + h:2 * 512 + h + 1], lhsT=PT[(hk, ki)][:, g * 128:(g + 1) * 128],
                                rhs=onesb[:, 0:1], start=(ki == 0), stop=(ki == nk - 1)),
                                reads=PT_n[(hk, ki)] + ["onesb"], writes=["ps2"])
                if bi == nb - 1 and nb > 1:
                    o_early["s"] = next_chunk("out")
                    for b2 in range(nb - 1):
                        o_unit(0, o_early["s"], b2, 3 + b2 % 3)
                        o_early["done"].add(b2)
                P.op("dve", lambda e: e.tensor_tensor(out=stat[:, 32:48], in0=ps[:, 1024:1040], in1=esink[:], op=ALU.add),
                     reads=["ps2", "esink"], writes=["st_den"])
                P.op("dve", lambda e: e.reciprocal(out=stat[:, 32:48], in_=stat[:, 32:48]), reads=["st_den"], writes=["st_rden"])
                for hk in range(2):
                    P.op("dve", lambda e, hk=hk: e.tensor_tensor(
                        out=o32[:, hk * 512:(hk + 1) * 512].rearrange("p (g d) -> p g d", d=64),
                        in0=ps[:, (6 + hk) * 512:(7 + hk) * 512].rearrange("p (g d) -> p g d", d=64),
                        in1=stat[:, 32 + hk * 8:40 + hk * 8].unsqueeze(2).to_broadcast([128, 8, 64]), op=ALU.mult),
                        reads=[f"ps{6 + hk}", "st_rden"], writes=o32_n)
                P.op("act", lambda e: e.activation(out=mixb[:, 0:1024], in_=o32, func=AF.Square, accum_out=stat[:, 2:3]),
                     reads=o32_n, writes=mixb_lo_n + ["st2"])
                rstd_of(stat[:, 2:3], stat[:, 3:4], 1.0 / 1024, ["st2"], "oa")
                P.op("dve", lambda e: e.scalar_tensor_tensor(out=mixb[:, 0:1024], in0=o32, scalar=stat[:, 3:4], in1=g_oa[:],
                                                             op0=ALU.mult, op1=ALU.mult),
                     reads=o32_n + ["oa_r", "g_oa"], writes=mixb_lo_n)
                if bi + 1 < nb:
                    a1_pe(bi + 1)
                transposes_to(mixb, 16, Tr[:, :, bi * 128:(bi + 1) * 128], mixb_n, [f"T{bi}"])
                P.dma("pool", "xld2", lambda e, bi=bi: e.dma_start(out=X1[:, bi, :], in_=xsrc(bi)),
                      writes=[f"X1_{bi}_lo", f"X1_{bi}_hi"])

            ckpt(3)
            for c in range(4):
                s = o_early["s"] if (c == 0 and o_early["s"] is not None) else next_chunk("out")
                for bi in range(nb):
                    if c == 0 and bi in o_early["done"]:
                        continue
                    pb = 2 + (c * nb + bi) % 4
                    o_unit(c, s, bi, pb)
                    if c == 3:
                        if bi >= 1:
                            transposes_to(mixb, 16, Tr[:, :, (bi - 1) * 128:bi * 128], mixb_n, [f"T{bi - 1}"])
                        x1n = [f"X1_{bi}_lo", f"X1_{bi}_hi"]
                        P.op("act", lambda e, bi=bi: e.activation(out=mixb, in_=X1[:, bi, :], func=AF.Square, accum_out=stat[:, 0:1]),
                             reads=x1n, writes=mixb_n + ["st0"])
                        rstd_of(stat[:, 0:1], stat[:, 1:2], 1.0 / D, ["st0"], "n1")
                        P.op("dve", lambda e, bi=bi: e.scalar_tensor_tensor(out=mixb, in0=X1[:, bi, :], scalar=stat[:, 1:2], in1=g_ffn[:],
                                                                            op0=ALU.mult, op1=ALU.mult),
                             reads=x1n + ["n1_r", "g_ffn"], writes=mixb_n)
            transposes_to(mixb, 16, Tr[:, :, (nb - 1) * 128:nb * 128], mixb_n, [f"T{nb - 1}"])

            ckpt(4)
            Tn = [f"T{bi}" for bi in range(nb)]
            ev = 0
            for fg in range(16):
                s = next_chunk("up")
                wv = SLOT_AP[s].rearrange("p (c n) -> p c n", c=16)
                for fc in range(4):
                    f = fg * 4 + fc
                    pb = f % 4
                    for dc in range(16):
                        P.op("pe", lambda e, pb=pb, dc=dc, fc=fc, wv=wv: e.matmul(
                            bank(pb, ntok), lhsT=wv[:, dc, fc * 128:(fc + 1) * 128], rhs=Tr[:, dc, 0:ntok], start=(dc == 0), stop=(dc == 15)),
                            reads=Tn + SLOT_N[s], writes=[f"ps{pb}"])
                    tr = ev % 2
                    ev += 1
                    P.op("act", lambda e, pb=pb, tr=tr: e.activation(out=tmpR[tr][:, 0:ntok], in_=bank(pb, ntok), func=AF.Relu),
                         reads=[f"ps{pb}"], writes=[f"tmpR{tr}"])
                    hv = hid(f, 0, ntok)
                    P.op("dve", lambda e, hv=hv, tr=tr: e.tensor_tensor(out=hv, in0=tmpR[tr][:, 0:ntok], in1=tmpR[tr][:, 0:ntok],
                                                                     op=ALU.mult),
                         reads=[f"tmpR{tr}"], writes=[hid_n(f)])

            ckpt(5)
            ydst = (lambda bi: ys.ap()) if sample else (lambda bi: yp.ap()[(4 * t + bi) * 128:(4 * t + bi + 1) * 128, :])
            yt = 0
            for r in range(2):
                for g8 in range(8):
                    s = next_chunk("down")
                    wv = SLOT_AP[s].rearrange("p (c n) -> p c n", c=8)
                    order = [(fc, bi, nh) for fc in range(8) for bi in range(nb) for nh in range(2)]
                    if g8 == 7:
                        order = [(fc, bi, nh) for bi in range(nb) for nh in range(2) for fc in range(8)]
                    for fc, bi, nh in order:
                        f = g8 * 8 + fc
                        pb = bi * 2 + nh
                        hv = hid(f, bi * 128, (bi + 1) * 128)
                        P.op("pe", lambda e, pb=pb, f=f, hv=hv, nh=nh, fc=fc, wv=wv: e.matmul(
                            bank(pb), lhsT=hv, rhs=wv[:, fc, nh * 512:(nh + 1) * 512],
                            start=(f == 0), stop=(f == 63)),
                            reads=[hid_n(f)] + SLOT_N[s], writes=[f"ps{pb}"])
                which = "lo" if r == 0 else "hi"
                for bi in range(nb):
                    ysl = yt % 2
                    yt += 1
                    ytmp = Tr[:, 8 * ysl:8 * ysl + 4, :].rearrange("p c n -> p (c n)").bitcast(F32)
                    for nh in range(2):
                        pb = bi * 2 + nh
                        P.op("dve", lambda e, pb=pb, bi=bi, nh=nh, ytmp=ytmp, r=r: e.tensor_tensor(
                            out=ytmp[:, nh * 512:(nh + 1) * 512], in0=bank(pb), in1=X1[:, bi, r * 1024 + nh * 512:r * 1024 + (nh + 1) * 512],
                            op=ALU.add),
                            reads=[f"ps{pb}", f"X1_{bi}_{which}"], writes=[f"ytmp{ysl}"] + [f"T{i}" for i in range(4)])
                    P.dma("pool", f"yst{ysl}", lambda e, bi=bi, ytmp=ytmp, r=r: e.dma_start(out=ydst(bi)[:, r * 1024:(r + 1) * 1024], in_=ytmp),
                          reads=[f"ytmp{ysl}"])
            for ysl in range(2):
                P.op("dve", lambda e: e.memset(stat[:, 60:61], 0.0), writes=[f"ytmp{ysl}"] + [f"T{i}" for i in range(4)] + ["st60"])

        try:
            for kind, t in (TILES if tiles is None else tiles):
                run_tile(kind, t)
            if tiles is None:
                assert wstate["i"] == len(chunks), (wstate, len(chunks))
        except _Stop:
            pass
        P.finalize(block)
    return nc


def _t5_bucket_static(n):
    import math
    try:
        import jax
        import jax.numpy as jnp
        with jax.default_device(jax.devices("cpu")[0]):
            nn = jnp.asarray(n, dtype=jnp.int32)
            half, max_exact = 16, 8
            offset = jnp.where(nn < 0, half, 0)
            a = jnp.abs(nn)
            af = jnp.maximum(a, 1).astype(jnp.float32)
            large = max_exact + (jnp.log(af / max_exact) / math.log(128 / max_exact) * (half - max_exact)).astype(jnp.int32)
            large = jnp.minimum(large, half - 1)
            return np.asarray(offset + jnp.where(a < max_exact, a, large))
    except Exception:
        nn = np.asarray(n, dtype=np.int32)
        half, max_exact = 16, 8
        offset = np.where(nn < 0, half, 0)
        a = np.abs(nn)
        af = np.maximum(a, 1).astype(np.float32)
        large = max_exact + (np.log(af / np.float32(max_exact)) / np.float32(math.log(128 / max_exact))
                             * np.float32(half - max_exact)).astype(np.int32)
        large = np.minimum(large, half - 1)
        return offset + np.where(a < max_exact, a, large)


_NC_CACHE = {}


def prep_inputs(x_prompt, x_sample, cache_attn_k, cache_attn_v, rel_bias_table, ln_mix_g, w_in,
                q_norm_g, k_norm_g, attn_sinks, sgu_norm_g, sgu_w, sgu_b, out_norm_attn_g,
                out_norm_sgu_g, w_out, ln_ffn_g, w_ffn_up, w_ffn_down):
    f = lambda a: np.ascontiguousarray(np.asarray(a, dtype=np.float32))
    x_prompt, x_sample = f(x_prompt), f(x_sample)
    hk, g, d = np.meshgrid(np.arange(2), np.arange(8), np.arange(64), indexing="ij")
    qcols = ((hk * 8 + g) * 64 + d).transpose(1, 0, 2).reshape(-1)
    perm = np.concatenate([qcols, np.arange(1024, 1280), np.arange(2304, 3328), np.arange(1280, 2304)])
    w_in_p = np.asarray(w_in)[0][:, perm]

    def img_cols(w, col0, wdt):
        return w[:, col0:col0 + wdt].reshape(16, 128, wdt).transpose(1, 0, 2).reshape(128, 16 * wdt)

    IN_COL0 = [0, 512, 1024, 1280, 1792, 2304, 2816]
    IN_W = [512, 512, 256, 512, 512, 512, 512]
    w_in_img = f(np.concatenate([img_cols(w_in_p, c0, wd) for c0, wd in zip(IN_COL0, IN_W)], axis=1))
    wo = np.asarray(w_out)[0]
    w_out_img = f(np.concatenate([img_cols(wo, c * 512, 512) for c in range(4)], axis=1))
    wu = np.asarray(w_ffn_up)[0]
    w_up_img = f(np.concatenate([img_cols(wu, c * 512, 512) for c in range(16)], axis=1))
    wd_ = np.asarray(w_ffn_down)[0]
    w_down_img = f(np.concatenate(
        [wd_[g * 1024:(g + 1) * 1024, r * 1024:(r + 1) * 1024].reshape(8, 128, 1024).transpose(1, 0, 2).reshape(128, 8192)
         for r in range(2) for g in range(8)], axis=1))
    m = np.arange(384)
    bucket = _t5_bucket_static(255 - m)
    oh = np.zeros((32, 384), np.float32)
    oh[bucket, m] = 1.0
    common = {
        "table": f(rel_bias_table), "oh": oh, "ident": np.eye(128, dtype=np.float32),
        "w_in": w_in_img, "w_out": w_out_img, "w_up": w_up_img, "w_down": w_down_img,
        "g_mix": f(ln_mix_g), "g_ffn": f(ln_ffn_g), "g_sgu": f(sgu_norm_g), "g_oa": f(out_norm_attn_g), "g_os": f(out_norm_sgu_g),
        "gq": f(np.asarray(q_norm_g)[0][None, :]), "gk": f(np.asarray(k_norm_g)[0][None, :]), "sinks": f(np.asarray(attn_sinks)[0].reshape(1, 16)),
        "wsT": f(np.asarray(sgu_w)[0].transpose(2, 0, 1)), "bsT": f(np.asarray(sgu_b)[0].T),
    }
    ck = np.asarray(cache_attn_k)[0]
    cvv = np.asarray(cache_attn_v)[0]
    in_maps = []
    for c in range(NCORES):
        mm = dict(common)
        mm["xp"] = x_prompt[c]
        mm["xs"] = f(x_sample[2 * c:2 * c + 2].reshape(128, D))
        mm["ckT"] = f(ck[2 * c:2 * c + 2].reshape(2, 128, 128).transpose(0, 2, 1))
        mm["cv"] = f(cvv[2 * c:2 * c + 2].reshape(2, 128, 128))
        in_maps.append(mm)
    return in_maps


def kernel(**inputs):
    in_maps = prep_inputs(**inputs)
    if "nc" not in _NC_CACHE:
        _NC_CACHE["nc"] = build_program()
    nc = _NC_CACHE["nc"]
    res = run_bass_kernel_spmd(nc, in_maps, core_ids=list(range(NCORES)))
    R = res.results
    y_prompt = np.stack([R[c]["yp"] for c in range(NCORES)]).astype(np.float32)
    y_sample = np.concatenate([R[c]["ys"].reshape(2, 64, D) for c in range(NCORES)]).astype(np.float32)
    kpo = np.stack([R[c]["kp"].reshape(128, 2, 64) for c in range(NCORES)])[None].astype(np.float32)
    vpo = np.stack([R[c]["vp"].reshape(128, 2, 64) for c in range(NCORES)])[None].astype(np.float32)
    kso = np.concatenate([R[c]["ks"].reshape(2, 64, 2, 64) for c in range(NCORES)])[None].astype(np.float32)
    vso = np.concatenate([R[c]["vs"].reshape(2, 64, 2, 64) for c in range(NCORES)])[None].astype(np.float32)
    sgo = np.concatenate([R[c]["sgv"].reshape(2, 64, 1024) for c in range(NCORES)])[None].astype(np.float32)
    return (y_prompt, y_sample, kpo, vpo, kso, vso, sgo)
```
